# Optimizing a Trainium2 kernel written in Bass

```python
import math
import jax
import jax.numpy as jnp
from jax import lax
import numpy as np


D_MODEL = 1024
BATCH = 16
SEQ = 2048
DEPTH = 1

CHUNK = 64
Q_BLOCK = 128
D_RNN = D_MODEL
RNN_BLOCKS = 16
RNN_BLOCK_W = D_RNN // RNN_BLOCKS
CONV_W = 4
LRU_C = 8.0
HEAD_DIM = 64
N_HEADS = D_MODEL // (2 * HEAD_DIM)
ATTN_W = N_HEADS * 2 * HEAD_DIM
D_FF = 4 * D_MODEL
ROPE_THETA = 10000.0
NORM_EPS = 1e-6
SUBLN_EPS = 1e-5
MASK_VALUE = -1e30
N_IN = 2 * D_RNN + 3 * ATTN_W + 2 * D_MODEL

kernel_name = "hybrid_rglru_diffattn_block"


def rms_norm(x, g, eps=NORM_EPS):
    x32 = x.astype(jnp.float32)
    y = x32 * lax.rsqrt(jnp.mean(x32 * x32, axis=-1, keepdims=True) + eps)
    return (y * g.astype(jnp.float32)).astype(x.dtype)


def rope(x):
    s, d = x.shape[1], x.shape[-1]
    half = d // 2
    inv_freq = ROPE_THETA ** (-jnp.arange(half, dtype=jnp.float32) * 2.0 / d)
    ang = jnp.arange(s, dtype=jnp.float32)[:, None] * inv_freq[None, :]
    cos = jnp.cos(ang)[None, :, None, :]
    sin = jnp.sin(ang)[None, :, None, :]
    x32 = x.astype(jnp.float32)
    x1, x2 = x32[..., :half], x32[..., half:]
    return jnp.concatenate([x1 * cos - x2 * sin, x2 * cos + x1 * sin], axis=-1).astype(x.dtype)


def causal_depthwise_conv(x, w, b):
    s = x.shape[1]
    xp = jnp.pad(x, ((0, 0), (CONV_W - 1, 0), (0, 0)))
    y = xp[:, 0:s] * w[0]
    for j in range(1, CONV_W):
        y = y + xp[:, j:j + s] * w[j]
    return y + b


def rg_lru(x, w_a, b_a, w_x, b_x, lam):
    bsz, s, _ = x.shape
    xb = x.reshape(bsz, s, RNN_BLOCKS, RNN_BLOCK_W)
    r = jax.nn.sigmoid((jnp.einsum('bsni,nij->bsnj', xb, w_a) + b_a).astype(jnp.float32)).reshape(bsz, s, D_RNN)
    i = jax.nn.sigmoid((jnp.einsum('bsni,nij->bsnj', xb, w_x) + b_x).astype(jnp.float32)).reshape(bsz, s, D_RNN)
    log_a = -LRU_C * r * jax.nn.softplus(-lam.astype(jnp.float32))
    a = jnp.exp(log_a)
    mult = jnp.sqrt(-jnp.expm1(2.0 * log_a))
    u = mult * i * x.astype(jnp.float32)

    def combine(left, right):
        a_l, h_l = left
        a_r, h_r = right
        return a_l * a_r, a_r * h_l + h_r

    _, h = lax.associative_scan(combine, (a, u), axis=1)
    return h.astype(x.dtype)


def diff_attention(q, k, v, lam, lam_init, subln_g):
    bsz, s = q.shape[0], q.shape[1]
    scale = HEAD_DIM ** -0.5
    chunk_id = jnp.arange(s) // CHUNK
    outs = []
    for q0 in range(0, s, Q_BLOCK):
        k_end = q0 + Q_BLOCK
        qb = q[:, q0:k_end]
        kb = k[:, :k_end]
        vb = v[:, :k_end]
        sc = jnp.einsum('bqhmd,bkhmd->bhmqk', qb, kb).astype(jnp.float32) * scale
        mask = chunk_id[None, :k_end] <= chunk_id[q0:k_end, None]
        sc = jnp.where(mask, sc, MASK_VALUE)
        p = jax.nn.softmax(sc, axis=-1)
        pd = p[:, :, 0] - lam * p[:, :, 1]
        outs.append(jnp.einsum('bhqk,bkhe->bqhe', pd.astype(v.dtype), vb))
    o = jnp.concatenate(outs, axis=1)
    o = rms_norm(o, subln_g, SUBLN_EPS) * (1.0 - lam_init)
    return o.reshape(bsz, s, ATTN_W)


def setup_inputs(seed: int = 0) -> dict:
    key = jax.random.key(seed)
    ks = jax.random.split(key, 24)

    def nrm(k, shape, fan_in):
        return jax.random.normal(k, shape, jnp.float32) * (fan_in ** -0.5)

    def gain(k, shape):
        return 1.0 + 0.01 * jax.random.normal(k, shape, jnp.float32)

    u = jax.random.uniform(ks[9], (DEPTH, D_RNN), jnp.float32, minval=0.9, maxval=0.999)
    root = u ** (1.0 / LRU_C)
    lru_lambda = jnp.log(root) - jnp.log1p(-root)
    return {
        "x": jax.random.normal(ks[0], (BATCH, SEQ, D_MODEL), jnp.float32),
        "norm1_g": gain(ks[1], (DEPTH, D_MODEL)),
        "w_in": nrm(ks[2], (DEPTH, D_MODEL, N_IN), D_MODEL),
        "conv_w": nrm(ks[3], (DEPTH, CONV_W, D_RNN), CONV_W),
        "conv_b": 0.01 * jax.random.normal(ks[4], (DEPTH, D_RNN), jnp.float32),
        "rg_a_w": nrm(ks[5], (DEPTH, RNN_BLOCKS, RNN_BLOCK_W, RNN_BLOCK_W), RNN_BLOCK_W),
        "rg_a_b": 0.01 * jax.random.normal(ks[6], (DEPTH, RNN_BLOCKS, RNN_BLOCK_W), jnp.float32),
        "rg_x_w": nrm(ks[7], (DEPTH, RNN_BLOCKS, RNN_BLOCK_W, RNN_BLOCK_W), RNN_BLOCK_W),
        "rg_x_b": 0.01 * jax.random.normal(ks[8], (DEPTH, RNN_BLOCKS, RNN_BLOCK_W), jnp.float32),
        "lru_lambda": lru_lambda,
        "lambda_q1": 0.1 * jax.random.normal(ks[10], (DEPTH, HEAD_DIM), jnp.float32),
        "lambda_k1": 0.1 * jax.random.normal(ks[11], (DEPTH, HEAD_DIM), jnp.float32),
        "lambda_q2": 0.1 * jax.random.normal(ks[12], (DEPTH, HEAD_DIM), jnp.float32),
        "lambda_k2": 0.1 * jax.random.normal(ks[13], (DEPTH, HEAD_DIM), jnp.float32),
        "subln_g": gain(ks[14], (DEPTH, 2 * HEAD_DIM)),
        "w_br_rnn": nrm(ks[15], (DEPTH, D_RNN, D_MODEL), D_RNN),
        "w_br_attn": nrm(ks[16], (DEPTH, ATTN_W, D_MODEL), ATTN_W),
        "w_out": nrm(ks[17], (DEPTH, D_MODEL, D_MODEL), D_MODEL),
        "norm2_g": gain(ks[18], (DEPTH, D_MODEL)),
        "w_mlp1": nrm(ks[19], (DEPTH, D_MODEL, D_FF), D_MODEL),
        "w_mlp2": nrm(ks[20], (DEPTH, D_FF, D_MODEL), D_FF),
        "normf_g": gain(ks[21], (D_MODEL,)),
    }


def reference(x, norm1_g, w_in, conv_w, conv_b, rg_a_w, rg_a_b, rg_x_w, rg_x_b, lru_lambda,
              lambda_q1, lambda_k1, lambda_q2, lambda_k2, subln_g, w_br_rnn, w_br_attn,
              w_out, norm2_g, w_mlp1, w_mlp2, normf_g):
    bsz, s = x.shape[0], x.shape[1]
    sizes = (D_RNN, D_RNN, ATTN_W, ATTN_W, ATTN_W, D_MODEL, D_MODEL)
    cuts = []
    acc = 0
    for w in sizes[:-1]:
        acc += w
        cuts.append(acc)
    for l in range(DEPTH):
        lam_init = 0.8 - 0.6 * math.exp(-0.3 * l)
        h = rms_norm(x, norm1_g[l])
        proj = h @ w_in[l]
        u_x, u_g, q, k, v, g_r, g_a = jnp.split(proj, cuts, axis=-1)
        xr = causal_depthwise_conv(u_x, conv_w[l], conv_b[l])
        y_rnn = rg_lru(xr, rg_a_w[l], rg_a_b[l], rg_x_w[l], rg_x_b[l], lru_lambda[l]) * jax.nn.gelu(u_g)
        q = rope(q.reshape(bsz, s, 2 * N_HEADS, HEAD_DIM)).reshape(bsz, s, N_HEADS, 2, HEAD_DIM)
        k = rope(k.reshape(bsz, s, 2 * N_HEADS, HEAD_DIM)).reshape(bsz, s, N_HEADS, 2, HEAD_DIM)
        v = v.reshape(bsz, s, N_HEADS, 2 * HEAD_DIM)
        f32 = jnp.float32
        lam = (jnp.exp(jnp.sum(lambda_q1[l].astype(f32) * lambda_k1[l].astype(f32)))
               - jnp.exp(jnp.sum(lambda_q2[l].astype(f32) * lambda_k2[l].astype(f32)))
               + lam_init)
        y_attn = diff_attention(q, k, v, lam, lam_init, subln_g[l])
        merged = (jax.nn.sigmoid(g_r) * (y_rnn @ w_br_rnn[l])
                  + jax.nn.sigmoid(g_a) * (y_attn @ w_br_attn[l]))
        x = x + merged @ w_out[l]
        h2 = rms_norm(x, norm2_g[l])
        x = x + jnp.square(jax.nn.relu(h2 @ w_mlp1[l])) @ w_mlp2[l]
    return rms_norm(x, normf_g)
```

```python
import math
import os
from contextlib import ExitStack

import numpy as np
import concourse.bass as bass
import concourse.mybir as mybir
from concourse.bass_utils import run_bass_kernel_spmd

F32 = mybir.dt.float32
BF16 = mybir.dt.bfloat16
AF = mybir.ActivationFunctionType
ALU = mybir.AluOpType
AX = mybir.AxisListType

D = 1024
S = 2048
NT = 16
NSUP = 4
NSEQ = 2
NCORES = 8
NDS = 16
LAM_INIT = 0.8 - 0.6 * math.exp(0.0)
NORM_EPS = 1e-6
NSTREAM = 3
LV = int(os.environ.get('KDBG_LV', '9'))
SUBLN_EPS = 1e-5
GELU_C = 0.7978845608028654


class Sched:
    def __init__(self, nc, es):
        self.nc = nc
        self.E = {"pe": nc.tensor, "act": nc.scalar, "dve": nc.vector, "pool": nc.gpsimd, "sp": nc.sync}
        self.sems = {e: es.enter_context(nc.semaphore("c_" + e)) for e in ("pe", "act", "dve", "pool")}
        self.cnt = {e: 0 for e in self.sems}
        self.sid = {e: "c_" + e for e in self.sems}
        self.epoch = 0
        self.seen = {e: {} for e in self.E}
        self.lastw = {}
        self.readers = {}
        self.dsems = [es.enter_context(nc.semaphore("d%d" % i)) for i in range(NDS)]
        self.dval = [0] * NDS
        self.di = 0
        self.nrd = 0
        self.out_toks = []
        self.own_last = {}

    def _deps(self, reads, writes):
        deps = []
        for k in reads:
            t = self.lastw.get(k)
            if t is not None:
                deps.append((t, 0))
            if k[0] == "ps":
                for t in self.readers.get(k, {}).values():
                    deps.append((t, 1))
        for k in writes:
            t = self.lastw.get(k)
            if t is not None:
                deps.append((t, 1))
            for t in self.readers.get(k, {}).values():
                deps.append((t, 1))
        return deps

    def _wait(self, eng, deps, is_dma):
        e = self.E[eng]
        seen = self.seen[eng]
        for (t, kind) in deps:
            sem, val, teng, sid = t
            if (not is_dma) and teng == eng:
                if eng == "pe" or kind != 0:
                    continue
            if seen.get(sid, 0) >= val:
                continue
            e.wait_ge(sem, val)
            seen[sid] = val

    def _record(self, tok, reads, writes):
        for k in writes:
            self.lastw[k] = tok
            self.readers[k] = {}
        for k in reads:
            if k in writes:
                continue
            d = self.readers.setdefault(k, {})
            if tok[2] == "dma":
                self.nrd += 1
                d[("dma", self.nrd)] = tok
            else:
                d[tok[2]] = tok

    def new_epoch(self, es):
        self.epoch += 1
        for e in ("pe", "act", "dve"):
            self.sems[e] = es.enter_context(self.nc.semaphore("c%d_%s" % (self.epoch, e)))
            self.cnt[e] = 0
            self.sid[e] = "c%d_%s" % (self.epoch, e)

    def op(self, eng, reads, writes, fn):
        self._wait(eng, self._deps(reads, writes), False)
        ins = fn(self.E[eng])
        self.cnt[eng] += 1
        ins.then_inc(self.sems[eng], 1)
        tok = (self.sems[eng], self.cnt[eng], eng, self.sid[eng])
        self._record(tok, reads, writes)
        return tok

    def dma_own(self, q, out, in_, reads, writes, own):
        self._wait(q, self._deps(reads, writes), True)
        sem, st = own
        if st["n"] > 0:
            self.E[q].wait_ge(sem, 16)
            self.E[q].sem_clear(sem)
        st["n"] += 1
        ins = self.E[q].dma_start(out=out, in_=in_)
        ins.then_inc(sem, 16)
        tok = (sem, 16, "dma", "own%s_%d" % (st["name"], st["n"]))
        self._record(tok, reads, writes)
        self.own_last[st["name"]] = (sem, tok[3])
        return tok

    def dma(self, q, out, in_, reads, writes, is_out=False):
        self._wait(q, self._deps(reads, writes), True)
        i = self.di
        self.di = (self.di + 1) % NDS
        sem = self.dsems[i]
        sid = "d%d" % i
        if self.dval[i] > 0 and self.seen[q].get(sid, 0) < self.dval[i]:
            self.E[q].wait_ge(sem, self.dval[i])
            self.seen[q][sid] = self.dval[i]
        ins = self.E[q].dma_start(out=out, in_=in_)
        self.dval[i] += 16
        ins.then_inc(sem, 16)
        tok = (sem, self.dval[i], "dma", sid)
        self._record(tok, reads, writes)
        if is_out:
            self.out_toks.append(tok)
        return tok

    def finish(self):
        e = self.E["sp"]
        for name, (sem, sid) in self.own_last.items():
            if self.seen["sp"].get(sid, 0) < 16:
                e.wait_ge(sem, 16)
        for i in range(NDS):
            sid = "d%d" % i
            if self.dval[i] > 0 and self.seen["sp"].get(sid, 0) < self.dval[i]:
                e.wait_ge(self.dsems[i], self.dval[i])
                self.seen["sp"][sid] = self.dval[i]
        for (sem, val, _, sid) in self.out_toks:
            if self.seen["sp"].get(sid, 0) < val:
                e.wait_ge(sem, val)
                self.seen["sp"][sid] = val


class Rot:
    def __init__(self, nc, name, n, shape, dtype):
        self.bufs = [nc.alloc_sbuf_tensor("sb_%s%d" % (name, i), shape, dtype) for i in range(n)]
        self.name = name
        self.i = 0

    def get(self):
        j = self.i % len(self.bufs)
        self.i += 1
        return self.bufs[j], (self.name, j)


class _Stop(Exception):
    pass


def build(debug=False, upto=None, skipB=False):
    nc = bass.Bass("TRN2", target_bir_lowering=False)
    es = ExitStack()

    def dram(name, shape, dt=F32, kind="ExternalInput"):
        return nc.dram_tensor(name, shape, dt, kind=kind).ap()

    x_d = dram("x", [NSEQ, S, D])
    out_d = dram("out", [NSEQ, S, D], kind="ExternalOutput")
    wB_d = dram("wB", [8, 128, 8 * 256])
    wD_d = dram("wD", [8, 128, 8 * 384])
    wE_d = dram("wE", [8, 128, 8 * 512])
    wF_d = dram("wF", [2, 128, 8 * 512])
    wG1_d = dram("wG1", [8, 128, 8 * 512])
    wG2_d = dram("wG2", [8, 128, 4 * 1024])
    gains_d = dram("gains", [3, 128, D])
    chp_d = dram("chp", [128, 8 * 8])
    wbd_d = dram("wbd", [128, 2 * 8 * 128])
    rope_d = dram("rope", [128, 2 * 16 * 32])
    lamv_d = dram("lamv", [128, 4 * 64])
    subg_d = dram("subg", [128, 128])
    ident_d = dram("ident", [128, 128])
    dbg = {}
    if debug:
        for nm in ("hT", "yrT", "yaT", "mT"):
            dbg[nm] = dram("dbg_" + nm, [128, 8 * S], BF16, kind="ExternalOutput")
        dbg["x1"] = dram("dbg_x1", [128, 16 * D], F32, kind="ExternalOutput")

    sc = Sched(nc, es)
    def A(name, shape, dt):
        return nc.alloc_sbuf_tensor("sb_" + name, shape, dt)

    hT = A("hT", [128, 8, S], BF16)
    BC = A("BC", [128, 16 * D], F32)
    BCb = BC.bitcast(BF16).reshape([128, 16, S])
    x1 = BC.reshape([128, 16, D])
    ATT = A("ATT", [128, 16448], BF16)
    mT = ATT[:, 0:16384].rearrange("p (k t) -> p k t", k=8)
    qkT = [ATT[:, i * 4096:(i + 1) * 4096].rearrange("p (a t) -> p a t", a=2) for i in range(2)]
    V1 = [ATT[:, 8192 + i * 2080: 8192 + (i + 1) * 2080].rearrange("p (t e) -> p t e", t=NT) for i in range(2)]
    yat = [ATT[:, 12352 + i * 2048: 12352 + (i + 1) * 2048].rearrange("p (t e) -> p t e", t=NT) for i in range(2)]
    ATT_KEYS = ([("qkT", a, T) for a in range(2) for T in range(4)] + [("V1", a) for a in range(2)]
                + [("yat", a, T) for a in range(2) for T in range(4)])
    M_KEYS = [("m", c, T) for c in range(8) for T in range(4)]
    wslot = [A("wslot%d" % i, [128, 4096], BF16) for i in range(3)]
    gain = A("gain", [128, D], F32)
    chp = A("chp", [128, 8, 8], F32)
    chq = A("chq", [128, 8, 4], F32)
    wbd = A("wbd", [128, 2, 8, 128], BF16)
    rope = A("rope", [128, 2, 16, 32], F32)
    lamv = A("lamv", [128, 4, 64], F32)
    subg = A("subg", [128, 128], F32)
    ident = A("ident", [128, 128], BF16)
    neglam = A("neglam", [128, 1], F32)
    junk = A("junk", [128, D], BF16)
    setup_t = A("setup_t", [128, 64], F32)
    setup_u = A("setup_u", [128, 64], F32)
    setup_s = A("setup_s", [128, 8], F32)
    setup_v = A("setup_v", [128, 4], F32)
    halo = A("halo", [128, 4], F32)
    hlast = A("hlast", [128, 1], F32)
    epsc = A("epsc", [128, 2], F32)

    ps = nc.alloc_psum_tensor("ps", [128, 4096], F32)
    psb = ps.bitcast(BF16)

    class PS:
        nxt = 0

    def ps1():
        b = PS.nxt % 8
        PS.nxt += 1
        return b

    def ps2():
        if PS.nxt % 2:
            PS.nxt += 1
        b = PS.nxt % 8
        PS.nxt += 2
        return b

    class PSL:
        nxt = 0

    def ps1lo():
        b = PSL.nxt % 4
        PSL.nxt += 1
        return b

    class PSP3:
        nxt = 0

    def ps1p3():
        b = PSP3.nxt % 3
        PSP3.nxt += 1
        return b

    def bank(b, n=512, off=0):
        return ps[:, b * 512 + off: b * 512 + off + n]

    xt_r = Rot(nc, "xt", 2, [128, D], F32)
    st_r = Rot(nc, "st", 12, [128, 4], F32)
    fino_r = Rot(nc, "fino", 6, [128, 256], F32)
    NBLK = 11
    blk = [A("blk%d" % i, [128, 516], F32) for i in range(NBLK)]
    blkb = [b_.bitcast(BF16) for b_ in blk]

    ATTf = ATT.bitcast(F32)
    NAB = 15
    pool_a = [(blk[j], blkb[j], ("blk", j)) for j in range(NBLK)]
    pool_b = pool_a + [(ATTf[:, j * 516:(j + 1) * 516], ATT[:, j * 1032:(j + 1) * 1032], ("ab", j))
                       for j in range(NAB)]
    M_KEYS.extend([("ab", j) for j in range(NAB)])

    class TP:
        i = 0
        pool = pool_a

    def tmp(n, dt=F32):
        j = TP.i % len(TP.pool)
        TP.i += 1
        f, bview, key = TP.pool[j]
        if dt == F32:
            return f[:, 0:n], key
        return bview[:, 0:n], key

    halo2 = [A("halo2_%d" % i, [128, 4], F32) for i in range(NSTREAM)]
    hlast2 = [A("hlast2_%d" % i, [128, 1], F32) for i in range(NSTREAM)]

    nc.allow_low_precision("bf16 matmul operands with fp32 accumulation (per problem tolerance)")

    sc.dma("sp", chp[:, :, :], chp_d.rearrange("p (c k) -> p c k", c=8), [], ["chp"])
    sc.dma("sp", rope[:, :, :, :], rope_d.rearrange("p (a t f) -> p a t f", a=2, t=16), [], ["rope"])
    sc.dma("sp", lamv[:, :, :], lamv_d.rearrange("p (a f) -> p a f", a=4), [], ["lamv"])
    sc.dma("sp", subg[:, :], subg_d, [], ["subg"])
    def own_sem(name):
        return (es.enter_context(nc.semaphore("o_" + name)), {"n": 0, "name": name})
    sc.dma_own("pool", wbd[:, :, :, :], wbd_d.rearrange("p (a c f) -> p a c f", a=2, c=8), [], ["wbd"],
               own_sem("wbd"))
    sc.dma_own("pool", ident[:, :], ident_d, [], ["ident"], own_sem("ident"))

    wlist = []
    for s in range(NSEQ):
        for cp in range(0 if skipB else 4):
            wlist.append((wB_d[2 * cp:2 * cp + 2].rearrange("c p n -> p c n"), 4096))
        for h in range(8):
            wlist.append((wD_d[h], 3072))
        for c in range(8):
            wlist.append((wE_d[c], 4096))
        for hf in range(2):
            wlist.append((wF_d[hf], 4096))
        for g in range(8):
            wlist.append((wG1_d[g], 4096))
            wlist.append((wG2_d[g], 4096))
        if debug:
            break
    WS = {"n": 0, "loaded": 0}

    def wnext(second=False):
        n = WS["n"]
        WS["n"] += 1
        while WS["loaded"] < min(n + (2 if second else 3), len(wlist)):
            m = WS["loaded"]
            src_, ncol = wlist[m]
            dst_ = wslot[m % 3][:, 0:ncol]
            if len(src_.shape) == 3:
                dst_ = dst_.rearrange("p (c n) -> p c n", c=src_.shape[1])
            sc.dma_own("pool", dst_, src_, [], [("w", m % 3)], own_sem("w%d" % m))
            WS["loaded"] += 1
        return wslot[n % 3], [("w", n % 3)]

    sc.op("dve", [], ["epsc"], lambda e: e.memset(epsc[:, 0:1], float(NORM_EPS)))
    sc.op("dve", [], ["epsc"], lambda e: e.memset(epsc[:, 1:2], float(SUBLN_EPS)))
    sc.op("dve", ["subg"], ["subg"], lambda e: e.tensor_scalar(
        subg[:, :], subg[:, :], (1.0 - LAM_INIT), None, ALU.mult))
    sc.op("act", ["chp"], ["setup_s"], lambda e: e.activation(
        setup_s[:, :], chp[:, :, 7], AF.Exp, scale=-1.0))
    sc.op("act", ["setup_s"], ["setup_s"], lambda e: e.activation(
        setup_s[:, :], setup_s[:, :], AF.Ln, bias=1.0))
    sc.op("dve", ["setup_s"], ["chq"], lambda e: e.tensor_scalar(
        chq[:, :, 2], setup_s[:, :], -8.0, None, ALU.mult))
    sc.op("dve", ["setup_s"], ["chq"], lambda e: e.tensor_scalar(
        chq[:, :, 3], setup_s[:, :], -4.0, None, ALU.mult))
    sc.op("dve", ["chp"], ["chq"], lambda e: e.tensor_scalar(
        chq[:, :, 0:2], chp[:, :, 5:7], 0.5, None, ALU.mult))
    sc.op("dve", ["lamv"], ["setup_t"], lambda e: e.tensor_tensor(
        setup_t[:, :], lamv[:, 0, :], lamv[:, 1, :], ALU.mult))
    sc.op("dve", ["setup_t"], ["sv0"], lambda e: e.tensor_reduce(
        setup_v[:, 0:1], setup_t[:, :], AX.X, ALU.add))
    sc.op("dve", ["lamv"], ["setup_u"], lambda e: e.tensor_tensor(
        setup_u[:, :], lamv[:, 2, :], lamv[:, 3, :], ALU.mult))
    sc.op("dve", ["setup_u"], ["sv1"], lambda e: e.tensor_reduce(
        setup_v[:, 1:2], setup_u[:, :], AX.X, ALU.add))
    sc.op("act", ["sv0"], ["sv2"], lambda e: e.activation(setup_v[:, 2:3], setup_v[:, 0:1], AF.Exp))
    sc.op("act", ["sv1"], ["sv3"], lambda e: e.activation(setup_v[:, 3:4], setup_v[:, 1:2], AF.Exp))
    sc.op("dve", ["sv2", "sv3"], ["neglam"], lambda e: e.scalar_tensor_tensor(
        neglam[:, :], setup_v[:, 3:4], -LAM_INIT, setup_v[:, 2:3], ALU.add, ALU.subtract))

    def load_gain(idx):
        sc.dma("sp", gain[:, :], gains_d[idx], [], ["gain"])

    def norm_to_T(src_ap, src_key, i):
        stt_, stk = st_r.get()
        sc.op("act", [src_key], [stk, "junk"], lambda e: e.activation(
            junk[:, :], src_ap, AF.Square, scale=1.0 / 32.0, accum_out=stt_[:, 0:1]))
        sc.op("act", [stk, "epsc"], [stk + ("l",)], lambda e: e.activation(
            stt_[:, 2:3], stt_[:, 0:1], AF.Ln, bias=epsc[:, 0:1]))
        sc.op("act", [stk + ("l",)], [stk + ("r",)], lambda e: e.activation(
            stt_[:, 1:2], stt_[:, 2:3], AF.Exp, scale=-0.5))
        hb, hbk = tmp(1024, BF16)
        sc.op("dve", [src_key, stk + ("r",), "gain"], [hbk], lambda e: e.scalar_tensor_tensor(
            hb, src_ap, stt_[:, 1:2], gain[:, :], ALU.mult, ALU.mult))
        def part_b():
            b = ps1()

            def tr(e):
                ins = None
                for kc in range(8):
                    ins = e.transpose(psb[:, b * 1024 + kc * 128: b * 1024 + (kc + 1) * 128],
                                      hb[:, kc * 128:(kc + 1) * 128], ident[:, :])
                return ins
            sc.op("pe", [hbk, "ident"], [("ps", b)], tr)
            sc.op("act", [("ps", b)], [("hT", i)], lambda e: e.copy(
                hT[:, :, i * 128:(i + 1) * 128],
                psb[:, b * 1024:(b + 1) * 1024].rearrange("p (k t) -> p k t", k=8)))
        return part_b

    def mm_group(b, off, n, lhs_fn, rhs_fn, nk, reads):
        def f(e):
            ins = None
            for k in range(nk):
                ins = e.matmul(bank(b, n, off), lhs_fn(k), rhs_fn(k), start=(k == 0), stop=(k == nk - 1))
            return ins
        return sc.op("pe", reads, [("ps", b)], f)

    def hT_keys(T):
        return [("hT", 4 * T + r) for r in range(4)]

    def dump(name, ap_sb, keys):
        if debug:
            sc.dma("sp", dbg[name], ap_sb, keys, [], is_out=True)

    def cut(name):
        if upto == name:
            raise _Stop()

    try:
      for s in range(NSEQ):
          if s > 0:
              sc.new_epoch(es)
          load_gain(0)
          pendA = None
          for i in range(NT):
              xt, xk = xt_r.get()
              sc.dma("sp", xt[:, :], x_d[s, i * 128:(i + 1) * 128, :], [], [xk])
              pb = norm_to_T(xt[:, :], xk, i)
              if pendA is not None:
                  pendA()
              pendA = pb
          pendA()
          if debug and s == 0:
              dump("hT", hT[:, :, :].rearrange("p k t -> p (k t)"), [("hT", i) for i in range(NT)])
          cut("A")

          def b_stream(c, w, wk, sl):
              halo_c = halo2[sl]
              hlast_c = hlast2[sl]
              hk = ("halo", sl)
              hlk = ("hlast", sl)
              for T in range(NSUP):
                  cols = slice(T * 512, (T + 1) * 512)
                  b_ux = 2 * sl + 1
                  mm_group(b_ux, 0, 512, lambda k: w[:, k, 0:128], lambda k: hT[:, k, cols], 8, wk + hT_keys(T))
                  b_ug = 2 * sl
                  mm_group(b_ug, 0, 512, lambda k: w[:, k, 128:256], lambda k: hT[:, k, cols], 8, wk + hT_keys(T))
                  yield
                  ux, uxk = tmp(515)
                  if T == 0:
                      sc.op("dve", [], [uxk], lambda e: e.memset(ux[:, 0:3], 0.0))
                  else:
                      sc.op("dve", [hk], [uxk], lambda e: e.tensor_copy(ux[:, 0:3], halo_c[:, 0:3]))
                  sc.op("act", [("ps", b_ux)], [uxk + ("m",)], lambda e: e.copy(ux[:, 3:515], bank(b_ux)))
                  yield
                  uxr = [uxk, uxk + ("m",)]
                  if T < NSUP - 1:
                      sc.op("dve", uxr, [hk], lambda e: e.tensor_copy(halo_c[:, 0:3], ux[:, 512:515]))
                  xr, xrk = tmp(512)
                  sc.op("dve", uxr + ["chp"], [xrk], lambda e: e.tensor_scalar(
                      xr, ux[:, 0:512], chp[:, c, 0:1], chp[:, c, 4:5], ALU.mult, ALU.add))
                  for j in range(1, 4):
                      sc.op("dve", uxr + [xrk, "chp"], [xrk], lambda e, j=j: e.scalar_tensor_tensor(
                          xr, ux[:, j:j + 512], chp[:, c, j:j + 1], xr, ALU.mult, ALU.add))
                  yield
                  xb, xbk = tmp(512, BF16)
                  sc.op("act", [xrk], [xbk], lambda e: e.copy(xb, xr))
                  b_ga = 2 * sl + 1
                  mm_group(b_ga, 0, 512, lambda k: wbd[:, 0, c, :], lambda k: xb, 1, [xbk, "wbd"])
                  yield
                  tr_, trk = tmp(512)
                  sc.op("act", [("ps", b_ga), "chq"], [trk], lambda e: e.activation(
                      tr_, bank(b_ga), AF.Tanh, bias=chq[:, c, 0:1], scale=0.5))
                  b_gx = 2 * sl + 1
                  mm_group(b_gx, 0, 512, lambda k: wbd[:, 1, c, :], lambda k: xb, 1, [xbk, "wbd"])
                  yield
                  ti_, tik = tmp(512)
                  sc.op("act", [("ps", b_gx), "chq"], [tik], lambda e: e.activation(
                      ti_, bank(b_gx), AF.Tanh, bias=chq[:, c, 1:2], scale=0.5))
                  yield
                  a_, ak = tmp(512)
                  sc.op("act", [trk, "chq"], [ak], lambda e: e.activation(
                      a_, tr_, AF.Exp, bias=chq[:, c, 3:4], scale=chq[:, c, 3:4]))
                  a2_, a2k = tmp(512)
                  sc.op("act", [trk, "chq"], [a2k], lambda e: e.activation(
                      a2_, tr_, AF.Exp, bias=chq[:, c, 2:3], scale=chq[:, c, 2:3]))
                  sc.op("act", [trk, "chq"], [trk], lambda e: e.activation(
                      tr_, tr_, AF.Tanh, bias=chq[:, c, 3:4], scale=chq[:, c, 3:4]))
                  yield
                  sc.op("dve", [a2k, trk], [a2k], lambda e: e.scalar_tensor_tensor(
                      a2_, a2_, 1.0, tr_, ALU.add, ALU.mult))
                  sc.op("dve", [tik, xrk], [tik], lambda e: e.scalar_tensor_tensor(
                      ti_, ti_, 1.0, xr, ALU.add, ALU.mult))
                  yield
                  sc.op("act", [a2k], [a2k], lambda e: e.activation(a2_, a2_, AF.Ln, scale=-1.0))
                  sc.op("act", [a2k], [a2k], lambda e: e.activation(a2_, a2_, AF.Exp, scale=0.5))
                  sq, sqk = tmp(512)
                  sc.op("act", [("ps", b_ug)], [sqk], lambda e: e.activation(sq, bank(b_ug), AF.Square))
                  yield
                  sc.op("dve", [tik, a2k], [tik], lambda e: e.scalar_tensor_tensor(
                      ti_, ti_, 0.5, a2_, ALU.mult, ALU.mult))
                  hs, hsk = tmp(512)
                  if T == 0:
                      sc.op("dve", [ak, tik], [hsk], lambda e: e.tensor_tensor_scan(
                          hs, a_, ti_, 0.0, ALU.mult, ALU.add))
                  else:
                      sc.op("dve", [ak, tik, hlk], [hsk], lambda e: e.tensor_tensor_scan(
                          hs, a_, ti_, hlast_c[:, 0:1], ALU.mult, ALU.add))
                  if T < NSUP - 1:
                      sc.op("dve", [hsk], [hlk], lambda e: e.tensor_copy(hlast_c[:, 0:1], hs[:, 511:512]))
                  yield
                  sc.op("dve", [sqk], [sqk], lambda e: e.tensor_scalar(sq, sq, 0.044715, 1.0, ALU.mult, ALU.add))
                  sc.op("dve", [sqk, ("ps", b_ug)], [sqk], lambda e: e.tensor_tensor(sq, sq, bank(b_ug), ALU.mult))
                  yield
                  sc.op("act", [sqk], [sqk], lambda e: e.activation(sq, sq, AF.Tanh, scale=GELU_C))
                  yield
                  sc.op("dve", [sqk, ("ps", b_ug)], [sqk], lambda e: e.scalar_tensor_tensor(
                      sq, sq, 1.0, bank(b_ug), ALU.add, ALU.mult))
                  sc.op("dve", [sqk, hsk], [("yr", c, T), ("x1", c)], lambda e: e.scalar_tensor_tensor(
                      BCb[:, c, cols], sq, 0.5, hs, ALU.mult, ALU.mult))
                  yield

          TP.pool = pool_b
          active = []
          nxt = {"c": 0, "w": None}

          def start_next(sl):
              c = nxt["c"]
              if c >= (0 if skipB else 8):
                  return None
              nxt["c"] += 1
              if c % 2 == 0:
                  nxt["w"] = wnext(second=True)
              wslB, wk = nxt["w"]
              w = wslB[:, (c % 2) * 2048:(c % 2 + 1) * 2048].rearrange("p (k n) -> p k n", k=8)
              return (b_stream(c, w, wk, sl), sl)
          for sl in range(NSTREAM):
              g_ = start_next(sl)
              if g_ is not None:
                  active.append(g_)
          while active:
              for item in list(active):
                  g_, sl = item
                  try:
                      next(g_)
                  except StopIteration:
                      active.remove(item)
                      n_ = start_next(sl)
                      if n_ is not None:
                          active.append(n_)
          TP.pool = pool_a
          if debug and s == 0 and not skipB:
              dump("yrT", BCb[:, 0:8, :].rearrange("p k t -> p (k t)"),
                   [("yr", c, T) for c in range(8) for T in range(4)])
          cut("B")

          def proj_head(h):
              slot = h % 2
              wsl, wk = wnext()
              w = wsl[:, 0:3072].rearrange("p (k n) -> p k n", k=8)
              sc.op("dve", [], [("V1", slot)] + M_KEYS, lambda e: e.memset(V1[slot][:, :, 128:130], 1.0))
              bt = None
              pend = []
              for i in range(NT):
                  r = i % 4
                  b = ps1p3()
                  mm_group(b, 0, 384, lambda k: hT[:, k, i * 128:(i + 1) * 128], lambda k: w[:, k, :], 8,
                           wk + [("hT", i)])
                  if LV < 2:
                      continue
                  X = bank(b, 256).rearrange("p (a h f) -> p a h f", a=4, h=2)
                  cosb = rope[:, 0, i, :].unsqueeze(1).unsqueeze(1).broadcast_to([128, 4, 2, 32])
                  sinb = rope[:, 1, i, :].unsqueeze(1).broadcast_to([128, 4, 32])
                  t1, t1k = tmp(256)
                  t1v = t1.rearrange("p (a h f) -> p a h f", a=4, h=2)
                  sc.op("dve", [("ps", b), "rope"], [t1k], lambda e: e.tensor_tensor(t1v, X, cosb, ALU.mult))
                  tA, tAk = tmp(256)
                  tAv = tA.rearrange("p (a h f) -> p a h f", a=4, h=2)
                  sc.op("dve", [("ps", b), "rope"], [tAk], lambda e: e.tensor_tensor(
                      tAv[:, :, 0, :], X[:, :, 1, :], sinb, ALU.mult))
                  sc.op("dve", [("ps", b), "rope"], [tAk + ("b",)], lambda e: e.tensor_tensor(
                      tAv[:, :, 1, :], X[:, :, 0, :], sinb, ALU.mult))
                  qkb, qkbk = tmp(256, BF16)
                  qkv = qkb.rearrange("p (a h f) -> p a h f", a=4, h=2)
                  sc.op("dve", [t1k, tAk], [qkbk], lambda e: e.tensor_tensor(
                      qkv[:, :, 0, :], t1v[:, :, 0, :], tAv[:, :, 0, :], ALU.subtract))
                  sc.op("dve", [t1k, tAk + ("b",)], [qkbk + ("b",)], lambda e: e.tensor_tensor(
                      qkv[:, :, 1, :], t1v[:, :, 1, :], tAv[:, :, 1, :], ALU.add))
                  if LV < 3:
                      continue
                  sc.op("act", [("ps", b)], [("V1", slot)] + (M_KEYS if i == 0 else []), lambda e: e.copy(
                      V1[slot][:, i, 0:128], bank(b, 128, 256)))
                  if LV < 4:
                      continue
                  bt = 3

                  def tail(r=r, bt=bt, qkb=qkb, qkbk=qkbk, i=i):
                      def trq(e):
                          e.transpose(psb[:, bt * 1024 + r * 128: bt * 1024 + (r + 1) * 128], qkb[:, 0:128],
                                      ident[:, :])
                          return e.transpose(psb[:, bt * 1024 + 512 + r * 128: bt * 1024 + 512 + (r + 1) * 128],
                                             qkb[:, 128:256], ident[:, :])
                      sc.op("pe", [qkbk, qkbk + ("b",), "ident"], [("ps", bt)], trq)
                      if r == 3:
                          T = i // 4
                          sc.op("act", [("ps", bt)], [("qkT", slot, T)] + M_KEYS, lambda e: e.copy(
                              qkT[slot][:, :, T * 512:(T + 1) * 512],
                              psb[:, bt * 1024:(bt + 1) * 1024].rearrange("p (a t) -> p a t", a=2)))
                  if len(pend) >= 2:
                      pend.pop(0)()
                  pend.append(tail)
              while pend:
                  pend.pop(0)()

          def attn_head(h):
              slot = h % 2
              qT = qkT[slot]
              units = [(T, m, j) for T in range(NSUP) for m in range(2) for j in range(4 * T + 4)]
              LOOK = 3
              st_info = {}

              def emit_S(n):
                  T, m, j = units[n]
                  pr = slice(m * 64, (m + 1) * 64)
                  r0 = max(0, j - 4 * T)
                  N = (4 - r0) * 128
                  q0 = T * 512 + r0 * 128
                  bs = ps1lo()
                  sc.op("pe", [("qkT", slot, T), ("qkT", slot, j // 4)], [("ps", bs)],
                        lambda e: e.matmul(bank(bs, N), qT[pr, 1, j * 128:(j + 1) * 128], qT[pr, 0, q0:q0 + N],
                                           start=True, stop=True))
                  et, etk = tmp(512, BF16)
                  sc.op("act", [("ps", bs)], [etk], lambda e: e.activation(
                      et[:, 0:N], bank(bs, N), AF.Exp, scale=0.125))
                  if j >= 4 * T:
                      sc.op("pool", [etk], [etk], lambda e: e.memset(et[64:128, 0:64], 0.0))
                  st_info[n] = (et, etk, r0)

              def emit_PV(n):
                  T, m, j = units[n]
                  et, etk, r0 = st_info.pop(n)
                  ob = 4 + 2 * m

                  def pv(e):
                      ins = None
                      for r in range(r0, 4):
                          bb = ob + r // 2
                          off = (r % 2) * 130
                          ins = e.matmul(bank(bb, 130, off), et[:, (r - r0) * 128:(r - r0 + 1) * 128],
                                         V1[slot][:, j, :], start=(j == 0 and r % 2 == 0),
                                         stop=(j == 4 * T + r and r % 2 == 1))
                      return ins
                  sc.op("pe", [etk, ("V1", slot)], [("ps", ob), ("ps", ob + 1)], pv)

              def finalize(T):
                  osb = []
                  for m in range(2):
                      pair = []
                      for half in range(2):
                          dst, dk = tmp(260)
                          bb = 4 + 2 * m + half
                          sc.op("dve", [("ps", bb)], [dk], lambda e: e.tensor_copy(dst, bank(bb, 260)))
                          pair.append((dst, dk))
                      osb.append(pair)
                  parts = []
                  for half in range(2):
                      (s0, s0k) = osb[0][half]
                      (s1, s1k) = osb[1][half]
                      O0 = s0.rearrange("p (a f) -> p a f", a=2)
                      O1 = s1.rearrange("p (a f) -> p a f", a=2)
                      stt_, stk = st_r.get()
                      sc.op("dve", [s0k], [stk], lambda e: e.reciprocal(stt_[:, 0:2], O0[:, :, 128]))
                      sc.op("dve", [s1k], [stk + ("b",)], lambda e: e.reciprocal(stt_[:, 2:4], O1[:, :, 128]))
                      sc.op("dve", [stk + ("b",), "neglam"], [stk + ("b",)], lambda e: e.tensor_scalar(
                          stt_[:, 2:4], stt_[:, 2:4], neglam[:, 0:1], None, ALU.mult))
                      o0t, o0k = fino_r.get()
                      o0 = o0t[:, :]
                      o0v = o0.rearrange("p (a f) -> p a f", a=2)
                      sc.op("dve", [s0k, stk], [o0k], lambda e: e.tensor_tensor(
                          o0v, O0[:, :, 0:128], stt_[:, 0:2].unsqueeze(2).broadcast_to([128, 2, 128]), ALU.mult))
                      o1, o1k = tmp(256)
                      o1v = o1.rearrange("p (a f) -> p a f", a=2)
                      sc.op("dve", [s1k, stk + ("b",)], [o1k], lambda e: e.tensor_tensor(
                          o1v, O1[:, :, 0:128], stt_[:, 2:4].unsqueeze(2).broadcast_to([128, 2, 128]), ALU.mult))
                      sc.op("dve", [o0k, o1k], [o0k], lambda e: e.tensor_tensor(o0, o0, o1, ALU.add))
                      sc.op("dve", [o0k], [o1k], lambda e: e.tensor_tensor(o1, o0, o0, ALU.mult))
                      st2, st2k = st_r.get()
                      sc.op("dve", [o1k], [st2k], lambda e: e.tensor_reduce(st2[:, 0:2], o1v, AX.X, ALU.add))
                      parts.append((o0v, o0k, st2, st2k))

                  def part2():
                      for half, (o0v, o0k, st2, st2k) in enumerate(parts):
                          sc.op("act", [st2k, "epsc"], [st2k], lambda e: e.activation(
                              st2[:, 0:2], st2[:, 0:2], AF.Ln, bias=epsc[:, 1:2], scale=1.0 / 128.0))
                          sc.op("act", [st2k], [st2k], lambda e: e.activation(
                              st2[:, 0:2], st2[:, 0:2], AF.Exp, scale=-0.5))
                          sc.op("dve", [o0k, st2k], [o0k], lambda e: e.tensor_tensor(
                              o0v, o0v, st2[:, 0:2].unsqueeze(2).broadcast_to([128, 2, 128]), ALU.mult))
                          i0 = 4 * T + 2 * half
                          sc.op("dve", [o0k, "subg"], [("yat", slot, T)] + M_KEYS, lambda e: e.tensor_tensor(
                              yat[slot][:, i0:i0 + 2, :], o0v,
                              subg[:, :].unsqueeze(1).broadcast_to([128, 2, 128]), ALU.mult))

                  def part3():
                      bt = ps1lo()

                      def try_(e):
                          ins = None
                          for r in range(4):
                              ins = e.transpose(psb[:, bt * 1024 + r * 128: bt * 1024 + (r + 1) * 128],
                                                yat[slot][:, 4 * T + r, :], ident[:, :])
                          return ins
                      sc.op("pe", [("yat", slot, T), "ident"], [("ps", bt)], try_)
                      sc.op("act", [("ps", bt)], [("ya", h, T), ("x1", 8 + h)], lambda e: e.copy(
                          BCb[:, 8 + h, T * 512:(T + 1) * 512], psb[:, bt * 1024: bt * 1024 + 512]))
                  deferred.append([6, part2])
                  deferred.append([12, part3])

              nU = len(units)
              for n in range(min(LOOK, nU)):
                  emit_S(n)
              for n in range(nU):
                  if n + LOOK < nU:
                      emit_S(n + LOOK)
                  emit_PV(n)
                  tick_deferred()
                  T, m, j = units[n]
                  if m == 1 and j == 4 * T + 3:
                      finalize(T)
                      if upto == "E0":
                          flush_deferred()
                      cut("E0")

          deferred = []

          def tick_deferred():
              for d in list(deferred):
                  d[0] -= 1
                  if d[0] <= 0:
                      deferred.remove(d)
                      d[1]()

          def flush_deferred():
              while deferred:
                  d = deferred.pop(0)
                  d[1]()

          proj_head(0)
          cut("D")
          for h in range(8):
              if h + 1 < 8:
                  proj_head(h + 1)
              attn_head(h)
          flush_deferred()
          if debug and s == 0:
              dump("yaT", BCb[:, 8:16, :].rearrange("p k t -> p (k t)"),
                   [("ya", c, T) for c in range(8) for T in range(4)])
          cut("E")

          for c in range(8):
              wsl, wk = wnext()
              w = wsl[:, 0:4096].rearrange("p (k n) -> p k n", k=8)
              for T in range(NSUP):
                  cols = slice(T * 512, (T + 1) * 512)
                  b_gr = ps1()
                  mm_group(b_gr, 0, 512, lambda k: w[:, k, 0:128], lambda k: hT[:, k, cols], 8, wk + hT_keys(T))
                  b_ga = ps1()
                  mm_group(b_ga, 0, 512, lambda k: w[:, k, 128:256], lambda k: hT[:, k, cols], 8, wk + hT_keys(T))
                  b_br = ps1()
                  mm_group(b_br, 0, 512, lambda k: w[:, k, 256:384], lambda k: BCb[:, k, cols], 8,
                           wk + [("yr", k, T) for k in range(8)])
                  b_ba = ps1()
                  mm_group(b_ba, 0, 512, lambda k: w[:, k, 384:512], lambda k: BCb[:, 8 + k, cols], 8,
                           wk + [("ya", k, T) for k in range(8)])
                  tr_, trk = tmp(512)
                  sc.op("act", [("ps", b_gr)], [trk], lambda e: e.activation(tr_, bank(b_gr), AF.Tanh, scale=0.5))
                  ta_, tak = tmp(512)
                  sc.op("act", [("ps", b_ga)], [tak], lambda e: e.activation(ta_, bank(b_ga), AF.Tanh, scale=0.5))
                  sc.op("dve", [trk, ("ps", b_br)], [trk], lambda e: e.scalar_tensor_tensor(
                      tr_, tr_, 1.0, bank(b_br), ALU.add, ALU.mult))
                  sc.op("dve", [tak, ("ps", b_ba)], [tak], lambda e: e.scalar_tensor_tensor(
                      ta_, ta_, 1.0, bank(b_ba), ALU.add, ALU.mult))
                  sc.op("dve", [trk, tak], [("m", c, T)] + ATT_KEYS, lambda e: e.tensor_tensor(
                      mT[:, c, cols], tr_, ta_, ALU.add))
          if debug and s == 0:
              dump("mT", mT.rearrange("p k t -> p (k t)"), [("m", c, T) for c in range(8) for T in range(4)])
          cut("E2")

          load_gain(1)
          wF = []
          for hf in range(2):
              wsl, wk = wnext(second=(hf == 1))
              wF.append((wsl[:, 0:4096].rearrange("p (k n) -> p k n", k=8), wk))
          pendF = []
          for i in range(NT):
              xt, xk = xt_r.get()
              sc.dma("sp", xt[:, :], x_d[s, i * 128:(i + 1) * 128, :], [], [xk])
              b = ps2()
              for half in range(2):
                  w, wk = wF[half]
                  mm_group(b + half, 0, 512, lambda k: mT[:, k, i * 128:(i + 1) * 128],
                           lambda k: w[:, k, :], 8, wk + [("m", k, i // 4) for k in range(8)])
              alias = [("yr", i, T) for T in range(4)] if i < 8 else [("ya", i - 8, T) for T in range(4)]
              sc.op("dve", [("ps", b), ("ps", b + 1), xk], [("x1", i)] + alias, lambda e: e.scalar_tensor_tensor(
                  x1[:, i, :], ps[:, b * 512:(b + 2) * 512], 0.5, xt[:, :], ALU.mult, ALU.add))
              if len(pendF) >= 2:
                  pendF.pop(0)()
              pendF.append(norm_to_T(x1[:, i, :], ("x1", i), i))
          while pendF:
              pendF.pop(0)()
          if debug and s == 0:
              dump("x1", x1[:, :, :].rearrange("p k t -> p (k t)"), [("x1", i) for i in range(NT)])
          cut("F")

          load_gain(2)
          TP.pool = pool_b
          gw = {}
          gitems = [(g, T) for g in range(8) for T in range(NSUP)]
          hid_of = {}

          def emit_mlp1(k):
              g, T = gitems[k]
              if T == 0:
                  wsl, wk1 = wnext(second=True)
                  gw[("w1", g)] = (wsl[:, 0:4096].rearrange("p (k n) -> p k n", k=8), wk1)
              w1, wk1 = gw[("w1", g)]
              cols = slice(T * 512, (T + 1) * 512)
              hids = []
              for hh in range(2):
                  b = ps2()
                  for cc in range(2):
                      ch = hh * 2 + cc
                      mm_group(b + cc, 0, 512, lambda k: w1[:, k, ch * 128:(ch + 1) * 128],
                               lambda k: hT[:, k, cols], 8, wk1 + hT_keys(T))
                  hid, hidk = tmp(1024, BF16)
                  for cc in range(2):
                      m_, mk = tmp(512)
                      sc.op("act", [("ps", b + cc)], [mk], lambda e: e.activation(m_, bank(b + cc), AF.Relu))
                      sc.op("dve", [mk], [hidk + (cc,)], lambda e: e.tensor_tensor(
                          hid[:, cc * 512:(cc + 1) * 512], m_, m_, ALU.mult))
                  hids.append((hid, hidk))
              hid_of[k] = hids

          def emit_mlp2(k):
              g, T = gitems[k]
              if T == 0:
                  wsl2, wk2 = wnext(second=True)
                  gw[("w2", g)] = (wsl2[:, 0:4096].rearrange("p (k n) -> p k n", k=4), wk2)
              w2, wk2 = gw[("w2", g)]
              hids = hid_of.pop(k)
              for r in range(4):
                  i = 4 * T + r
                  b = ps2()
                  for half in range(2):
                      mm_group(b + half, 0, 512,
                               lambda k_: hids[k_ // 2][0][:, (k_ % 2) * 512 + r * 128:(k_ % 2) * 512 + (r + 1) * 128],
                               lambda k_: w2[:, k_, half * 512:(half + 1) * 512], 4,
                               wk2 + [hids[a_][1] + (c_,) for a_ in range(2) for c_ in range(2)])
                  sc.op("dve", [("ps", b), ("ps", b + 1), ("x1", i)], [("x1", i)], lambda e: e.tensor_tensor(
                      x1[:, i, :], ps[:, b * 512:(b + 2) * 512], x1[:, i, :], ALU.add))
                  if g == 7:
                      stt_, stk = st_r.get()
                      sc.op("act", [("x1", i)], [stk, "junk"], lambda e: e.activation(
                          junk[:, :], x1[:, i, :], AF.Square, scale=1.0 / 32.0, accum_out=stt_[:, 0:1]))
                      sc.op("act", [stk, "epsc"], [stk + ("l",)], lambda e: e.activation(
                          stt_[:, 2:3], stt_[:, 0:1], AF.Ln, bias=epsc[:, 0:1]))
                      sc.op("act", [stk + ("l",)], [stk + ("r",)], lambda e: e.activation(
                          stt_[:, 1:2], stt_[:, 2:3], AF.Exp, scale=-0.5))
                      ob, obk = xt_r.get()
                      sc.op("dve", [("x1", i), stk + ("r",), "gain"], [obk], lambda e: e.scalar_tensor_tensor(
                          ob[:, :], x1[:, i, :], stt_[:, 1:2], gain[:, :], ALU.mult, ALU.mult))
                      sc.dma("sp", out_d[s, i * 128:(i + 1) * 128, :], ob[:, :], [obk], [], is_out=True)

          emit_mlp1(0)
          for k in range(len(gitems)):
              if k + 1 < len(gitems):
                  emit_mlp1(k + 1)
              emit_mlp2(k)
          TP.pool = pool_a
          if debug:
              break

    except _Stop:
        pass
    sc.finish()
    return nc


def _tile_rows(w, ncols_group):
    K, N = w.shape
    kc = K // 128
    g = N // ncols_group
    return np.ascontiguousarray(
        w.reshape(kc, 128, g, ncols_group).transpose(2, 1, 0, 3).reshape(g, 128, kc * ncols_group))


def _host_layout(inp):
    f = np.float32
    w_in = np.asarray(inp["w_in"][0], f)
    ux = w_in[:, 0:1024].reshape(1024, 8, 128)
    ug = w_in[:, 1024:2048].reshape(1024, 8, 128)
    wB = _tile_rows(np.concatenate([ux, ug], axis=2).reshape(1024, 8 * 256), 256)
    q = w_in[:, 2048:3072].reshape(1024, 8, 128)
    k = w_in[:, 3072:4096].reshape(1024, 8, 128)
    v = w_in[:, 4096:5120].reshape(1024, 8, 128)
    wD = _tile_rows(np.concatenate([q, k, v], axis=2).reshape(1024, 8 * 384), 384)
    gr = w_in[:, 5120:6144].reshape(1024, 8, 128)
    ga = w_in[:, 6144:7168].reshape(1024, 8, 128)
    br = np.asarray(inp["w_br_rnn"][0], f).reshape(1024, 8, 128)
    ba = np.asarray(inp["w_br_attn"][0], f).reshape(1024, 8, 128)
    wE = _tile_rows(np.concatenate([gr, ga, br, ba], axis=2).reshape(1024, 8 * 512), 512)
    wF = _tile_rows(np.asarray(inp["w_out"][0], f), 512)
    wG1 = _tile_rows(np.asarray(inp["w_mlp1"][0], f), 512)
    w2 = np.asarray(inp["w_mlp2"][0], f)
    wG2 = np.ascontiguousarray(w2.reshape(8, 4, 128, 1024).transpose(0, 2, 1, 3).reshape(8, 128, 4096))
    gains = np.stack([np.broadcast_to(np.asarray(inp[n], f).reshape(1, D), (128, D))
                      for n in ("norm1_g", "norm2_g", "normf_g")]).copy()

    def fm(vv):
        return np.asarray(vv, f).reshape(8, 128).T
    chp = np.zeros((128, 8, 8), f)
    cw = np.asarray(inp["conv_w"][0], f)
    for j in range(4):
        chp[:, :, j] = fm(cw[j])
    chp[:, :, 4] = fm(inp["conv_b"][0])
    chp[:, :, 5] = fm(np.asarray(inp["rg_a_b"][0]).reshape(-1))
    chp[:, :, 6] = fm(np.asarray(inp["rg_x_b"][0]).reshape(-1))
    chp[:, :, 7] = fm(inp["lru_lambda"][0])
    wbd = np.zeros((128, 2, 8, 128), f)
    for a, nm in enumerate(("rg_a_w", "rg_x_w")):
        ww = np.asarray(inp[nm][0], f)
        for c in range(8):
            wbd[0:64, a, c, 0:64] = ww[2 * c]
            wbd[64:128, a, c, 64:128] = ww[2 * c + 1]
    half = 32
    inv_freq = 10000.0 ** (-np.arange(half, dtype=np.float64) * 2.0 / 64.0)
    ang = np.arange(S, dtype=np.float64)[:, None] * inv_freq[None, :]
    rope = np.zeros((128, 2, 16, 32), f)
    rope[:, 0] = np.cos(ang).astype(f).reshape(16, 128, 32).transpose(1, 0, 2)
    rope[:, 1] = np.sin(ang).astype(f).reshape(16, 128, 32).transpose(1, 0, 2)
    lamv = np.stack([np.broadcast_to(np.asarray(inp[n][0], f).reshape(1, 64), (128, 64))
                     for n in ("lambda_q1", "lambda_k1", "lambda_q2", "lambda_k2")], axis=1).copy()
    subg = np.broadcast_to(np.asarray(inp["subln_g"][0], f).reshape(1, 128), (128, 128)).copy()
    return {
        "wB": wB, "wD": wD, "wE": wE, "wF": wF, "wG1": wG1, "wG2": wG2,
        "gains": gains, "chp": chp.reshape(128, 64), "wbd": wbd.reshape(128, 2048),
        "rope": rope.reshape(128, 1024), "lamv": lamv.reshape(128, 256), "subg": subg,
        "ident": np.eye(128, dtype=f),
    }


_NC_CACHE = {}


def kernel(**inputs):
    x = np.ascontiguousarray(np.asarray(inputs["x"], np.float32))
    shared = _host_layout(inputs)
    if "nc" not in _NC_CACHE:
        _NC_CACHE["nc"] = build(False)
    nc = _NC_CACHE["nc"]
    in_maps = []
    for c in range(NCORES):
        m = dict(shared)
        m["x"] = x[c * NSEQ:(c + 1) * NSEQ]
        in_maps.append(m)
    res = run_bass_kernel_spmd(nc, in_maps, core_ids=list(range(NCORES)))
    return np.concatenate([np.asarray(r["out"], np.float32) for r in res.results], axis=0)
```

```python
import math
import os
from contextlib import ExitStack

import numpy as np
import concourse.bass as bass
import concourse.mybir as mybir
from concourse.bass_utils import run_bass_kernel_spmd

F32 = mybir.dt.float32
BF16 = mybir.dt.bfloat16
AF = mybir.ActivationFunctionType
ALU = mybir.AluOpType
AX = mybir.AxisListType

D = 1024
S = 2048
NT = 16
NSUP = 4
NSEQ = 2
NCORES = 8
NDS = 16
LAM_INIT = 0.8 - 0.6 * math.exp(0.0)
NORM_EPS = 1e-6
NSTREAM = 3
LV = int(os.environ.get('KDBG_LV', '9'))
SUBLN_EPS = 1e-5
GELU_C = 0.7978845608028654


class Sched:
    def __init__(self, nc, es):
        self.nc = nc
        self.E = {"pe": nc.tensor, "act": nc.scalar, "dve": nc.vector, "pool": nc.gpsimd, "sp": nc.sync}
        self.sems = {e: es.enter_context(nc.semaphore("c_" + e)) for e in ("pe", "act", "dve", "pool")}
        self.cnt = {e: 0 for e in self.sems}
        self.sid = {e: "c_" + e for e in self.sems}
        self.epoch = 0
        self.seen = {e: {} for e in self.E}
        self.lastw = {}
        self.readers = {}
        self.dsems = [es.enter_context(nc.semaphore("d%d" % i)) for i in range(NDS)]
        self.dval = [0] * NDS
        self.di = 0
        self.nrd = 0
        self.out_toks = []
        self.own_last = {}

    def _deps(self, reads, writes):
        deps = []
        for k in reads:
            t = self.lastw.get(k)
            if t is not None:
                deps.append((t, 0))
            if k[0] == "ps":
                for t in self.readers.get(k, {}).values():
                    deps.append((t, 1))
        for k in writes:
            t = self.lastw.get(k)
            if t is not None:
                deps.append((t, 1))
            for t in self.readers.get(k, {}).values():
                deps.append((t, 1))
        return deps

    def _wait(self, eng, deps, is_dma):
        e = self.E[eng]
        seen = self.seen[eng]
        for (t, kind) in deps:
            sem, val, teng, sid = t
            if (not is_dma) and teng == eng:
                if eng == "pe" or kind != 0:
                    continue
            if seen.get(sid, 0) >= val:
                continue
            e.wait_ge(sem, val)
            seen[sid] = val

    def _record(self, tok, reads, writes):
        for k in writes:
            self.lastw[k] = tok
            self.readers[k] = {}
        for k in reads:
            if k in writes:
                continue
            d = self.readers.setdefault(k, {})
            if tok[2] == "dma":
                self.nrd += 1
                d[("dma", self.nrd)] = tok
            else:
                d[tok[2]] = tok

    def new_epoch(self, es):
        self.epoch += 1
        for e in ("pe", "act", "dve"):
            self.sems[e] = es.enter_context(self.nc.semaphore("c%d_%s" % (self.epoch, e)))
            self.cnt[e] = 0
            self.sid[e] = "c%d_%s" % (self.epoch, e)

    def op(self, eng, reads, writes, fn):
        self._wait(eng, self._deps(reads, writes), False)
        ins = fn(self.E[eng])
        self.cnt[eng] += 1
        ins.then_inc(self.sems[eng], 1)
        tok = (self.sems[eng], self.cnt[eng], eng, self.sid[eng])
        self._record(tok, reads, writes)
        return tok

    def dma_own(self, q, out, in_, reads, writes, own):
        self._wait(q, self._deps(reads, writes), True)
        sem, st = own
        if st["n"] > 0:
            self.E[q].wait_ge(sem, 16)
            self.E[q].sem_clear(sem)
        st["n"] += 1
        ins = self.E[q].dma_start(out=out, in_=in_)
        ins.then_inc(sem, 16)
        tok = (sem, 16, "dma", "own%s_%d" % (st["name"], st["n"]))
        self._record(tok, reads, writes)
        self.own_last[st["name"]] = (sem, tok[3])
        return tok

    def dma(self, q, out, in_, reads, writes, is_out=False):
        self._wait(q, self._deps(reads, writes), True)
        i = self.di
        self.di = (self.di + 1) % NDS
        sem = self.dsems[i]
        sid = "d%d" % i
        if self.dval[i] > 0 and self.seen[q].get(sid, 0) < self.dval[i]:
            self.E[q].wait_ge(sem, self.dval[i])
            self.seen[q][sid] = self.dval[i]
        ins = self.E[q].dma_start(out=out, in_=in_)
        self.dval[i] += 16
        ins.then_inc(sem, 16)
        tok = (sem, self.dval[i], "dma", sid)
        self._record(tok, reads, writes)
        if is_out:
            self.out_toks.append(tok)
        return tok

    def finish(self):
        e = self.E["sp"]
        for name, (sem, sid) in self.own_last.items():
            if self.seen["sp"].get(sid, 0) < 16:
                e.wait_ge(sem, 16)
        for i in range(NDS):
            sid = "d%d" % i
            if self.dval[i] > 0 and self.seen["sp"].get(sid, 0) < self.dval[i]:
                e.wait_ge(self.dsems[i], self.dval[i])
                self.seen["sp"][sid] = self.dval[i]
        for (sem, val, _, sid) in self.out_toks:
            if self.seen["sp"].get(sid, 0) < val:
                e.wait_ge(sem, val)
                self.seen["sp"][sid] = val


class Rot:
    def __init__(self, nc, name, n, shape, dtype):
        self.bufs = [nc.alloc_sbuf_tensor("sb_%s%d" % (name, i), shape, dtype) for i in range(n)]
        self.name = name
        self.i = 0

    def get(self):
        j = self.i % len(self.bufs)
        self.i += 1
        return self.bufs[j], (self.name, j)


class _Stop(Exception):
    pass


def build(debug=False, upto=None, skipB=False):
    nc = bass.Bass("TRN2", target_bir_lowering=False)
    es = ExitStack()

    def dram(name, shape, dt=F32, kind="ExternalInput"):
        return nc.dram_tensor(name, shape, dt, kind=kind).ap()

    x_d = dram("x", [NSEQ, S, D])
    out_d = dram("out", [NSEQ, S, D], kind="ExternalOutput")
    wB_d = dram("wB", [8, 128, 8 * 256])
    wD_d = dram("wD", [8, 128, 8 * 384])
    wE_d = dram("wE", [8, 128, 8 * 512])
    wF_d = dram("wF", [2, 128, 8 * 512])
    wG1_d = dram("wG1", [8, 128, 8 * 512])
    wG2_d = dram("wG2", [8, 128, 4 * 1024])
    gains_d = dram("gains", [3, 128, D])
    chp_d = dram("chp", [128, 8 * 8])
    wbd_d = dram("wbd", [128, 2 * 8 * 128])
    rope_d = dram("rope", [128, 2 * 16 * 32])
    lamv_d = dram("lamv", [128, 4 * 64])
    subg_d = dram("subg", [128, 128])
    ident_d = dram("ident", [128, 128])
    dbg = {}
    if debug:
        for nm in ("hT", "yrT", "yaT", "mT"):
            dbg[nm] = dram("dbg_" + nm, [128, 8 * S], BF16, kind="ExternalOutput")
        dbg["x1"] = dram("dbg_x1", [128, 16 * D], F32, kind="ExternalOutput")

    sc = Sched(nc, es)
    def A(name, shape, dt):
        return nc.alloc_sbuf_tensor("sb_" + name, shape, dt)

    hT = A("hT", [128, 8, S], BF16)
    BC = A("BC", [128, 16 * D], F32)
    BCb = BC.bitcast(BF16).reshape([128, 16, S])
    x1 = BC.reshape([128, 16, D])
    ATT = A("ATT", [128, 16448], BF16)
    mT = ATT[:, 0:16384].rearrange("p (k t) -> p k t", k=8)
    qkT = [ATT[:, i * 4096:(i + 1) * 4096].rearrange("p (a t) -> p a t", a=2) for i in range(2)]
    V1 = [ATT[:, 8192 + i * 2080: 8192 + (i + 1) * 2080].rearrange("p (t e) -> p t e", t=NT) for i in range(2)]
    yat = [ATT[:, 12352 + i * 2048: 12352 + (i + 1) * 2048].rearrange("p (t e) -> p t e", t=NT) for i in range(2)]
    ATT_KEYS = ([("qkT", a, T) for a in range(2) for T in range(4)] + [("V1", a) for a in range(2)]
                + [("yat", a, T) for a in range(2) for T in range(4)])
    M_KEYS = [("m", c, T) for c in range(8) for T in range(4)]
    wslot = [A("wslot%d" % i, [128, 4096], BF16) for i in range(3)]
    gain = A("gain", [128, D], F32)
    chp = A("chp", [128, 8, 8], F32)
    chq = A("chq", [128, 8, 4], F32)
    wbd = A("wbd", [128, 2, 8, 128], BF16)
    rope = A("rope", [128, 2, 16, 32], F32)
    lamv = A("lamv", [128, 4, 64], F32)
    subg = A("subg", [128, 128], F32)
    ident = A("ident", [128, 128], BF16)
    neglam = A("neglam", [128, 1], F32)
    junk = A("junk", [128, D], BF16)
    setup_t = A("setup_t", [128, 64], F32)
    setup_u = A("setup_u", [128, 64], F32)
    setup_s = A("setup_s", [128, 8], F32)
    setup_v = A("setup_v", [128, 4], F32)
    halo = A("halo", [128, 4], F32)
    hlast = A("hlast", [128, 1], F32)
    epsc = A("epsc", [128, 2], F32)

    ps = nc.alloc_psum_tensor("ps", [128, 4096], F32)
    psb = ps.bitcast(BF16)

    class PS:
        nxt = 0

    def ps1():
        b = PS.nxt % 8
        PS.nxt += 1
        return b

    def ps2():
        if PS.nxt % 2:
            PS.nxt += 1
        b = PS.nxt % 8
        PS.nxt += 2
        return b

    class PSL:
        nxt = 0

    def ps1lo():
        b = PSL.nxt % 4
        PSL.nxt += 1
        return b

    class PSP3:
        nxt = 0

    def ps1p3():
        b = PSP3.nxt % 3
        PSP3.nxt += 1
        return b

    def bank(b, n=512, off=0):
        return ps[:, b * 512 + off: b * 512 + off + n]

    xt_r = Rot(nc, "xt", 2, [128, D], F32)
    st_r = Rot(nc, "st", 12, [128, 4], F32)
    fino_r = Rot(nc, "fino", 6, [128, 256], F32)
    NBLK = 11
    blk = [A("blk%d" % i, [128, 516], F32) for i in range(NBLK)]
    blkb = [b_.bitcast(BF16) for b_ in blk]

    ATTf = ATT.bitcast(F32)
    NAB = 15
    pool_a = [(blk[j], blkb[j], ("blk", j)) for j in range(NBLK)]
    pool_b = pool_a + [(ATTf[:, j * 516:(j + 1) * 516], ATT[:, j * 1032:(j + 1) * 1032], ("ab", j))
                       for j in range(NAB)]
    M_KEYS.extend([("ab", j) for j in range(NAB)])

    class TP:
        i = 0
        pool = pool_a

    def tmp(n, dt=F32):
        j = TP.i % len(TP.pool)
        TP.i += 1
        f, bview, key = TP.pool[j]
        if dt == F32:
            return f[:, 0:n], key
        return bview[:, 0:n], key

    halo2 = [A("halo2_%d" % i, [128, 4], F32) for i in range(NSTREAM)]
    hlast2 = [A("hlast2_%d" % i, [128, 1], F32) for i in range(NSTREAM)]

    nc.allow_low_precision("bf16 matmul operands with fp32 accumulation (per problem tolerance)")

    sc.dma("sp", chp[:, :, :], chp_d.rearrange("p (c k) -> p c k", c=8), [], ["chp"])
    sc.dma("sp", rope[:, :, :, :], rope_d.rearrange("p (a t f) -> p a t f", a=2, t=16), [], ["rope"])
    sc.dma("sp", lamv[:, :, :], lamv_d.rearrange("p (a f) -> p a f", a=4), [], ["lamv"])
    sc.dma("sp", subg[:, :], subg_d, [], ["subg"])
    def own_sem(name):
        return (es.enter_context(nc.semaphore("o_" + name)), {"n": 0, "name": name})
    sc.dma_own("pool", wbd[:, :, :, :], wbd_d.rearrange("p (a c f) -> p a c f", a=2, c=8), [], ["wbd"],
               own_sem("wbd"))
    sc.dma_own("pool", ident[:, :], ident_d, [], ["ident"], own_sem("ident"))

    wlist = []
    for s in range(NSEQ):
        for cp in range(0 if skipB else 4):
            wlist.append((wB_d[2 * cp:2 * cp + 2].rearrange("c p n -> p c n"), 4096))
        for h in range(8):
            wlist.append((wD_d[h], 3072))
        for c in range(8):
            wlist.append((wE_d[c], 4096))
        for hf in range(2):
            wlist.append((wF_d[hf], 4096))
        for g in range(8):
            wlist.append((wG1_d[g], 4096))
            wlist.append((wG2_d[g], 4096))
        if debug:
            break
    WS = {"n": 0, "loaded": 0}

    def wnext(second=False):
        n = WS["n"]
        WS["n"] += 1
        while WS["loaded"] < min(n + (2 if second else 3), len(wlist)):
            m = WS["loaded"]
            src_, ncol = wlist[m]
            dst_ = wslot[m % 3][:, 0:ncol]
            if len(src_.shape) == 3:
                dst_ = dst_.rearrange("p (c n) -> p c n", c=src_.shape[1])
            sc.dma_own("pool", dst_, src_, [], [("w", m % 3)], own_sem("w%d" % m))
            WS["loaded"] += 1
        return wslot[n % 3], [("w", n % 3)]

    sc.op("dve", [], ["epsc"], lambda e: e.memset(epsc[:, 0:1], float(NORM_EPS)))
    sc.op("dve", [], ["epsc"], lambda e: e.memset(epsc[:, 1:2], float(SUBLN_EPS)))
    sc.op("dve", ["subg"], ["subg"], lambda e: e.tensor_scalar(
        subg[:, :], subg[:, :], (1.0 - LAM_INIT), None, ALU.mult))
    sc.op("act", ["chp"], ["setup_s"], lambda e: e.activation(
        setup_s[:, :], chp[:, :, 7], AF.Exp, scale=-1.0))
    sc.op("act", ["setup_s"], ["setup_s"], lambda e: e.activation(
        setup_s[:, :], setup_s[:, :], AF.Ln, bias=1.0))
    sc.op("dve", ["setup_s"], ["chq"], lambda e: e.tensor_scalar(
        chq[:, :, 2], setup_s[:, :], -8.0, None, ALU.mult))
    sc.op("dve", ["setup_s"], ["chq"], lambda e: e.tensor_scalar(
        chq[:, :, 3], setup_s[:, :], -4.0, None, ALU.mult))
    sc.op("dve", ["chp"], ["chq"], lambda e: e.tensor_scalar(
        chq[:, :, 0:2], chp[:, :, 5:7], 0.5, None, ALU.mult))
    sc.op("dve", ["lamv"], ["setup_t"], lambda e: e.tensor_tensor(
        setup_t[:, :], lamv[:, 0, :], lamv[:, 1, :], ALU.mult))
    sc.op("dve", ["setup_t"], ["sv0"], lambda e: e.tensor_reduce(
        setup_v[:, 0:1], setup_t[:, :], AX.X, ALU.add))
    sc.op("dve", ["lamv"], ["setup_u"], lambda e: e.tensor_tensor(
        setup_u[:, :], lamv[:, 2, :], lamv[:, 3, :], ALU.mult))
    sc.op("dve", ["setup_u"], ["sv1"], lambda e: e.tensor_reduce(
        setup_v[:, 1:2], setup_u[:, :], AX.X, ALU.add))
    sc.op("act", ["sv0"], ["sv2"], lambda e: e.activation(setup_v[:, 2:3], setup_v[:, 0:1], AF.Exp))
    sc.op("act", ["sv1"], ["sv3"], lambda e: e.activation(setup_v[:, 3:4], setup_v[:, 1:2], AF.Exp))
    sc.op("dve", ["sv2", "sv3"], ["neglam"], lambda e: e.scalar_tensor_tensor(
        neglam[:, :], setup_v[:, 3:4], -LAM_INIT, setup_v[:, 2:3], ALU.add, ALU.subtract))

    def load_gain(idx):
        sc.dma("sp", gain[:, :], gains_d[idx], [], ["gain"])

    def norm_to_T(src_ap, src_key, i):
        stt_, stk = st_r.get()
        sc.op("act", [src_key], [stk, "junk"], lambda e: e.activation(
            junk[:, :], src_ap, AF.Square, scale=1.0 / 32.0, accum_out=stt_[:, 0:1]))
        sc.op("act", [stk, "epsc"], [stk + ("l",)], lambda e: e.activation(
            stt_[:, 2:3], stt_[:, 0:1], AF.Ln, bias=epsc[:, 0:1]))
        sc.op("act", [stk + ("l",)], [stk + ("r",)], lambda e: e.activation(
            stt_[:, 1:2], stt_[:, 2:3], AF.Exp, scale=-0.5))
        hb, hbk = tmp(1024, BF16)
        sc.op("dve", [src_key, stk + ("r",), "gain"], [hbk], lambda e: e.scalar_tensor_tensor(
            hb, src_ap, stt_[:, 1:2], gain[:, :], ALU.mult, ALU.mult))
        def part_b():
            b = ps1()

            def tr(e):
                ins = None
                for kc in range(8):
                    ins = e.transpose(psb[:, b * 1024 + kc * 128: b * 1024 + (kc + 1) * 128],
                                      hb[:, kc * 128:(kc + 1) * 128], ident[:, :])
                return ins
            sc.op("pe", [hbk, "ident"], [("ps", b)], tr)
            sc.op("act", [("ps", b)], [("hT", i)], lambda e: e.copy(
                hT[:, :, i * 128:(i + 1) * 128],
                psb[:, b * 1024:(b + 1) * 1024].rearrange("p (k t) -> p k t", k=8)))
        return part_b

    def mm_group(b, off, n, lhs_fn, rhs_fn, nk, reads):
        def f(e):
            ins = None
            for k in range(nk):
                ins = e.matmul(bank(b, n, off), lhs_fn(k), rhs_fn(k), start=(k == 0), stop=(k == nk - 1))
            return ins
        return sc.op("pe", reads, [("ps", b)], f)

    def hT_keys(T):
        return [("hT", 4 * T + r) for r in range(4)]

    def dump(name, ap_sb, keys):
        if debug:
            sc.dma("sp", dbg[name], ap_sb, keys, [], is_out=True)

    def cut(name):
        if upto == name:
            raise _Stop()

    try:
      for s in range(NSEQ):
          if s > 0:
              sc.new_epoch(es)
          load_gain(0)
          pendA = []
          for i in range(NT):
              xt, xk = xt_r.get()
              sc.dma("sp", xt[:, :], x_d[s, i * 128:(i + 1) * 128, :], [], [xk])
              pb = norm_to_T(xt[:, :], xk, i)
              if len(pendA) >= 2:
                  pendA.pop(0)()
              pendA.append(pb)
          while pendA:
              pendA.pop(0)()
          if debug and s == 0:
              dump("hT", hT[:, :, :].rearrange("p k t -> p (k t)"), [("hT", i) for i in range(NT)])
          cut("A")

          def b_stream(c, w, wk, sl):
              halo_c = halo2[sl]
              hlast_c = hlast2[sl]
              hk = ("halo", sl)
              hlk = ("hlast", sl)
              for T in range(NSUP):
                  cols = slice(T * 512, (T + 1) * 512)
                  b_ux = 2 * sl + 1
                  mm_group(b_ux, 0, 512, lambda k: w[:, k, 0:128], lambda k: hT[:, k, cols], 8, wk + hT_keys(T))
                  b_ug = 2 * sl
                  mm_group(b_ug, 0, 512, lambda k: w[:, k, 128:256], lambda k: hT[:, k, cols], 8, wk + hT_keys(T))
                  yield
                  ux, uxk = tmp(515)
                  if T == 0:
                      sc.op("dve", [], [uxk], lambda e: e.memset(ux[:, 0:3], 0.0))
                  else:
                      sc.op("dve", [hk], [uxk], lambda e: e.tensor_copy(ux[:, 0:3], halo_c[:, 0:3]))
                  sc.op("act", [("ps", b_ux)], [uxk + ("m",)], lambda e: e.copy(ux[:, 3:515], bank(b_ux)))
                  yield
                  uxr = [uxk, uxk + ("m",)]
                  if T < NSUP - 1:
                      sc.op("dve", uxr, [hk], lambda e: e.tensor_copy(halo_c[:, 0:3], ux[:, 512:515]))
                  xr, xrk = tmp(512)
                  sc.op("dve", uxr + ["chp"], [xrk], lambda e: e.tensor_scalar(
                      xr, ux[:, 0:512], chp[:, c, 0:1], chp[:, c, 4:5], ALU.mult, ALU.add))
                  for j in range(1, 4):
                      sc.op("dve", uxr + [xrk, "chp"], [xrk], lambda e, j=j: e.scalar_tensor_tensor(
                          xr, ux[:, j:j + 512], chp[:, c, j:j + 1], xr, ALU.mult, ALU.add))
                  yield
                  xb, xbk = tmp(512, BF16)
                  sc.op("act", [xrk], [xbk], lambda e: e.copy(xb, xr))
                  b_ga = 2 * sl + 1
                  mm_group(b_ga, 0, 512, lambda k: wbd[:, 0, c, :], lambda k: xb, 1, [xbk, "wbd"])
                  yield
                  tr_, trk = tmp(512)
                  sc.op("act", [("ps", b_ga), "chq"], [trk], lambda e: e.activation(
                      tr_, bank(b_ga), AF.Tanh, bias=chq[:, c, 0:1], scale=0.5))
                  b_gx = 2 * sl + 1
                  mm_group(b_gx, 0, 512, lambda k: wbd[:, 1, c, :], lambda k: xb, 1, [xbk, "wbd"])
                  yield
                  ti_, tik = tmp(512)
                  sc.op("act", [("ps", b_gx), "chq"], [tik], lambda e: e.activation(
                      ti_, bank(b_gx), AF.Tanh, bias=chq[:, c, 1:2], scale=0.5))
                  yield
                  a_, ak = tmp(512)
                  sc.op("act", [trk, "chq"], [ak], lambda e: e.activation(
                      a_, tr_, AF.Exp, bias=chq[:, c, 3:4], scale=chq[:, c, 3:4]))
                  a2_, a2k = tmp(512)
                  sc.op("act", [trk, "chq"], [a2k], lambda e: e.activation(
                      a2_, tr_, AF.Exp, bias=chq[:, c, 2:3], scale=chq[:, c, 2:3]))
                  sc.op("act", [trk, "chq"], [trk], lambda e: e.activation(
                      tr_, tr_, AF.Tanh, bias=chq[:, c, 3:4], scale=chq[:, c, 3:4]))
                  yield
                  sc.op("dve", [a2k, trk], [a2k], lambda e: e.scalar_tensor_tensor(
                      a2_, a2_, 1.0, tr_, ALU.add, ALU.mult))
                  sc.op("dve", [tik, xrk], [tik], lambda e: e.scalar_tensor_tensor(
                      ti_, ti_, 1.0, xr, ALU.add, ALU.mult))
                  yield
                  sc.op("act", [a2k], [a2k], lambda e: e.activation(a2_, a2_, AF.Ln, scale=-1.0))
                  sc.op("act", [a2k], [a2k], lambda e: e.activation(a2_, a2_, AF.Exp, scale=0.5))
                  sq, sqk = tmp(512)
                  sc.op("act", [("ps", b_ug)], [sqk], lambda e: e.activation(sq, bank(b_ug), AF.Square))
                  yield
                  sc.op("dve", [tik, a2k], [tik], lambda e: e.scalar_tensor_tensor(
                      ti_, ti_, 0.5, a2_, ALU.mult, ALU.mult))
                  hs, hsk = tmp(512)
                  if T == 0:
                      sc.op("dve", [ak, tik], [hsk], lambda e: e.tensor_tensor_scan(
                          hs, a_, ti_, 0.0, ALU.mult, ALU.add))
                  else:
                      sc.op("dve", [ak, tik, hlk], [hsk], lambda e: e.tensor_tensor_scan(
                          hs, a_, ti_, hlast_c[:, 0:1], ALU.mult, ALU.add))
                  if T < NSUP - 1:
                      sc.op("dve", [hsk], [hlk], lambda e: e.tensor_copy(hlast_c[:, 0:1], hs[:, 511:512]))
                  yield
                  sc.op("dve", [sqk], [sqk], lambda e: e.tensor_scalar(sq, sq, 0.044715, 1.0, ALU.mult, ALU.add))
                  sc.op("dve", [sqk, ("ps", b_ug)], [sqk], lambda e: e.tensor_tensor(sq, sq, bank(b_ug), ALU.mult))
                  yield
                  sc.op("act", [sqk], [sqk], lambda e: e.activation(sq, sq, AF.Tanh, scale=GELU_C))
                  yield
                  sc.op("dve", [sqk, ("ps", b_ug)], [sqk], lambda e: e.scalar_tensor_tensor(
                      sq, sq, 1.0, bank(b_ug), ALU.add, ALU.mult))
                  sc.op("dve", [sqk, hsk], [("yr", c, T), ("x1", c)], lambda e: e.scalar_tensor_tensor(
                      BCb[:, c, cols], sq, 0.5, hs, ALU.mult, ALU.mult))
                  yield

          TP.pool = pool_b
          active = []
          nxt = {"c": 0, "w": None}

          def start_next(sl):
              c = nxt["c"]
              if c >= (0 if skipB else 8):
                  return None
              nxt["c"] += 1
              if c % 2 == 0:
                  nxt["w"] = wnext(second=True)
              wslB, wk = nxt["w"]
              w = wslB[:, (c % 2) * 2048:(c % 2 + 1) * 2048].rearrange("p (k n) -> p k n", k=8)
              return (b_stream(c, w, wk, sl), sl)
          for sl in range(NSTREAM):
              g_ = start_next(sl)
              if g_ is not None:
                  active.append(g_)
          while active:
              for item in list(active):
                  g_, sl = item
                  try:
                      next(g_)
                  except StopIteration:
                      active.remove(item)
                      n_ = start_next(sl)
                      if n_ is not None:
                          active.append(n_)
          TP.pool = pool_a
          if debug and s == 0 and not skipB:
              dump("yrT", BCb[:, 0:8, :].rearrange("p k t -> p (k t)"),
                   [("yr", c, T) for c in range(8) for T in range(4)])
          cut("B")

          def proj_head(h):
              slot = h % 2
              wsl, wk = wnext()
              w = wsl[:, 0:3072].rearrange("p (k n) -> p k n", k=8)
              sc.op("dve", [], [("V1", slot)] + M_KEYS, lambda e: e.memset(V1[slot][:, :, 128:130], 1.0))
              bt = None
              pend = []
              for i in range(NT):
                  r = i % 4
                  b = ps1p3()
                  mm_group(b, 0, 384, lambda k: hT[:, k, i * 128:(i + 1) * 128], lambda k: w[:, k, :], 8,
                           wk + [("hT", i)])
                  if LV < 2:
                      continue
                  X = bank(b, 256).rearrange("p (a h f) -> p a h f", a=4, h=2)
                  cosb = rope[:, 0, i, :].unsqueeze(1).unsqueeze(1).broadcast_to([128, 4, 2, 32])
                  sinb = rope[:, 1, i, :].unsqueeze(1).broadcast_to([128, 4, 32])
                  t1, t1k = tmp(256)
                  t1v = t1.rearrange("p (a h f) -> p a h f", a=4, h=2)
                  sc.op("dve", [("ps", b), "rope"], [t1k], lambda e: e.tensor_tensor(t1v, X, cosb, ALU.mult))
                  tA, tAk = tmp(256)
                  tAv = tA.rearrange("p (a h f) -> p a h f", a=4, h=2)
                  sc.op("dve", [("ps", b), "rope"], [tAk], lambda e: e.tensor_tensor(
                      tAv[:, :, 0, :], X[:, :, 1, :], sinb, ALU.mult))
                  sc.op("dve", [("ps", b), "rope"], [tAk + ("b",)], lambda e: e.tensor_tensor(
                      tAv[:, :, 1, :], X[:, :, 0, :], sinb, ALU.mult))
                  qkb, qkbk = tmp(256, BF16)
                  qkv = qkb.rearrange("p (a h f) -> p a h f", a=4, h=2)
                  sc.op("dve", [t1k, tAk], [qkbk], lambda e: e.tensor_tensor(
                      qkv[:, :, 0, :], t1v[:, :, 0, :], tAv[:, :, 0, :], ALU.subtract))
                  sc.op("dve", [t1k, tAk + ("b",)], [qkbk + ("b",)], lambda e: e.tensor_tensor(
                      qkv[:, :, 1, :], t1v[:, :, 1, :], tAv[:, :, 1, :], ALU.add))
                  if LV < 3:
                      continue
                  sc.op("act", [("ps", b)], [("V1", slot)] + (M_KEYS if i == 0 else []), lambda e: e.copy(
                      V1[slot][:, i, 0:128], bank(b, 128, 256)))
                  if LV < 4:
                      continue
                  bt = 3

                  def tail(r=r, bt=bt, qkb=qkb, qkbk=qkbk, i=i):
                      def trq(e):
                          e.transpose(psb[:, bt * 1024 + r * 128: bt * 1024 + (r + 1) * 128], qkb[:, 0:128],
                                      ident[:, :])
                          return e.transpose(psb[:, bt * 1024 + 512 + r * 128: bt * 1024 + 512 + (r + 1) * 128],
                                             qkb[:, 128:256], ident[:, :])
                      sc.op("pe", [qkbk, qkbk + ("b",), "ident"], [("ps", bt)], trq)
                      if r == 3:
                          T = i // 4
                          sc.op("act", [("ps", bt)], [("qkT", slot, T)] + M_KEYS, lambda e: e.copy(
                              qkT[slot][:, :, T * 512:(T + 1) * 512],
                              psb[:, bt * 1024:(bt + 1) * 1024].rearrange("p (a t) -> p a t", a=2)))
                  if len(pend) >= 3:
                      pend.pop(0)()
                  pend.append(tail)
              while pend:
                  pend.pop(0)()

          def attn_head(h):
              slot = h % 2
              qT = qkT[slot]
              units = [(T, m, j) for T in range(NSUP) for m in range(2) for j in range(4 * T + 4)]
              LOOK = 3
              st_info = {}

              def emit_S(n):
                  T, m, j = units[n]
                  pr = slice(m * 64, (m + 1) * 64)
                  r0 = max(0, j - 4 * T)
                  N = (4 - r0) * 128
                  q0 = T * 512 + r0 * 128
                  bs = ps1lo()
                  sc.op("pe", [("qkT", slot, T), ("qkT", slot, j // 4)], [("ps", bs)],
                        lambda e: e.matmul(bank(bs, N), qT[pr, 1, j * 128:(j + 1) * 128], qT[pr, 0, q0:q0 + N],
                                           start=True, stop=True))
                  et, etk = tmp(512, BF16)
                  sc.op("act", [("ps", bs)], [etk], lambda e: e.activation(
                      et[:, 0:N], bank(bs, N), AF.Exp, scale=0.125))
                  if j >= 4 * T:
                      sc.op("pool", [etk], [etk], lambda e: e.memset(et[64:128, 0:64], 0.0))
                  st_info[n] = (et, etk, r0)

              def emit_PV(n):
                  T, m, j = units[n]
                  et, etk, r0 = st_info.pop(n)
                  ob = 4 + 2 * m

                  def pv(e):
                      ins = None
                      for r in range(r0, 4):
                          bb = ob + r // 2
                          off = (r % 2) * 130
                          ins = e.matmul(bank(bb, 130, off), et[:, (r - r0) * 128:(r - r0 + 1) * 128],
                                         V1[slot][:, j, :], start=(j == 0 and r % 2 == 0),
                                         stop=(j == 4 * T + r and r % 2 == 1))
                      return ins
                  sc.op("pe", [etk, ("V1", slot)], [("ps", ob), ("ps", ob + 1)], pv)

              def finalize(T):
                  osb = []
                  for m in range(2):
                      pair = []
                      for half in range(2):
                          dst, dk = tmp(260)
                          bb = 4 + 2 * m + half
                          sc.op("dve", [("ps", bb)], [dk], lambda e: e.tensor_copy(dst, bank(bb, 260)))
                          pair.append((dst, dk))
                      osb.append(pair)
                  parts = []
                  for half in range(2):
                      (s0, s0k) = osb[0][half]
                      (s1, s1k) = osb[1][half]
                      O0 = s0.rearrange("p (a f) -> p a f", a=2)
                      O1 = s1.rearrange("p (a f) -> p a f", a=2)
                      stt_, stk = st_r.get()
                      sc.op("dve", [s0k], [stk], lambda e: e.reciprocal(stt_[:, 0:2], O0[:, :, 128]))
                      sc.op("dve", [s1k], [stk + ("b",)], lambda e: e.reciprocal(stt_[:, 2:4], O1[:, :, 128]))
                      sc.op("dve", [stk + ("b",), "neglam"], [stk + ("b",)], lambda e: e.tensor_scalar(
                          stt_[:, 2:4], stt_[:, 2:4], neglam[:, 0:1], None, ALU.mult))
                      o0t, o0k = fino_r.get()
                      o0 = o0t[:, :]
                      o0v = o0.rearrange("p (a f) -> p a f", a=2)
                      sc.op("dve", [s0k, stk], [o0k], lambda e: e.tensor_tensor(
                          o0v, O0[:, :, 0:128], stt_[:, 0:2].unsqueeze(2).broadcast_to([128, 2, 128]), ALU.mult))
                      o1, o1k = tmp(256)
                      o1v = o1.rearrange("p (a f) -> p a f", a=2)
                      sc.op("dve", [s1k, stk + ("b",)], [o1k], lambda e: e.tensor_tensor(
                          o1v, O1[:, :, 0:128], stt_[:, 2:4].unsqueeze(2).broadcast_to([128, 2, 128]), ALU.mult))
                      sc.op("dve", [o0k, o1k], [o0k], lambda e: e.tensor_tensor(o0, o0, o1, ALU.add))
                      sc.op("dve", [o0k], [o1k], lambda e: e.tensor_tensor(o1, o0, o0, ALU.mult))
                      st2, st2k = st_r.get()
                      sc.op("dve", [o1k], [st2k], lambda e: e.tensor_reduce(st2[:, 0:2], o1v, AX.X, ALU.add))
                      parts.append((o0v, o0k, st2, st2k))

                  def part2():
                      for half, (o0v, o0k, st2, st2k) in enumerate(parts):
                          sc.op("act", [st2k, "epsc"], [st2k], lambda e: e.activation(
                              st2[:, 0:2], st2[:, 0:2], AF.Ln, bias=epsc[:, 1:2], scale=1.0 / 128.0))
                          sc.op("act", [st2k], [st2k], lambda e: e.activation(
                              st2[:, 0:2], st2[:, 0:2], AF.Exp, scale=-0.5))
                          sc.op("dve", [o0k, st2k], [o0k], lambda e: e.tensor_tensor(
                              o0v, o0v, st2[:, 0:2].unsqueeze(2).broadcast_to([128, 2, 128]), ALU.mult))
                          i0 = 4 * T + 2 * half
                          sc.op("dve", [o0k, "subg"], [("yat", slot, T)] + M_KEYS, lambda e: e.tensor_tensor(
                              yat[slot][:, i0:i0 + 2, :], o0v,
                              subg[:, :].unsqueeze(1).broadcast_to([128, 2, 128]), ALU.mult))

                  def part3():
                      bt = ps1lo()

                      def try_(e):
                          ins = None
                          for r in range(4):
                              ins = e.transpose(psb[:, bt * 1024 + r * 128: bt * 1024 + (r + 1) * 128],
                                                yat[slot][:, 4 * T + r, :], ident[:, :])
                          return ins
                      sc.op("pe", [("yat", slot, T), "ident"], [("ps", bt)], try_)
                      sc.op("act", [("ps", bt)], [("ya", h, T), ("x1", 8 + h)], lambda e: e.copy(
                          BCb[:, 8 + h, T * 512:(T + 1) * 512], psb[:, bt * 1024: bt * 1024 + 512]))
                  deferred.append([6, part2])
                  deferred.append([12, part3])

              nU = len(units)
              for n in range(min(LOOK, nU)):
                  emit_S(n)
              for n in range(nU):
                  if n + LOOK < nU:
                      emit_S(n + LOOK)
                  emit_PV(n)
                  tick_deferred()
                  T, m, j = units[n]
                  if m == 1 and j == 4 * T + 3:
                      finalize(T)
                      if upto == "E0":
                          flush_deferred()
                      cut("E0")

          deferred = []

          def tick_deferred():
              for d in list(deferred):
                  d[0] -= 1
                  if d[0] <= 0:
                      deferred.remove(d)
                      d[1]()

          def flush_deferred():
              while deferred:
                  d = deferred.pop(0)
                  d[1]()

          proj_head(0)
          cut("D")
          for h in range(8):
              if h + 1 < 8:
                  proj_head(h + 1)
              attn_head(h)
          flush_deferred()
          if debug and s == 0:
              dump("yaT", BCb[:, 8:16, :].rearrange("p k t -> p (k t)"),
                   [("ya", c, T) for c in range(8) for T in range(4)])
          cut("E")

          for c in range(8):
              wsl, wk = wnext()
              w = wsl[:, 0:4096].rearrange("p (k n) -> p k n", k=8)
              for T in range(NSUP):
                  cols = slice(T * 512, (T + 1) * 512)
                  b_gr = ps1()
                  mm_group(b_gr, 0, 512, lambda k: w[:, k, 0:128], lambda k: hT[:, k, cols], 8, wk + hT_keys(T))
                  b_ga = ps1()
                  mm_group(b_ga, 0, 512, lambda k: w[:, k, 128:256], lambda k: hT[:, k, cols], 8, wk + hT_keys(T))
                  b_br = ps1()
                  mm_group(b_br, 0, 512, lambda k: w[:, k, 256:384], lambda k: BCb[:, k, cols], 8,
                           wk + [("yr", k, T) for k in range(8)])
                  b_ba = ps1()
                  mm_group(b_ba, 0, 512, lambda k: w[:, k, 384:512], lambda k: BCb[:, 8 + k, cols], 8,
                           wk + [("ya", k, T) for k in range(8)])
                  tr_, trk = tmp(512)
                  sc.op("act", [("ps", b_gr)], [trk], lambda e: e.activation(tr_, bank(b_gr), AF.Tanh, scale=0.5))
                  ta_, tak = tmp(512)
                  sc.op("act", [("ps", b_ga)], [tak], lambda e: e.activation(ta_, bank(b_ga), AF.Tanh, scale=0.5))
                  sc.op("dve", [trk, ("ps", b_br)], [trk], lambda e: e.scalar_tensor_tensor(
                      tr_, tr_, 1.0, bank(b_br), ALU.add, ALU.mult))
                  sc.op("dve", [tak, ("ps", b_ba)], [tak], lambda e: e.scalar_tensor_tensor(
                      ta_, ta_, 1.0, bank(b_ba), ALU.add, ALU.mult))
                  sc.op("dve", [trk, tak], [("m", c, T)] + ATT_KEYS, lambda e: e.tensor_tensor(
                      mT[:, c, cols], tr_, ta_, ALU.add))
          if debug and s == 0:
              dump("mT", mT.rearrange("p k t -> p (k t)"), [("m", c, T) for c in range(8) for T in range(4)])
          cut("E2")

          load_gain(1)
          wF = []
          for hf in range(2):
              wsl, wk = wnext(second=(hf == 1))
              wF.append((wsl[:, 0:4096].rearrange("p (k n) -> p k n", k=8), wk))
          pendF = []
          for i in range(NT):
              xt, xk = xt_r.get()
              sc.dma("sp", xt[:, :], x_d[s, i * 128:(i + 1) * 128, :], [], [xk])
              b = ps2()
              for half in range(2):
                  w, wk = wF[half]
                  mm_group(b + half, 0, 512, lambda k: mT[:, k, i * 128:(i + 1) * 128],
                           lambda k: w[:, k, :], 8, wk + [("m", k, i // 4) for k in range(8)])
              alias = [("yr", i, T) for T in range(4)] if i < 8 else [("ya", i - 8, T) for T in range(4)]
              sc.op("dve", [("ps", b), ("ps", b + 1), xk], [("x1", i)] + alias, lambda e: e.scalar_tensor_tensor(
                  x1[:, i, :], ps[:, b * 512:(b + 2) * 512], 0.5, xt[:, :], ALU.mult, ALU.add))
              if len(pendF) >= 2:
                  pendF.pop(0)()
              pendF.append(norm_to_T(x1[:, i, :], ("x1", i), i))
          while pendF:
              pendF.pop(0)()
          if debug and s == 0:
              dump("x1", x1[:, :, :].rearrange("p k t -> p (k t)"), [("x1", i) for i in range(NT)])
          cut("F")

          load_gain(2)
          TP.pool = pool_b
          gw = {}
          gitems = [(g, T) for g in range(8) for T in range(NSUP)]
          hid_of = {}

          def emit_mlp1(k):
              g, T = gitems[k]
              if T == 0:
                  wsl, wk1 = wnext(second=True)
                  gw[("w1", g)] = (wsl[:, 0:4096].rearrange("p (k n) -> p k n", k=8), wk1)
              w1, wk1 = gw[("w1", g)]
              cols = slice(T * 512, (T + 1) * 512)
              hids = []
              for hh in range(2):
                  b = ps2()
                  for cc in range(2):
                      ch = hh * 2 + cc
                      mm_group(b + cc, 0, 512, lambda k: w1[:, k, ch * 128:(ch + 1) * 128],
                               lambda k: hT[:, k, cols], 8, wk1 + hT_keys(T))
                  hid, hidk = tmp(1024, BF16)
                  for cc in range(2):
                      m_, mk = tmp(512)
                      sc.op("act", [("ps", b + cc)], [mk], lambda e: e.activation(m_, bank(b + cc), AF.Relu))
                      sc.op("dve", [mk], [hidk + (cc,)], lambda e: e.tensor_tensor(
                          hid[:, cc * 512:(cc + 1) * 512], m_, m_, ALU.mult))
                  hids.append((hid, hidk))
              hid_of[k] = hids

          def emit_mlp2(k):
              g, T = gitems[k]
              if T == 0:
                  wsl2, wk2 = wnext(second=True)
                  gw[("w2", g)] = (wsl2[:, 0:4096].rearrange("p (k n) -> p k n", k=4), wk2)
              w2, wk2 = gw[("w2", g)]
              hids = hid_of.pop(k)
              for r in range(4):
                  i = 4 * T + r
                  b = ps2()
                  for half in range(2):
                      mm_group(b + half, 0, 512,
                               lambda k_: hids[k_ // 2][0][:, (k_ % 2) * 512 + r * 128:(k_ % 2) * 512 + (r + 1) * 128],
                               lambda k_: w2[:, k_, half * 512:(half + 1) * 512], 4,
                               wk2 + [hids[a_][1] + (c_,) for a_ in range(2) for c_ in range(2)])
                  sc.op("dve", [("ps", b), ("ps", b + 1), ("x1", i)], [("x1", i)], lambda e: e.tensor_tensor(
                      x1[:, i, :], ps[:, b * 512:(b + 2) * 512], x1[:, i, :], ALU.add))
                  if g == 7:
                      stt_, stk = st_r.get()
                      sc.op("act", [("x1", i)], [stk, "junk"], lambda e: e.activation(
                          junk[:, :], x1[:, i, :], AF.Square, scale=1.0 / 32.0, accum_out=stt_[:, 0:1]))
                      sc.op("act", [stk, "epsc"], [stk + ("l",)], lambda e: e.activation(
                          stt_[:, 2:3], stt_[:, 0:1], AF.Ln, bias=epsc[:, 0:1]))
                      sc.op("act", [stk + ("l",)], [stk + ("r",)], lambda e: e.activation(
                          stt_[:, 1:2], stt_[:, 2:3], AF.Exp, scale=-0.5))
                      ob, obk = xt_r.get()
                      sc.op("dve", [("x1", i), stk + ("r",), "gain"], [obk], lambda e: e.scalar_tensor_tensor(
                          ob[:, :], x1[:, i, :], stt_[:, 1:2], gain[:, :], ALU.mult, ALU.mult))
                      sc.dma("sp", out_d[s, i * 128:(i + 1) * 128, :], ob[:, :], [obk], [], is_out=True)

          emit_mlp1(0)
          for k in range(len(gitems)):
              if k + 1 < len(gitems):
                  emit_mlp1(k + 1)
              emit_mlp2(k)
          TP.pool = pool_a
          if debug:
              break

    except _Stop:
        pass
    sc.finish()
    return nc


def _tile_rows(w, ncols_group):
    K, N = w.shape
    kc = K // 128
    g = N // ncols_group
    return np.ascontiguousarray(
        w.reshape(kc, 128, g, ncols_group).transpose(2, 1, 0, 3).reshape(g, 128, kc * ncols_group))


def _host_layout(inp):
    f = np.float32
    w_in = np.asarray(inp["w_in"][0], f)
    ux = w_in[:, 0:1024].reshape(1024, 8, 128)
    ug = w_in[:, 1024:2048].reshape(1024, 8, 128)
    wB = _tile_rows(np.concatenate([ux, ug], axis=2).reshape(1024, 8 * 256), 256)
    q = w_in[:, 2048:3072].reshape(1024, 8, 128)
    k = w_in[:, 3072:4096].reshape(1024, 8, 128)
    v = w_in[:, 4096:5120].reshape(1024, 8, 128)
    wD = _tile_rows(np.concatenate([q, k, v], axis=2).reshape(1024, 8 * 384), 384)
    gr = w_in[:, 5120:6144].reshape(1024, 8, 128)
    ga = w_in[:, 6144:7168].reshape(1024, 8, 128)
    br = np.asarray(inp["w_br_rnn"][0], f).reshape(1024, 8, 128)
    ba = np.asarray(inp["w_br_attn"][0], f).reshape(1024, 8, 128)
    wE = _tile_rows(np.concatenate([gr, ga, br, ba], axis=2).reshape(1024, 8 * 512), 512)
    wF = _tile_rows(np.asarray(inp["w_out"][0], f), 512)
    wG1 = _tile_rows(np.asarray(inp["w_mlp1"][0], f), 512)
    w2 = np.asarray(inp["w_mlp2"][0], f)
    wG2 = np.ascontiguousarray(w2.reshape(8, 4, 128, 1024).transpose(0, 2, 1, 3).reshape(8, 128, 4096))
    gains = np.stack([np.broadcast_to(np.asarray(inp[n], f).reshape(1, D), (128, D))
                      for n in ("norm1_g", "norm2_g", "normf_g")]).copy()

    def fm(vv):
        return np.asarray(vv, f).reshape(8, 128).T
    chp = np.zeros((128, 8, 8), f)
    cw = np.asarray(inp["conv_w"][0], f)
    for j in range(4):
        chp[:, :, j] = fm(cw[j])
    chp[:, :, 4] = fm(inp["conv_b"][0])
    chp[:, :, 5] = fm(np.asarray(inp["rg_a_b"][0]).reshape(-1))
    chp[:, :, 6] = fm(np.asarray(inp["rg_x_b"][0]).reshape(-1))
    chp[:, :, 7] = fm(inp["lru_lambda"][0])
    wbd = np.zeros((128, 2, 8, 128), f)
    for a, nm in enumerate(("rg_a_w", "rg_x_w")):
        ww = np.asarray(inp[nm][0], f)
        for c in range(8):
            wbd[0:64, a, c, 0:64] = ww[2 * c]
            wbd[64:128, a, c, 64:128] = ww[2 * c + 1]
    half = 32
    inv_freq = 10000.0 ** (-np.arange(half, dtype=np.float64) * 2.0 / 64.0)
    ang = np.arange(S, dtype=np.float64)[:, None] * inv_freq[None, :]
    rope = np.zeros((128, 2, 16, 32), f)
    rope[:, 0] = np.cos(ang).astype(f).reshape(16, 128, 32).transpose(1, 0, 2)
    rope[:, 1] = np.sin(ang).astype(f).reshape(16, 128, 32).transpose(1, 0, 2)
    lamv = np.stack([np.broadcast_to(np.asarray(inp[n][0], f).reshape(1, 64), (128, 64))
                     for n in ("lambda_q1", "lambda_k1", "lambda_q2", "lambda_k2")], axis=1).copy()
    subg = np.broadcast_to(np.asarray(inp["subln_g"][0], f).reshape(1, 128), (128, 128)).copy()
    return {
        "wB": wB, "wD": wD, "wE": wE, "wF": wF, "wG1": wG1, "wG2": wG2,
        "gains": gains, "chp": chp.reshape(128, 64), "wbd": wbd.reshape(128, 2048),
        "rope": rope.reshape(128, 1024), "lamv": lamv.reshape(128, 256), "subg": subg,
        "ident": np.eye(128, dtype=f),
    }


_NC_CACHE = {}


def kernel(**inputs):
    x = np.ascontiguousarray(np.asarray(inputs["x"], np.float32))
    shared = _host_layout(inputs)
    if "nc" not in _NC_CACHE:
        _NC_CACHE["nc"] = build(False)
    nc = _NC_CACHE["nc"]
    in_maps = []
    for c in range(NCORES):
        m = dict(shared)
        m["x"] = x[c * NSEQ:(c + 1) * NSEQ]
        in_maps.append(m)
    res = run_bass_kernel_spmd(nc, in_maps, core_ids=list(range(NCORES)))
    return np.concatenate([np.asarray(r["out"], np.float32) for r in res.results], axis=0)
```

```python
import math
import os
from contextlib import ExitStack

import numpy as np
import concourse.bass as bass
import concourse.mybir as mybir
from concourse.bass_utils import run_bass_kernel_spmd

F32 = mybir.dt.float32
BF16 = mybir.dt.bfloat16
AF = mybir.ActivationFunctionType
ALU = mybir.AluOpType
AX = mybir.AxisListType

D = 1024
S = 2048
NT = 16
NSUP = 4
NSEQ = 2
NCORES = 8
NDS = 16
LAM_INIT = 0.8 - 0.6 * math.exp(0.0)
NORM_EPS = 1e-6
NSTREAM = 3
LV = int(os.environ.get('KDBG_LV', '9'))
SUBLN_EPS = 1e-5
GELU_C = 0.7978845608028654


class Sched:
    def __init__(self, nc, es):
        self.nc = nc
        self.E = {"pe": nc.tensor, "act": nc.scalar, "dve": nc.vector, "pool": nc.gpsimd, "sp": nc.sync}
        self.sems = {e: es.enter_context(nc.semaphore("c_" + e)) for e in ("pe", "act", "dve", "pool")}
        self.cnt = {e: 0 for e in self.sems}
        self.sid = {e: "c_" + e for e in self.sems}
        self.epoch = 0
        self.seen = {e: {} for e in self.E}
        self.lastw = {}
        self.readers = {}
        self.dsems = [es.enter_context(nc.semaphore("d%d" % i)) for i in range(NDS)]
        self.dval = [0] * NDS
        self.di = 0
        self.nrd = 0
        self.out_toks = []
        self.own_last = {}

    def _deps(self, reads, writes):
        deps = []
        for k in reads:
            t = self.lastw.get(k)
            if t is not None:
                deps.append((t, 0))
            if k[0] == "ps":
                for t in self.readers.get(k, {}).values():
                    deps.append((t, 1))
        for k in writes:
            t = self.lastw.get(k)
            if t is not None:
                deps.append((t, 1))
            for t in self.readers.get(k, {}).values():
                deps.append((t, 1))
        return deps

    def _wait(self, eng, deps, is_dma):
        e = self.E[eng]
        seen = self.seen[eng]
        for (t, kind) in deps:
            sem, val, teng, sid = t
            if (not is_dma) and teng == eng:
                if eng == "pe" or kind != 0:
                    continue
            if seen.get(sid, 0) >= val:
                continue
            e.wait_ge(sem, val)
            seen[sid] = val

    def _record(self, tok, reads, writes):
        for k in writes:
            self.lastw[k] = tok
            self.readers[k] = {}
        for k in reads:
            if k in writes:
                continue
            d = self.readers.setdefault(k, {})
            if tok[2] == "dma":
                self.nrd += 1
                d[("dma", self.nrd)] = tok
            else:
                d[tok[2]] = tok

    def new_epoch(self, es):
        self.epoch += 1
        for e in ("pe", "act", "dve"):
            self.sems[e] = es.enter_context(self.nc.semaphore("c%d_%s" % (self.epoch, e)))
            self.cnt[e] = 0
            self.sid[e] = "c%d_%s" % (self.epoch, e)

    def op(self, eng, reads, writes, fn):
        self._wait(eng, self._deps(reads, writes), False)
        ins = fn(self.E[eng])
        self.cnt[eng] += 1
        ins.then_inc(self.sems[eng], 1)
        tok = (self.sems[eng], self.cnt[eng], eng, self.sid[eng])
        self._record(tok, reads, writes)
        return tok

    def dma_own(self, q, out, in_, reads, writes, own):
        self._wait(q, self._deps(reads, writes), True)
        sem, st = own
        if st["n"] > 0:
            self.E[q].wait_ge(sem, 16)
            self.E[q].sem_clear(sem)
        st["n"] += 1
        ins = self.E[q].dma_start(out=out, in_=in_)
        ins.then_inc(sem, 16)
        tok = (sem, 16, "dma", "own%s_%d" % (st["name"], st["n"]))
        self._record(tok, reads, writes)
        self.own_last[st["name"]] = (sem, tok[3])
        return tok

    def dma(self, q, out, in_, reads, writes, is_out=False):
        self._wait(q, self._deps(reads, writes), True)
        i = self.di
        self.di = (self.di + 1) % NDS
        sem = self.dsems[i]
        sid = "d%d" % i
        if self.dval[i] > 0 and self.seen[q].get(sid, 0) < self.dval[i]:
            self.E[q].wait_ge(sem, self.dval[i])
            self.seen[q][sid] = self.dval[i]
        ins = self.E[q].dma_start(out=out, in_=in_)
        self.dval[i] += 16
        ins.then_inc(sem, 16)
        tok = (sem, self.dval[i], "dma", sid)
        self._record(tok, reads, writes)
        if is_out:
            self.out_toks.append(tok)
        return tok

    def finish(self):
        e = self.E["sp"]
        for name, (sem, sid) in self.own_last.items():
            if self.seen["sp"].get(sid, 0) < 16:
                e.wait_ge(sem, 16)
        for i in range(NDS):
            sid = "d%d" % i
            if self.dval[i] > 0 and self.seen["sp"].get(sid, 0) < self.dval[i]:
                e.wait_ge(self.dsems[i], self.dval[i])
                self.seen["sp"][sid] = self.dval[i]
        for (sem, val, _, sid) in self.out_toks:
            if self.seen["sp"].get(sid, 0) < val:
                e.wait_ge(sem, val)
                self.seen["sp"][sid] = val


class Rot:
    def __init__(self, nc, name, n, shape, dtype):
        self.bufs = [nc.alloc_sbuf_tensor("sb_%s%d" % (name, i), shape, dtype) for i in range(n)]
        self.name = name
        self.i = 0

    def get(self):
        j = self.i % len(self.bufs)
        self.i += 1
        return self.bufs[j], (self.name, j)


class _Stop(Exception):
    pass


def build(debug=False, upto=None, skipB=False):
    nc = bass.Bass("TRN2", target_bir_lowering=False)
    es = ExitStack()

    def dram(name, shape, dt=F32, kind="ExternalInput"):
        return nc.dram_tensor(name, shape, dt, kind=kind).ap()

    x_d = dram("x", [NSEQ, S, D])
    out_d = dram("out", [NSEQ, S, D], kind="ExternalOutput")
    wB_d = dram("wB", [8, 128, 8 * 256])
    wD_d = dram("wD", [8, 128, 8 * 384])
    wE_d = dram("wE", [8, 128, 8 * 512])
    wF_d = dram("wF", [2, 128, 8 * 512])
    wG1_d = dram("wG1", [8, 128, 8 * 512])
    wG2_d = dram("wG2", [8, 128, 4 * 1024])
    gains_d = dram("gains", [3, 128, D])
    chp_d = dram("chp", [128, 8 * 8])
    wbd_d = dram("wbd", [128, 2 * 8 * 128])
    rope_d = dram("rope", [128, 2 * 16 * 32])
    lamv_d = dram("lamv", [128, 4 * 64])
    subg_d = dram("subg", [128, 128])
    ident_d = dram("ident", [128, 128])
    dbg = {}
    if debug:
        for nm in ("hT", "yrT", "yaT", "mT"):
            dbg[nm] = dram("dbg_" + nm, [128, 8 * S], BF16, kind="ExternalOutput")
        dbg["x1"] = dram("dbg_x1", [128, 16 * D], F32, kind="ExternalOutput")

    sc = Sched(nc, es)
    def A(name, shape, dt):
        return nc.alloc_sbuf_tensor("sb_" + name, shape, dt)

    hT = A("hT", [128, 8, S], BF16)
    BC = A("BC", [128, 16 * D], F32)
    BCb = BC.bitcast(BF16).reshape([128, 16, S])
    x1 = BC.reshape([128, 16, D])
    ATT = A("ATT", [128, 16448], BF16)
    mT = ATT[:, 0:16384].rearrange("p (k t) -> p k t", k=8)
    qkT = [ATT[:, i * 4096:(i + 1) * 4096].rearrange("p (a t) -> p a t", a=2) for i in range(2)]
    V1 = [ATT[:, 8192 + i * 2080: 8192 + (i + 1) * 2080].rearrange("p (t e) -> p t e", t=NT) for i in range(2)]
    yat = [ATT[:, 12352 + i * 2048: 12352 + (i + 1) * 2048].rearrange("p (t e) -> p t e", t=NT) for i in range(2)]
    ATT_KEYS = ([("qkT", a, T) for a in range(2) for T in range(4)] + [("V1", a) for a in range(2)]
                + [("yat", a, T) for a in range(2) for T in range(4)])
    M_KEYS = [("m", c, T) for c in range(8) for T in range(4)]
    wslot = [A("wslot%d" % i, [128, 4096], BF16) for i in range(3)]
    gain = A("gain", [128, D], F32)
    chp = A("chp", [128, 8, 8], F32)
    chq = A("chq", [128, 8, 4], F32)
    wbd = A("wbd", [128, 2, 8, 128], BF16)
    rope = A("rope", [128, 2, 16, 32], F32)
    lamv = A("lamv", [128, 4, 64], F32)
    subg = A("subg", [128, 128], F32)
    ident = A("ident", [128, 128], BF16)
    neglam = A("neglam", [128, 1], F32)
    junk = A("junk", [128, D], BF16)
    setup_t = A("setup_t", [128, 64], F32)
    setup_u = A("setup_u", [128, 64], F32)
    setup_s = A("setup_s", [128, 8], F32)
    setup_v = A("setup_v", [128, 4], F32)
    halo = A("halo", [128, 4], F32)
    hlast = A("hlast", [128, 1], F32)
    epsc = A("epsc", [128, 2], F32)

    ps = nc.alloc_psum_tensor("ps", [128, 4096], F32)
    psb = ps.bitcast(BF16)

    class PS:
        nxt = 0

    def ps1():
        b = PS.nxt % 8
        PS.nxt += 1
        return b

    def ps2():
        if PS.nxt % 2:
            PS.nxt += 1
        b = PS.nxt % 8
        PS.nxt += 2
        return b

    class PSL:
        nxt = 0

    def ps1lo():
        b = PSL.nxt % 4
        PSL.nxt += 1
        return b

    class PSP3:
        nxt = 0

    def ps1p3():
        b = PSP3.nxt % 3
        PSP3.nxt += 1
        return b

    def bank(b, n=512, off=0):
        return ps[:, b * 512 + off: b * 512 + off + n]

    xt_r = Rot(nc, "xt", 2, [128, D], F32)
    st_r = Rot(nc, "st", 12, [128, 4], F32)
    fino_r = Rot(nc, "fino", 6, [128, 256], F32)
    NBLK = 11
    blk = [A("blk%d" % i, [128, 516], F32) for i in range(NBLK)]
    blkb = [b_.bitcast(BF16) for b_ in blk]

    ATTf = ATT.bitcast(F32)
    NAB = 15
    pool_a = [(blk[j], blkb[j], ("blk", j)) for j in range(NBLK)]
    pool_b = pool_a + [(ATTf[:, j * 516:(j + 1) * 516], ATT[:, j * 1032:(j + 1) * 1032], ("ab", j))
                       for j in range(NAB)]
    M_KEYS.extend([("ab", j) for j in range(NAB)])

    class TP:
        i = 0
        pool = pool_a

    def tmp(n, dt=F32):
        j = TP.i % len(TP.pool)
        TP.i += 1
        f, bview, key = TP.pool[j]
        if dt == F32:
            return f[:, 0:n], key
        return bview[:, 0:n], key

    halo2 = [A("halo2_%d" % i, [128, 4], F32) for i in range(NSTREAM)]
    hlast2 = [A("hlast2_%d" % i, [128, 1], F32) for i in range(NSTREAM)]

    nc.allow_low_precision("bf16 matmul operands with fp32 accumulation (per problem tolerance)")

    sc.dma("sp", chp[:, :, :], chp_d.rearrange("p (c k) -> p c k", c=8), [], ["chp"])
    sc.dma("sp", rope[:, :, :, :], rope_d.rearrange("p (a t f) -> p a t f", a=2, t=16), [], ["rope"])
    sc.dma("sp", lamv[:, :, :], lamv_d.rearrange("p (a f) -> p a f", a=4), [], ["lamv"])
    sc.dma("sp", subg[:, :], subg_d, [], ["subg"])
    def own_sem(name):
        return (es.enter_context(nc.semaphore("o_" + name)), {"n": 0, "name": name})
    sc.dma_own("pool", wbd[:, :, :, :], wbd_d.rearrange("p (a c f) -> p a c f", a=2, c=8), [], ["wbd"],
               own_sem("wbd"))
    sc.dma_own("pool", ident[:, :], ident_d, [], ["ident"], own_sem("ident"))

    wlist = []
    for s in range(NSEQ):
        for cp in range(0 if skipB else 4):
            wlist.append((wB_d[2 * cp:2 * cp + 2].rearrange("c p n -> p c n"), 4096))
        for h in range(8):
            wlist.append((wD_d[h], 3072))
        for c in range(8):
            wlist.append((wE_d[c], 4096))
        for hf in range(2):
            wlist.append((wF_d[hf], 4096))
        for g in range(8):
            wlist.append((wG1_d[g], 4096))
            wlist.append((wG2_d[g], 4096))
        if debug:
            break
    WS = {"n": 0, "loaded": 0}

    def wnext(second=False):
        n = WS["n"]
        WS["n"] += 1
        while WS["loaded"] < min(n + (2 if second else 3), len(wlist)):
            m = WS["loaded"]
            src_, ncol = wlist[m]
            dst_ = wslot[m % 3][:, 0:ncol]
            if len(src_.shape) == 3:
                dst_ = dst_.rearrange("p (c n) -> p c n", c=src_.shape[1])
            sc.dma_own("pool", dst_, src_, [], [("w", m % 3)], own_sem("w%d" % m))
            WS["loaded"] += 1
        return wslot[n % 3], [("w", n % 3)]

    sc.op("dve", [], ["epsc"], lambda e: e.memset(epsc[:, 0:1], float(NORM_EPS)))
    sc.op("dve", [], ["epsc"], lambda e: e.memset(epsc[:, 1:2], float(SUBLN_EPS)))
    sc.op("dve", ["subg"], ["subg"], lambda e: e.tensor_scalar(
        subg[:, :], subg[:, :], (1.0 - LAM_INIT), None, ALU.mult))
    sc.op("act", ["chp"], ["setup_s"], lambda e: e.activation(
        setup_s[:, :], chp[:, :, 7], AF.Exp, scale=-1.0))
    sc.op("act", ["setup_s"], ["setup_s"], lambda e: e.activation(
        setup_s[:, :], setup_s[:, :], AF.Ln, bias=1.0))
    sc.op("dve", ["setup_s"], ["chq"], lambda e: e.tensor_scalar(
        chq[:, :, 2], setup_s[:, :], -8.0, None, ALU.mult))
    sc.op("dve", ["setup_s"], ["chq"], lambda e: e.tensor_scalar(
        chq[:, :, 3], setup_s[:, :], -4.0, None, ALU.mult))
    sc.op("dve", ["chp"], ["chq"], lambda e: e.tensor_scalar(
        chq[:, :, 0:2], chp[:, :, 5:7], 0.5, None, ALU.mult))
    sc.op("dve", ["lamv"], ["setup_t"], lambda e: e.tensor_tensor(
        setup_t[:, :], lamv[:, 0, :], lamv[:, 1, :], ALU.mult))
    sc.op("dve", ["setup_t"], ["sv0"], lambda e: e.tensor_reduce(
        setup_v[:, 0:1], setup_t[:, :], AX.X, ALU.add))
    sc.op("dve", ["lamv"], ["setup_u"], lambda e: e.tensor_tensor(
        setup_u[:, :], lamv[:, 2, :], lamv[:, 3, :], ALU.mult))
    sc.op("dve", ["setup_u"], ["sv1"], lambda e: e.tensor_reduce(
        setup_v[:, 1:2], setup_u[:, :], AX.X, ALU.add))
    sc.op("act", ["sv0"], ["sv2"], lambda e: e.activation(setup_v[:, 2:3], setup_v[:, 0:1], AF.Exp))
    sc.op("act", ["sv1"], ["sv3"], lambda e: e.activation(setup_v[:, 3:4], setup_v[:, 1:2], AF.Exp))
    sc.op("dve", ["sv2", "sv3"], ["neglam"], lambda e: e.scalar_tensor_tensor(
        neglam[:, :], setup_v[:, 3:4], -LAM_INIT, setup_v[:, 2:3], ALU.add, ALU.subtract))

    def load_gain(idx):
        sc.dma("sp", gain[:, :], gains_d[idx], [], ["gain"])

    def norm_to_T(src_ap, src_key, i):
        stt_, stk = st_r.get()
        sc.op("act", [src_key], [stk, "junk"], lambda e: e.activation(
            junk[:, :], src_ap, AF.Square, scale=1.0 / 32.0, accum_out=stt_[:, 0:1]))
        sc.op("act", [stk, "epsc"], [stk + ("l",)], lambda e: e.activation(
            stt_[:, 2:3], stt_[:, 0:1], AF.Ln, bias=epsc[:, 0:1]))
        sc.op("act", [stk + ("l",)], [stk + ("r",)], lambda e: e.activation(
            stt_[:, 1:2], stt_[:, 2:3], AF.Exp, scale=-0.5))
        hb, hbk = tmp(1024, BF16)
        sc.op("dve", [src_key, stk + ("r",), "gain"], [hbk], lambda e: e.scalar_tensor_tensor(
            hb, src_ap, stt_[:, 1:2], gain[:, :], ALU.mult, ALU.mult))
        def part_b():
            b = ps1()

            def tr(e):
                ins = None
                for kc in range(8):
                    ins = e.transpose(psb[:, b * 1024 + kc * 128: b * 1024 + (kc + 1) * 128],
                                      hb[:, kc * 128:(kc + 1) * 128], ident[:, :])
                return ins
            sc.op("pe", [hbk, "ident"], [("ps", b)], tr)
            sc.op("act", [("ps", b)], [("hT", i)], lambda e: e.copy(
                hT[:, :, i * 128:(i + 1) * 128],
                psb[:, b * 1024:(b + 1) * 1024].rearrange("p (k t) -> p k t", k=8)))
        return part_b

    def mm_group(b, off, n, lhs_fn, rhs_fn, nk, reads):
        def f(e):
            ins = None
            for k in range(nk):
                ins = e.matmul(bank(b, n, off), lhs_fn(k), rhs_fn(k), start=(k == 0), stop=(k == nk - 1))
            return ins
        return sc.op("pe", reads, [("ps", b)], f)

    def hT_keys(T):
        return [("hT", 4 * T + r) for r in range(4)]

    def dump(name, ap_sb, keys):
        if debug:
            sc.dma("sp", dbg[name], ap_sb, keys, [], is_out=True)

    def cut(name):
        if upto == name:
            raise _Stop()

    try:
      for s in range(NSEQ):
          if s > 0:
              sc.new_epoch(es)
          load_gain(0)
          pendA = None
          for i in range(NT):
              xt, xk = xt_r.get()
              sc.dma("sp", xt[:, :], x_d[s, i * 128:(i + 1) * 128, :], [], [xk])
              pb = norm_to_T(xt[:, :], xk, i)
              if pendA is not None:
                  pendA()
              pendA = pb
          pendA()
          if debug and s == 0:
              dump("hT", hT[:, :, :].rearrange("p k t -> p (k t)"), [("hT", i) for i in range(NT)])
          cut("A")

          def b_stream(c, w, wk, sl):
              halo_c = halo2[sl]
              hlast_c = hlast2[sl]
              hk = ("halo", sl)
              hlk = ("hlast", sl)
              for T in range(NSUP):
                  cols = slice(T * 512, (T + 1) * 512)
                  b_ux = 2 * sl + 1
                  mm_group(b_ux, 0, 512, lambda k: w[:, k, 0:128], lambda k: hT[:, k, cols], 8, wk + hT_keys(T))
                  b_ug = 2 * sl
                  mm_group(b_ug, 0, 512, lambda k: w[:, k, 128:256], lambda k: hT[:, k, cols], 8, wk + hT_keys(T))
                  yield
                  ux, uxk = tmp(515)
                  if T == 0:
                      sc.op("dve", [], [uxk], lambda e: e.memset(ux[:, 0:3], 0.0))
                  else:
                      sc.op("dve", [hk], [uxk], lambda e: e.tensor_copy(ux[:, 0:3], halo_c[:, 0:3]))
                  sc.op("act", [("ps", b_ux)], [uxk + ("m",)], lambda e: e.copy(ux[:, 3:515], bank(b_ux)))
                  yield
                  uxr = [uxk, uxk + ("m",)]
                  if T < NSUP - 1:
                      sc.op("dve", uxr, [hk], lambda e: e.tensor_copy(halo_c[:, 0:3], ux[:, 512:515]))
                  xr, xrk = tmp(512)
                  sc.op("dve", uxr + ["chp"], [xrk], lambda e: e.tensor_scalar(
                      xr, ux[:, 0:512], chp[:, c, 0:1], chp[:, c, 4:5], ALU.mult, ALU.add))
                  for j in range(1, 4):
                      sc.op("dve", uxr + [xrk, "chp"], [xrk], lambda e, j=j: e.scalar_tensor_tensor(
                          xr, ux[:, j:j + 512], chp[:, c, j:j + 1], xr, ALU.mult, ALU.add))
                  yield
                  xb, xbk = tmp(512, BF16)
                  sc.op("act", [xrk], [xbk], lambda e: e.copy(xb, xr))
                  b_ga = 2 * sl + 1
                  mm_group(b_ga, 0, 512, lambda k: wbd[:, 0, c, :], lambda k: xb, 1, [xbk, "wbd"])
                  yield
                  tr_, trk = tmp(512)
                  sc.op("act", [("ps", b_ga), "chq"], [trk], lambda e: e.activation(
                      tr_, bank(b_ga), AF.Tanh, bias=chq[:, c, 0:1], scale=0.5))
                  b_gx = 2 * sl + 1
                  mm_group(b_gx, 0, 512, lambda k: wbd[:, 1, c, :], lambda k: xb, 1, [xbk, "wbd"])
                  yield
                  ti_, tik = tmp(512)
                  sc.op("act", [("ps", b_gx), "chq"], [tik], lambda e: e.activation(
                      ti_, bank(b_gx), AF.Tanh, bias=chq[:, c, 1:2], scale=0.5))
                  yield
                  a_, ak = tmp(512)
                  sc.op("act", [trk, "chq"], [ak], lambda e: e.activation(
                      a_, tr_, AF.Exp, bias=chq[:, c, 3:4], scale=chq[:, c, 3:4]))
                  a2_, a2k = tmp(512)
                  sc.op("act", [trk, "chq"], [a2k], lambda e: e.activation(
                      a2_, tr_, AF.Exp, bias=chq[:, c, 2:3], scale=chq[:, c, 2:3]))
                  sc.op("act", [trk, "chq"], [trk], lambda e: e.activation(
                      tr_, tr_, AF.Tanh, bias=chq[:, c, 3:4], scale=chq[:, c, 3:4]))
                  yield
                  sc.op("dve", [a2k, trk], [a2k], lambda e: e.scalar_tensor_tensor(
                      a2_, a2_, 1.0, tr_, ALU.add, ALU.mult))
                  sc.op("dve", [tik, xrk], [tik], lambda e: e.scalar_tensor_tensor(
                      ti_, ti_, 1.0, xr, ALU.add, ALU.mult))
                  yield
                  sc.op("act", [a2k], [a2k], lambda e: e.activation(a2_, a2_, AF.Ln, scale=-1.0))
                  sc.op("act", [a2k], [a2k], lambda e: e.activation(a2_, a2_, AF.Exp, scale=0.5))
                  sq, sqk = tmp(512)
                  sc.op("act", [("ps", b_ug)], [sqk], lambda e: e.activation(sq, bank(b_ug), AF.Square))
                  yield
                  sc.op("dve", [tik, a2k], [tik], lambda e: e.scalar_tensor_tensor(
                      ti_, ti_, 0.5, a2_, ALU.mult, ALU.mult))
                  hs, hsk = tmp(512)
                  if T == 0:
                      sc.op("dve", [ak, tik], [hsk], lambda e: e.tensor_tensor_scan(
                          hs, a_, ti_, 0.0, ALU.mult, ALU.add))
                  else:
                      sc.op("dve", [ak, tik, hlk], [hsk], lambda e: e.tensor_tensor_scan(
                          hs, a_, ti_, hlast_c[:, 0:1], ALU.mult, ALU.add))
                  if T < NSUP - 1:
                      sc.op("dve", [hsk], [hlk], lambda e: e.tensor_copy(hlast_c[:, 0:1], hs[:, 511:512]))
                  yield
                  sc.op("dve", [sqk], [sqk], lambda e: e.tensor_scalar(sq, sq, 0.044715, 1.0, ALU.mult, ALU.add))
                  sc.op("dve", [sqk, ("ps", b_ug)], [sqk], lambda e: e.tensor_tensor(sq, sq, bank(b_ug), ALU.mult))
                  yield
                  sc.op("act", [sqk], [sqk], lambda e: e.activation(sq, sq, AF.Tanh, scale=GELU_C))
                  yield
                  sc.op("dve", [sqk, ("ps", b_ug)], [sqk], lambda e: e.scalar_tensor_tensor(
                      sq, sq, 1.0, bank(b_ug), ALU.add, ALU.mult))
                  sc.op("dve", [sqk, hsk], [("yr", c, T), ("x1", c)], lambda e: e.scalar_tensor_tensor(
                      BCb[:, c, cols], sq, 0.5, hs, ALU.mult, ALU.mult))
                  yield

          TP.pool = pool_b
          active = []
          nxt = {"c": 0, "w": None}

          def start_next(sl):
              c = nxt["c"]
              if c >= (0 if skipB else 8):
                  return None
              nxt["c"] += 1
              if c % 2 == 0:
                  nxt["w"] = wnext(second=True)
              wslB, wk = nxt["w"]
              w = wslB[:, (c % 2) * 2048:(c % 2 + 1) * 2048].rearrange("p (k n) -> p k n", k=8)
              return (b_stream(c, w, wk, sl), sl)
          for sl in range(NSTREAM):
              g_ = start_next(sl)
              if g_ is not None:
                  active.append(g_)
          while active:
              for item in list(active):
                  g_, sl = item
                  try:
                      next(g_)
                  except StopIteration:
                      active.remove(item)
                      n_ = start_next(sl)
                      if n_ is not None:
                          active.append(n_)
          TP.pool = pool_a
          if debug and s == 0 and not skipB:
              dump("yrT", BCb[:, 0:8, :].rearrange("p k t -> p (k t)"),
                   [("yr", c, T) for c in range(8) for T in range(4)])
          cut("B")

          def proj_head(h):
              slot = h % 2
              wsl, wk = wnext()
              w = wsl[:, 0:3072].rearrange("p (k n) -> p k n", k=8)
              sc.op("dve", [], [("V1", slot)] + M_KEYS, lambda e: e.memset(V1[slot][:, :, 128:130], 1.0))
              bt = None
              pend = []
              for i in range(NT):
                  r = i % 4
                  b = ps1p3()
                  mm_group(b, 0, 384, lambda k: hT[:, k, i * 128:(i + 1) * 128], lambda k: w[:, k, :], 8,
                           wk + [("hT", i)])
                  if LV < 2:
                      continue
                  X = bank(b, 256).rearrange("p (a h f) -> p a h f", a=4, h=2)
                  cosb = rope[:, 0, i, :].unsqueeze(1).unsqueeze(1).broadcast_to([128, 4, 2, 32])
                  sinb = rope[:, 1, i, :].unsqueeze(1).broadcast_to([128, 4, 32])
                  t1, t1k = tmp(256)
                  t1v = t1.rearrange("p (a h f) -> p a h f", a=4, h=2)
                  sc.op("dve", [("ps", b), "rope"], [t1k], lambda e: e.tensor_tensor(t1v, X, cosb, ALU.mult))
                  tA, tAk = tmp(256)
                  tAv = tA.rearrange("p (a h f) -> p a h f", a=4, h=2)
                  sc.op("dve", [("ps", b), "rope"], [tAk], lambda e: e.tensor_tensor(
                      tAv[:, :, 0, :], X[:, :, 1, :], sinb, ALU.mult))
                  sc.op("dve", [("ps", b), "rope"], [tAk + ("b",)], lambda e: e.tensor_tensor(
                      tAv[:, :, 1, :], X[:, :, 0, :], sinb, ALU.mult))
                  qkb, qkbk = tmp(256, BF16)
                  qkv = qkb.rearrange("p (a h f) -> p a h f", a=4, h=2)
                  sc.op("dve", [t1k, tAk], [qkbk], lambda e: e.tensor_tensor(
                      qkv[:, :, 0, :], t1v[:, :, 0, :], tAv[:, :, 0, :], ALU.subtract))
                  sc.op("dve", [t1k, tAk + ("b",)], [qkbk + ("b",)], lambda e: e.tensor_tensor(
                      qkv[:, :, 1, :], t1v[:, :, 1, :], tAv[:, :, 1, :], ALU.add))
                  if LV < 3:
                      continue
                  sc.op("act", [("ps", b)], [("V1", slot)] + (M_KEYS if i == 0 else []), lambda e: e.copy(
                      V1[slot][:, i, 0:128], bank(b, 128, 256)))
                  if LV < 4:
                      continue
                  bt = 3

                  def tail(r=r, bt=bt, qkb=qkb, qkbk=qkbk, i=i):
                      def trq(e):
                          e.transpose(psb[:, bt * 1024 + r * 128: bt * 1024 + (r + 1) * 128], qkb[:, 0:128],
                                      ident[:, :])
                          return e.transpose(psb[:, bt * 1024 + 512 + r * 128: bt * 1024 + 512 + (r + 1) * 128],
                                             qkb[:, 128:256], ident[:, :])
                      sc.op("pe", [qkbk, qkbk + ("b",), "ident"], [("ps", bt)], trq)
                      if r == 3:
                          T = i // 4
                          sc.op("act", [("ps", bt)], [("qkT", slot, T)] + M_KEYS, lambda e: e.copy(
                              qkT[slot][:, :, T * 512:(T + 1) * 512],
                              psb[:, bt * 1024:(bt + 1) * 1024].rearrange("p (a t) -> p a t", a=2)))
                  if len(pend) >= 3:
                      pend.pop(0)()
                  pend.append(tail)
              while pend:
                  pend.pop(0)()

          def attn_head(h):
              slot = h % 2
              qT = qkT[slot]
              units = [(T, m, j) for T in range(NSUP) for m in range(2) for j in range(4 * T + 4)]
              LOOK = 3
              st_info = {}

              def emit_S(n):
                  T, m, j = units[n]
                  pr = slice(m * 64, (m + 1) * 64)
                  r0 = max(0, j - 4 * T)
                  N = (4 - r0) * 128
                  q0 = T * 512 + r0 * 128
                  bs = ps1lo()
                  sc.op("pe", [("qkT", slot, T), ("qkT", slot, j // 4)], [("ps", bs)],
                        lambda e: e.matmul(bank(bs, N), qT[pr, 1, j * 128:(j + 1) * 128], qT[pr, 0, q0:q0 + N],
                                           start=True, stop=True))
                  et, etk = tmp(512, BF16)
                  sc.op("act", [("ps", bs)], [etk], lambda e: e.activation(
                      et[:, 0:N], bank(bs, N), AF.Exp, scale=0.125))
                  if j >= 4 * T:
                      sc.op("pool", [etk], [etk], lambda e: e.memset(et[64:128, 0:64], 0.0))
                  st_info[n] = (et, etk, r0)

              def emit_PV(n):
                  T, m, j = units[n]
                  et, etk, r0 = st_info.pop(n)
                  ob = 4 + 2 * m

                  def pv(e):
                      ins = None
                      for r in range(r0, 4):
                          bb = ob + r // 2
                          off = (r % 2) * 130
                          ins = e.matmul(bank(bb, 130, off), et[:, (r - r0) * 128:(r - r0 + 1) * 128],
                                         V1[slot][:, j, :], start=(j == 0 and r % 2 == 0),
                                         stop=(j == 4 * T + r and r % 2 == 1))
                      return ins
                  sc.op("pe", [etk, ("V1", slot)], [("ps", ob), ("ps", ob + 1)], pv)

              def finalize(T):
                  osb = []
                  for m in range(2):
                      pair = []
                      for half in range(2):
                          dst, dk = tmp(260)
                          bb = 4 + 2 * m + half
                          sc.op("dve", [("ps", bb)], [dk], lambda e: e.tensor_copy(dst, bank(bb, 260)))
                          pair.append((dst, dk))
                      osb.append(pair)
                  parts = []
                  for half in range(2):
                      (s0, s0k) = osb[0][half]
                      (s1, s1k) = osb[1][half]
                      O0 = s0.rearrange("p (a f) -> p a f", a=2)
                      O1 = s1.rearrange("p (a f) -> p a f", a=2)
                      stt_, stk = st_r.get()
                      sc.op("dve", [s0k], [stk], lambda e: e.reciprocal(stt_[:, 0:2], O0[:, :, 128]))
                      sc.op("dve", [s1k], [stk + ("b",)], lambda e: e.reciprocal(stt_[:, 2:4], O1[:, :, 128]))
                      sc.op("dve", [stk + ("b",), "neglam"], [stk + ("b",)], lambda e: e.tensor_scalar(
                          stt_[:, 2:4], stt_[:, 2:4], neglam[:, 0:1], None, ALU.mult))
                      o0t, o0k = fino_r.get()
                      o0 = o0t[:, :]
                      o0v = o0.rearrange("p (a f) -> p a f", a=2)
                      sc.op("dve", [s0k, stk], [o0k], lambda e: e.tensor_tensor(
                          o0v, O0[:, :, 0:128], stt_[:, 0:2].unsqueeze(2).broadcast_to([128, 2, 128]), ALU.mult))
                      o1, o1k = tmp(256)
                      o1v = o1.rearrange("p (a f) -> p a f", a=2)
                      sc.op("dve", [s1k, stk + ("b",)], [o1k], lambda e: e.tensor_tensor(
                          o1v, O1[:, :, 0:128], stt_[:, 2:4].unsqueeze(2).broadcast_to([128, 2, 128]), ALU.mult))
                      sc.op("dve", [o0k, o1k], [o0k], lambda e: e.tensor_tensor(o0, o0, o1, ALU.add))
                      sc.op("dve", [o0k], [o1k], lambda e: e.tensor_tensor(o1, o0, o0, ALU.mult))
                      st2, st2k = st_r.get()
                      sc.op("dve", [o1k], [st2k], lambda e: e.tensor_reduce(st2[:, 0:2], o1v, AX.X, ALU.add))
                      parts.append((o0v, o0k, st2, st2k))

                  def part2():
                      for half, (o0v, o0k, st2, st2k) in enumerate(parts):
                          sc.op("act", [st2k, "epsc"], [st2k], lambda e: e.activation(
                              st2[:, 0:2], st2[:, 0:2], AF.Ln, bias=epsc[:, 1:2], scale=1.0 / 128.0))
                          sc.op("act", [st2k], [st2k], lambda e: e.activation(
                              st2[:, 0:2], st2[:, 0:2], AF.Exp, scale=-0.5))
                          sc.op("dve", [o0k, st2k], [o0k], lambda e: e.tensor_tensor(
                              o0v, o0v, st2[:, 0:2].unsqueeze(2).broadcast_to([128, 2, 128]), ALU.mult))
                          i0 = 4 * T + 2 * half
                          sc.op("dve", [o0k, "subg"], [("yat", slot, T)] + M_KEYS, lambda e: e.tensor_tensor(
                              yat[slot][:, i0:i0 + 2, :], o0v,
                              subg[:, :].unsqueeze(1).broadcast_to([128, 2, 128]), ALU.mult))

                  def part3():
                      bt = ps1lo()

                      def try_(e):
                          ins = None
                          for r in range(4):
                              ins = e.transpose(psb[:, bt * 1024 + r * 128: bt * 1024 + (r + 1) * 128],
                                                yat[slot][:, 4 * T + r, :], ident[:, :])
                          return ins
                      sc.op("pe", [("yat", slot, T), "ident"], [("ps", bt)], try_)
                      sc.op("act", [("ps", bt)], [("ya", h, T), ("x1", 8 + h)], lambda e: e.copy(
                          BCb[:, 8 + h, T * 512:(T + 1) * 512], psb[:, bt * 1024: bt * 1024 + 512]))
                  deferred.append([6, part2])
                  deferred.append([12, part3])

              nU = len(units)
              for n in range(min(LOOK, nU)):
                  emit_S(n)
              for n in range(nU):
                  if n + LOOK < nU:
                      emit_S(n + LOOK)
                  emit_PV(n)
                  tick_deferred()
                  T, m, j = units[n]
                  if m == 1 and j == 4 * T + 3:
                      finalize(T)
                      if upto == "E0":
                          flush_deferred()
                      cut("E0")

          deferred = []

          def tick_deferred():
              for d in list(deferred):
                  d[0] -= 1
                  if d[0] <= 0:
                      deferred.remove(d)
                      d[1]()

          def flush_deferred():
              while deferred:
                  d = deferred.pop(0)
                  d[1]()

          proj_head(0)
          cut("D")
          for h in range(8):
              if h + 1 < 8:
                  proj_head(h + 1)
              attn_head(h)
          flush_deferred()
          if debug and s == 0:
              dump("yaT", BCb[:, 8:16, :].rearrange("p k t -> p (k t)"),
                   [("ya", c, T) for c in range(8) for T in range(4)])
          cut("E")

          for c in range(8):
              wsl, wk = wnext()
              w = wsl[:, 0:4096].rearrange("p (k n) -> p k n", k=8)
              for T in range(NSUP):
                  cols = slice(T * 512, (T + 1) * 512)
                  b_gr = ps1()
                  mm_group(b_gr, 0, 512, lambda k: w[:, k, 0:128], lambda k: hT[:, k, cols], 8, wk + hT_keys(T))
                  b_ga = ps1()
                  mm_group(b_ga, 0, 512, lambda k: w[:, k, 128:256], lambda k: hT[:, k, cols], 8, wk + hT_keys(T))
                  b_br = ps1()
                  mm_group(b_br, 0, 512, lambda k: w[:, k, 256:384], lambda k: BCb[:, k, cols], 8,
                           wk + [("yr", k, T) for k in range(8)])
                  b_ba = ps1()
                  mm_group(b_ba, 0, 512, lambda k: w[:, k, 384:512], lambda k: BCb[:, 8 + k, cols], 8,
                           wk + [("ya", k, T) for k in range(8)])
                  tr_, trk = tmp(512)
                  sc.op("act", [("ps", b_gr)], [trk], lambda e: e.activation(tr_, bank(b_gr), AF.Tanh, scale=0.5))
                  ta_, tak = tmp(512)
                  sc.op("act", [("ps", b_ga)], [tak], lambda e: e.activation(ta_, bank(b_ga), AF.Tanh, scale=0.5))
                  sc.op("dve", [trk, ("ps", b_br)], [trk], lambda e: e.scalar_tensor_tensor(
                      tr_, tr_, 1.0, bank(b_br), ALU.add, ALU.mult))
                  sc.op("dve", [tak, ("ps", b_ba)], [tak], lambda e: e.scalar_tensor_tensor(
                      ta_, ta_, 1.0, bank(b_ba), ALU.add, ALU.mult))
                  sc.op("dve", [trk, tak], [("m", c, T)] + ATT_KEYS, lambda e: e.tensor_tensor(
                      mT[:, c, cols], tr_, ta_, ALU.add))
          if debug and s == 0:
              dump("mT", mT.rearrange("p k t -> p (k t)"), [("m", c, T) for c in range(8) for T in range(4)])
          cut("E2")

          load_gain(1)
          wF = []
          for hf in range(2):
              wsl, wk = wnext(second=(hf == 1))
              wF.append((wsl[:, 0:4096].rearrange("p (k n) -> p k n", k=8), wk))
          pendF = []
          for i in range(NT):
              xt, xk = xt_r.get()
              sc.dma("sp", xt[:, :], x_d[s, i * 128:(i + 1) * 128, :], [], [xk])
              b = ps2()
              for half in range(2):
                  w, wk = wF[half]
                  mm_group(b + half, 0, 512, lambda k: mT[:, k, i * 128:(i + 1) * 128],
                           lambda k: w[:, k, :], 8, wk + [("m", k, i // 4) for k in range(8)])
              alias = [("yr", i, T) for T in range(4)] if i < 8 else [("ya", i - 8, T) for T in range(4)]
              sc.op("dve", [("ps", b), ("ps", b + 1), xk], [("x1", i)] + alias, lambda e: e.scalar_tensor_tensor(
                  x1[:, i, :], ps[:, b * 512:(b + 2) * 512], 0.5, xt[:, :], ALU.mult, ALU.add))
              if len(pendF) >= 2:
                  pendF.pop(0)()
              pendF.append(norm_to_T(x1[:, i, :], ("x1", i), i))
          while pendF:
              pendF.pop(0)()
          if debug and s == 0:
              dump("x1", x1[:, :, :].rearrange("p k t -> p (k t)"), [("x1", i) for i in range(NT)])
          cut("F")

          load_gain(2)
          TP.pool = pool_b
          gw = {}
          gitems = [(g, T) for g in range(8) for T in range(NSUP)]
          hid_of = {}

          def emit_mlp1(k):
              g, T = gitems[k]
              if T == 0:
                  wsl, wk1 = wnext(second=True)
                  gw[("w1", g)] = (wsl[:, 0:4096].rearrange("p (k n) -> p k n", k=8), wk1)
              w1, wk1 = gw[("w1", g)]
              cols = slice(T * 512, (T + 1) * 512)
              hids = []
              for hh in range(2):
                  b = ps2()
                  for cc in range(2):
                      ch = hh * 2 + cc
                      mm_group(b + cc, 0, 512, lambda k: w1[:, k, ch * 128:(ch + 1) * 128],
                               lambda k: hT[:, k, cols], 8, wk1 + hT_keys(T))
                  hid, hidk = tmp(1024, BF16)
                  for cc in range(2):
                      m_, mk = tmp(512)
                      sc.op("act", [("ps", b + cc)], [mk], lambda e: e.activation(m_, bank(b + cc), AF.Relu))
                      sc.op("dve", [mk], [hidk + (cc,)], lambda e: e.tensor_tensor(
                          hid[:, cc * 512:(cc + 1) * 512], m_, m_, ALU.mult))
                  hids.append((hid, hidk))
              hid_of[k] = hids

          def emit_mlp2(k):
              g, T = gitems[k]
              if T == 0:
                  wsl2, wk2 = wnext(second=True)
                  gw[("w2", g)] = (wsl2[:, 0:4096].rearrange("p (k n) -> p k n", k=4), wk2)
              w2, wk2 = gw[("w2", g)]
              hids = hid_of.pop(k)
              for r in range(4):
                  i = 4 * T + r
                  b = ps2()
                  for half in range(2):
                      mm_group(b + half, 0, 512,
                               lambda k_: hids[k_ // 2][0][:, (k_ % 2) * 512 + r * 128:(k_ % 2) * 512 + (r + 1) * 128],
                               lambda k_: w2[:, k_, half * 512:(half + 1) * 512], 4,
                               wk2 + [hids[a_][1] + (c_,) for a_ in range(2) for c_ in range(2)])
                  sc.op("dve", [("ps", b), ("ps", b + 1), ("x1", i)], [("x1", i)], lambda e: e.tensor_tensor(
                      x1[:, i, :], ps[:, b * 512:(b + 2) * 512], x1[:, i, :], ALU.add))
                  if g == 7:
                      stt_, stk = st_r.get()
                      sc.op("act", [("x1", i)], [stk, "junk"], lambda e: e.activation(
                          junk[:, :], x1[:, i, :], AF.Square, scale=1.0 / 32.0, accum_out=stt_[:, 0:1]))
                      sc.op("act", [stk, "epsc"], [stk + ("l",)], lambda e: e.activation(
                          stt_[:, 2:3], stt_[:, 0:1], AF.Ln, bias=epsc[:, 0:1]))
                      sc.op("act", [stk + ("l",)], [stk + ("r",)], lambda e: e.activation(
                          stt_[:, 1:2], stt_[:, 2:3], AF.Exp, scale=-0.5))
                      ob, obk = xt_r.get()
                      sc.op("dve", [("x1", i), stk + ("r",), "gain"], [obk], lambda e: e.scalar_tensor_tensor(
                          ob[:, :], x1[:, i, :], stt_[:, 1:2], gain[:, :], ALU.mult, ALU.mult))
                      sc.dma("sp", out_d[s, i * 128:(i + 1) * 128, :], ob[:, :], [obk], [], is_out=True)

          emit_mlp1(0)
          for k in range(len(gitems)):
              if k + 1 < len(gitems):
                  emit_mlp1(k + 1)
              emit_mlp2(k)
          TP.pool = pool_a
          if debug:
              break

    except _Stop:
        pass
    sc.finish()
    return nc


def _tile_rows(w, ncols_group):
    K, N = w.shape
    kc = K // 128
    g = N // ncols_group
    return np.ascontiguousarray(
        w.reshape(kc, 128, g, ncols_group).transpose(2, 1, 0, 3).reshape(g, 128, kc * ncols_group))


def _host_layout(inp):
    f = np.float32
    w_in = np.asarray(inp["w_in"][0], f)
    ux = w_in[:, 0:1024].reshape(1024, 8, 128)
    ug = w_in[:, 1024:2048].reshape(1024, 8, 128)
    wB = _tile_rows(np.concatenate([ux, ug], axis=2).reshape(1024, 8 * 256), 256)
    q = w_in[:, 2048:3072].reshape(1024, 8, 128)
    k = w_in[:, 3072:4096].reshape(1024, 8, 128)
    v = w_in[:, 4096:5120].reshape(1024, 8, 128)
    wD = _tile_rows(np.concatenate([q, k, v], axis=2).reshape(1024, 8 * 384), 384)
    gr = w_in[:, 5120:6144].reshape(1024, 8, 128)
    ga = w_in[:, 6144:7168].reshape(1024, 8, 128)
    br = np.asarray(inp["w_br_rnn"][0], f).reshape(1024, 8, 128)
    ba = np.asarray(inp["w_br_attn"][0], f).reshape(1024, 8, 128)
    wE = _tile_rows(np.concatenate([gr, ga, br, ba], axis=2).reshape(1024, 8 * 512), 512)
    wF = _tile_rows(np.asarray(inp["w_out"][0], f), 512)
    wG1 = _tile_rows(np.asarray(inp["w_mlp1"][0], f), 512)
    w2 = np.asarray(inp["w_mlp2"][0], f)
    wG2 = np.ascontiguousarray(w2.reshape(8, 4, 128, 1024).transpose(0, 2, 1, 3).reshape(8, 128, 4096))
    gains = np.stack([np.broadcast_to(np.asarray(inp[n], f).reshape(1, D), (128, D))
                      for n in ("norm1_g", "norm2_g", "normf_g")]).copy()

    def fm(vv):
        return np.asarray(vv, f).reshape(8, 128).T
    chp = np.zeros((128, 8, 8), f)
    cw = np.asarray(inp["conv_w"][0], f)
    for j in range(4):
        chp[:, :, j] = fm(cw[j])
    chp[:, :, 4] = fm(inp["conv_b"][0])
    chp[:, :, 5] = fm(np.asarray(inp["rg_a_b"][0]).reshape(-1))
    chp[:, :, 6] = fm(np.asarray(inp["rg_x_b"][0]).reshape(-1))
    chp[:, :, 7] = fm(inp["lru_lambda"][0])
    wbd = np.zeros((128, 2, 8, 128), f)
    for a, nm in enumerate(("rg_a_w", "rg_x_w")):
        ww = np.asarray(inp[nm][0], f)
        for c in range(8):
            wbd[0:64, a, c, 0:64] = ww[2 * c]
            wbd[64:128, a, c, 64:128] = ww[2 * c + 1]
    half = 32
    inv_freq = 10000.0 ** (-np.arange(half, dtype=np.float64) * 2.0 / 64.0)
    ang = np.arange(S, dtype=np.float64)[:, None] * inv_freq[None, :]
    rope = np.zeros((128, 2, 16, 32), f)
    rope[:, 0] = np.cos(ang).astype(f).reshape(16, 128, 32).transpose(1, 0, 2)
    rope[:, 1] = np.sin(ang).astype(f).reshape(16, 128, 32).transpose(1, 0, 2)
    lamv = np.stack([np.broadcast_to(np.asarray(inp[n][0], f).reshape(1, 64), (128, 64))
                     for n in ("lambda_q1", "lambda_k1", "lambda_q2", "lambda_k2")], axis=1).copy()
    subg = np.broadcast_to(np.asarray(inp["subln_g"][0], f).reshape(1, 128), (128, 128)).copy()
    return {
        "wB": wB, "wD": wD, "wE": wE, "wF": wF, "wG1": wG1, "wG2": wG2,
        "gains": gains, "chp": chp.reshape(128, 64), "wbd": wbd.reshape(128, 2048),
        "rope": rope.reshape(128, 1024), "lamv": lamv.reshape(128, 256), "subg": subg,
        "ident": np.eye(128, dtype=f),
    }


_NC_CACHE = {}


def kernel(**inputs):
    x = np.ascontiguousarray(np.asarray(inputs["x"], np.float32))
    shared = _host_layout(inputs)
    if "nc" not in _NC_CACHE:
        _NC_CACHE["nc"] = build(False)
    nc = _NC_CACHE["nc"]
    in_maps = []
    for c in range(NCORES):
        m = dict(shared)
        m["x"] = x[c * NSEQ:(c + 1) * NSEQ]
        in_maps.append(m)
    res = run_bass_kernel_spmd(nc, in_maps, core_ids=list(range(NCORES)))
    return np.concatenate([np.asarray(r["out"], np.float32) for r in res.results], axis=0)
```

```python
import math
import os
from contextlib import ExitStack

import numpy as np
import concourse.bass as bass
import concourse.mybir as mybir
from concourse.bass_utils import run_bass_kernel_spmd

F32 = mybir.dt.float32
BF16 = mybir.dt.bfloat16
AF = mybir.ActivationFunctionType
ALU = mybir.AluOpType
AX = mybir.AxisListType

D = 1024
S = 2048
NT = 16
NSUP = 4
NSEQ = 2
NCORES = 8
NDS = 16
LAM_INIT = 0.8 - 0.6 * math.exp(0.0)
NORM_EPS = 1e-6
NSTREAM = 4
LV = int(os.environ.get('KDBG_LV', '9'))
SUBLN_EPS = 1e-5
GELU_C = 0.7978845608028654


class Sched:
    def __init__(self, nc, es):
        self.nc = nc
        self.E = {"pe": nc.tensor, "act": nc.scalar, "dve": nc.vector, "pool": nc.gpsimd, "sp": nc.sync}
        self.sems = {e: es.enter_context(nc.semaphore("c_" + e)) for e in ("pe", "act", "dve", "pool")}
        self.cnt = {e: 0 for e in self.sems}
        self.sid = {e: "c_" + e for e in self.sems}
        self.epoch = 0
        self.seen = {e: {} for e in self.E}
        self.lastw = {}
        self.readers = {}
        self.dsems = [es.enter_context(nc.semaphore("d%d" % i)) for i in range(NDS)]
        self.dval = [0] * NDS
        self.di = 0
        self.nrd = 0
        self.out_toks = []
        self.own_last = {}

    def _deps(self, reads, writes):
        deps = []
        for k in reads:
            t = self.lastw.get(k)
            if t is not None:
                deps.append((t, 0))
            if k[0] == "ps":
                for t in self.readers.get(k, {}).values():
                    deps.append((t, 1))
        for k in writes:
            t = self.lastw.get(k)
            if t is not None:
                deps.append((t, 1))
            for t in self.readers.get(k, {}).values():
                deps.append((t, 1))
        return deps

    def _wait(self, eng, deps, is_dma):
        e = self.E[eng]
        seen = self.seen[eng]
        for (t, kind) in deps:
            sem, val, teng, sid = t
            if (not is_dma) and teng == eng:
                if eng == "pe" or kind != 0:
                    continue
            if seen.get(sid, 0) >= val:
                continue
            e.wait_ge(sem, val)
            seen[sid] = val

    def _record(self, tok, reads, writes):
        for k in writes:
            self.lastw[k] = tok
            self.readers[k] = {}
        for k in reads:
            if k in writes:
                continue
            d = self.readers.setdefault(k, {})
            if tok[2] == "dma":
                self.nrd += 1
                d[("dma", self.nrd)] = tok
            else:
                d[tok[2]] = tok

    def new_epoch(self, es):
        self.epoch += 1
        for e in ("pe", "act", "dve"):
            self.sems[e] = es.enter_context(self.nc.semaphore("c%d_%s" % (self.epoch, e)))
            self.cnt[e] = 0
            self.sid[e] = "c%d_%s" % (self.epoch, e)

    def op(self, eng, reads, writes, fn):
        self._wait(eng, self._deps(reads, writes), False)
        ins = fn(self.E[eng])
        self.cnt[eng] += 1
        ins.then_inc(self.sems[eng], 1)
        tok = (self.sems[eng], self.cnt[eng], eng, self.sid[eng])
        self._record(tok, reads, writes)
        return tok

    def dma_own(self, q, out, in_, reads, writes, own):
        self._wait(q, self._deps(reads, writes), True)
        sem, st = own
        if st["n"] > 0:
            self.E[q].wait_ge(sem, 16)
            self.E[q].sem_clear(sem)
        st["n"] += 1
        ins = self.E[q].dma_start(out=out, in_=in_)
        ins.then_inc(sem, 16)
        tok = (sem, 16, "dma", "own%s_%d" % (st["name"], st["n"]))
        self._record(tok, reads, writes)
        self.own_last[st["name"]] = (sem, tok[3])
        return tok

    def dma(self, q, out, in_, reads, writes, is_out=False):
        self._wait(q, self._deps(reads, writes), True)
        i = self.di
        self.di = (self.di + 1) % NDS
        sem = self.dsems[i]
        sid = "d%d" % i
        if self.dval[i] > 0 and self.seen[q].get(sid, 0) < self.dval[i]:
            self.E[q].wait_ge(sem, self.dval[i])
            self.seen[q][sid] = self.dval[i]
        ins = self.E[q].dma_start(out=out, in_=in_)
        self.dval[i] += 16
        ins.then_inc(sem, 16)
        tok = (sem, self.dval[i], "dma", sid)
        self._record(tok, reads, writes)
        if is_out:
            self.out_toks.append(tok)
        return tok

    def finish(self):
        e = self.E["sp"]
        for name, (sem, sid) in self.own_last.items():
            if self.seen["sp"].get(sid, 0) < 16:
                e.wait_ge(sem, 16)
        for i in range(NDS):
            sid = "d%d" % i
            if self.dval[i] > 0 and self.seen["sp"].get(sid, 0) < self.dval[i]:
                e.wait_ge(self.dsems[i], self.dval[i])
                self.seen["sp"][sid] = self.dval[i]
        for (sem, val, _, sid) in self.out_toks:
            if self.seen["sp"].get(sid, 0) < val:
                e.wait_ge(sem, val)
                self.seen["sp"][sid] = val


class Rot:
    def __init__(self, nc, name, n, shape, dtype):
        self.bufs = [nc.alloc_sbuf_tensor("sb_%s%d" % (name, i), shape, dtype) for i in range(n)]
        self.name = name
        self.i = 0

    def get(self):
        j = self.i % len(self.bufs)
        self.i += 1
        return self.bufs[j], (self.name, j)


class _Stop(Exception):
    pass


def build(debug=False, upto=None, skipB=False):
    nc = bass.Bass("TRN2", target_bir_lowering=False)
    es = ExitStack()

    def dram(name, shape, dt=F32, kind="ExternalInput"):
        return nc.dram_tensor(name, shape, dt, kind=kind).ap()

    x_d = dram("x", [NSEQ, S, D])
    out_d = dram("out", [NSEQ, S, D], kind="ExternalOutput")
    wB_d = dram("wB", [8, 128, 8 * 256])
    wD_d = dram("wD", [8, 128, 8 * 384])
    wE_d = dram("wE", [8, 128, 8 * 512])
    wF_d = dram("wF", [2, 128, 8 * 512])
    wG1_d = dram("wG1", [8, 128, 8 * 512])
    wG2_d = dram("wG2", [8, 128, 4 * 1024])
    gains_d = dram("gains", [3, 128, D])
    chp_d = dram("chp", [128, 8 * 8])
    wbd_d = dram("wbd", [128, 2 * 8 * 128])
    rope_d = dram("rope", [128, 2 * 16 * 32])
    lamv_d = dram("lamv", [128, 4 * 64])
    subg_d = dram("subg", [128, 128])
    ident_d = dram("ident", [128, 128])
    dbg = {}
    if debug:
        for nm in ("hT", "yrT", "yaT", "mT"):
            dbg[nm] = dram("dbg_" + nm, [128, 8 * S], BF16, kind="ExternalOutput")
        dbg["x1"] = dram("dbg_x1", [128, 16 * D], F32, kind="ExternalOutput")

    sc = Sched(nc, es)
    def A(name, shape, dt):
        return nc.alloc_sbuf_tensor("sb_" + name, shape, dt)

    hT = A("hT", [128, 8, S], BF16)
    BC = A("BC", [128, 16 * D], F32)
    BCb = BC.bitcast(BF16).reshape([128, 16, S])
    x1 = BC.reshape([128, 16, D])
    ATT = A("ATT", [128, 16448], BF16)
    mT = ATT[:, 0:16384].rearrange("p (k t) -> p k t", k=8)
    qkT = [ATT[:, i * 4096:(i + 1) * 4096].rearrange("p (a t) -> p a t", a=2) for i in range(2)]
    V1 = [ATT[:, 8192 + i * 2080: 8192 + (i + 1) * 2080].rearrange("p (t e) -> p t e", t=NT) for i in range(2)]
    yat = [ATT[:, 12352 + i * 2048: 12352 + (i + 1) * 2048].rearrange("p (t e) -> p t e", t=NT) for i in range(2)]
    ATT_KEYS = ([("qkT", a, T) for a in range(2) for T in range(4)] + [("V1", a) for a in range(2)]
                + [("yat", a, T) for a in range(2) for T in range(4)])
    M_KEYS = [("m", c, T) for c in range(8) for T in range(4)]
    wslot = [A("wslot%d" % i, [128, 4096], BF16) for i in range(3)]
    gain = A("gain", [128, D], F32)
    chp = A("chp", [128, 8, 8], F32)
    chq = A("chq", [128, 8, 4], F32)
    wbd = A("wbd", [128, 2, 8, 128], BF16)
    rope = A("rope", [128, 2, 16, 32], F32)
    lamv = A("lamv", [128, 4, 64], F32)
    subg = A("subg", [128, 128], F32)
    ident = A("ident", [128, 128], BF16)
    neglam = A("neglam", [128, 1], F32)
    junk = A("junk", [128, D], BF16)
    setup_t = A("setup_t", [128, 64], F32)
    setup_u = A("setup_u", [128, 64], F32)
    setup_s = A("setup_s", [128, 8], F32)
    setup_v = A("setup_v", [128, 4], F32)
    halo = A("halo", [128, 4], F32)
    hlast = A("hlast", [128, 1], F32)
    epsc = A("epsc", [128, 2], F32)

    ps = nc.alloc_psum_tensor("ps", [128, 4096], F32)
    psb = ps.bitcast(BF16)

    class PS:
        nxt = 0

    def ps1():
        b = PS.nxt % 8
        PS.nxt += 1
        return b

    def ps2():
        if PS.nxt % 2:
            PS.nxt += 1
        b = PS.nxt % 8
        PS.nxt += 2
        return b

    class PSL:
        nxt = 0

    def ps1lo():
        b = PSL.nxt % 4
        PSL.nxt += 1
        return b

    class PSP3:
        nxt = 0

    def ps1p3():
        b = PSP3.nxt % 3
        PSP3.nxt += 1
        return b

    def bank(b, n=512, off=0):
        return ps[:, b * 512 + off: b * 512 + off + n]

    xt_r = Rot(nc, "xt", 2, [128, D], F32)
    st_r = Rot(nc, "st", 12, [128, 4], F32)
    fino_r = Rot(nc, "fino", 6, [128, 256], F32)
    NBLK = 11
    blk = [A("blk%d" % i, [128, 516], F32) for i in range(NBLK)]
    blkb = [b_.bitcast(BF16) for b_ in blk]

    ATTf = ATT.bitcast(F32)
    NAB = 15
    pool_a = [(blk[j], blkb[j], ("blk", j)) for j in range(NBLK)]
    pool_b = pool_a + [(ATTf[:, j * 516:(j + 1) * 516], ATT[:, j * 1032:(j + 1) * 1032], ("ab", j))
                       for j in range(NAB)]
    M_KEYS.extend([("ab", j) for j in range(NAB)])

    class TP:
        i = 0
        pool = pool_a

    def tmp(n, dt=F32):
        j = TP.i % len(TP.pool)
        TP.i += 1
        f, bview, key = TP.pool[j]
        if dt == F32:
            return f[:, 0:n], key
        return bview[:, 0:n], key

    halo2 = [A("halo2_%d" % i, [128, 4], F32) for i in range(NSTREAM)]
    hlast2 = [A("hlast2_%d" % i, [128, 1], F32) for i in range(NSTREAM)]

    nc.allow_low_precision("bf16 matmul operands with fp32 accumulation (per problem tolerance)")

    sc.dma("sp", chp[:, :, :], chp_d.rearrange("p (c k) -> p c k", c=8), [], ["chp"])
    sc.dma("sp", rope[:, :, :, :], rope_d.rearrange("p (a t f) -> p a t f", a=2, t=16), [], ["rope"])
    sc.dma("sp", lamv[:, :, :], lamv_d.rearrange("p (a f) -> p a f", a=4), [], ["lamv"])
    sc.dma("sp", subg[:, :], subg_d, [], ["subg"])
    def own_sem(name):
        return (es.enter_context(nc.semaphore("o_" + name)), {"n": 0, "name": name})
    sc.dma_own("pool", wbd[:, :, :, :], wbd_d.rearrange("p (a c f) -> p a c f", a=2, c=8), [], ["wbd"],
               own_sem("wbd"))
    sc.dma_own("pool", ident[:, :], ident_d, [], ["ident"], own_sem("ident"))

    wlist = []
    for s in range(NSEQ):
        for cp in range(0 if skipB else 4):
            wlist.append((wB_d[2 * cp:2 * cp + 2].rearrange("c p n -> p c n"), 4096))
        for h in range(8):
            wlist.append((wD_d[h], 3072))
        for c in range(8):
            wlist.append((wE_d[c], 4096))
        for hf in range(2):
            wlist.append((wF_d[hf], 4096))
        for g in range(8):
            wlist.append((wG1_d[g], 4096))
            wlist.append((wG2_d[g], 4096))
        if debug:
            break
    WS = {"n": 0, "loaded": 0}

    def wnext(second=False):
        n = WS["n"]
        WS["n"] += 1
        while WS["loaded"] < min(n + (2 if second else 3), len(wlist)):
            m = WS["loaded"]
            src_, ncol = wlist[m]
            dst_ = wslot[m % 3][:, 0:ncol]
            if len(src_.shape) == 3:
                dst_ = dst_.rearrange("p (c n) -> p c n", c=src_.shape[1])
            sc.dma_own("pool", dst_, src_, [], [("w", m % 3)], own_sem("w%d" % m))
            WS["loaded"] += 1
        return wslot[n % 3], [("w", n % 3)]

    sc.op("dve", [], ["epsc"], lambda e: e.memset(epsc[:, 0:1], float(NORM_EPS)))
    sc.op("dve", [], ["epsc"], lambda e: e.memset(epsc[:, 1:2], float(SUBLN_EPS)))
    sc.op("dve", ["subg"], ["subg"], lambda e: e.tensor_scalar(
        subg[:, :], subg[:, :], (1.0 - LAM_INIT), None, ALU.mult))
    sc.op("act", ["chp"], ["setup_s"], lambda e: e.activation(
        setup_s[:, :], chp[:, :, 7], AF.Exp, scale=-1.0))
    sc.op("act", ["setup_s"], ["setup_s"], lambda e: e.activation(
        setup_s[:, :], setup_s[:, :], AF.Ln, bias=1.0))
    sc.op("dve", ["setup_s"], ["chq"], lambda e: e.tensor_scalar(
        chq[:, :, 2], setup_s[:, :], -8.0, None, ALU.mult))
    sc.op("dve", ["setup_s"], ["chq"], lambda e: e.tensor_scalar(
        chq[:, :, 3], setup_s[:, :], -4.0, None, ALU.mult))
    sc.op("dve", ["chp"], ["chq"], lambda e: e.tensor_scalar(
        chq[:, :, 0:2], chp[:, :, 5:7], 0.5, None, ALU.mult))
    sc.op("dve", ["lamv"], ["setup_t"], lambda e: e.tensor_tensor(
        setup_t[:, :], lamv[:, 0, :], lamv[:, 1, :], ALU.mult))
    sc.op("dve", ["setup_t"], ["sv0"], lambda e: e.tensor_reduce(
        setup_v[:, 0:1], setup_t[:, :], AX.X, ALU.add))
    sc.op("dve", ["lamv"], ["setup_u"], lambda e: e.tensor_tensor(
        setup_u[:, :], lamv[:, 2, :], lamv[:, 3, :], ALU.mult))
    sc.op("dve", ["setup_u"], ["sv1"], lambda e: e.tensor_reduce(
        setup_v[:, 1:2], setup_u[:, :], AX.X, ALU.add))
    sc.op("act", ["sv0"], ["sv2"], lambda e: e.activation(setup_v[:, 2:3], setup_v[:, 0:1], AF.Exp))
    sc.op("act", ["sv1"], ["sv3"], lambda e: e.activation(setup_v[:, 3:4], setup_v[:, 1:2], AF.Exp))
    sc.op("dve", ["sv2", "sv3"], ["neglam"], lambda e: e.scalar_tensor_tensor(
        neglam[:, :], setup_v[:, 3:4], -LAM_INIT, setup_v[:, 2:3], ALU.add, ALU.subtract))

    def load_gain(idx):
        sc.dma("sp", gain[:, :], gains_d[idx], [], ["gain"])

    def norm_to_T(src_ap, src_key, i):
        stt_, stk = st_r.get()
        sc.op("act", [src_key], [stk, "junk"], lambda e: e.activation(
            junk[:, :], src_ap, AF.Square, scale=1.0 / 32.0, accum_out=stt_[:, 0:1]))
        sc.op("act", [stk, "epsc"], [stk + ("l",)], lambda e: e.activation(
            stt_[:, 2:3], stt_[:, 0:1], AF.Ln, bias=epsc[:, 0:1]))
        sc.op("act", [stk + ("l",)], [stk + ("r",)], lambda e: e.activation(
            stt_[:, 1:2], stt_[:, 2:3], AF.Exp, scale=-0.5))
        hb, hbk = tmp(1024, BF16)
        sc.op("dve", [src_key, stk + ("r",), "gain"], [hbk], lambda e: e.scalar_tensor_tensor(
            hb, src_ap, stt_[:, 1:2], gain[:, :], ALU.mult, ALU.mult))
        def part_b():
            b = ps1()

            def tr(e):
                ins = None
                for kc in range(8):
                    ins = e.transpose(psb[:, b * 1024 + kc * 128: b * 1024 + (kc + 1) * 128],
                                      hb[:, kc * 128:(kc + 1) * 128], ident[:, :])
                return ins
            sc.op("pe", [hbk, "ident"], [("ps", b)], tr)
            sc.op("act", [("ps", b)], [("hT", i)], lambda e: e.copy(
                hT[:, :, i * 128:(i + 1) * 128],
                psb[:, b * 1024:(b + 1) * 1024].rearrange("p (k t) -> p k t", k=8)))
        return part_b

    def mm_group(b, off, n, lhs_fn, rhs_fn, nk, reads):
        def f(e):
            ins = None
            for k in range(nk):
                ins = e.matmul(bank(b, n, off), lhs_fn(k), rhs_fn(k), start=(k == 0), stop=(k == nk - 1))
            return ins
        return sc.op("pe", reads, [("ps", b)], f)

    def hT_keys(T):
        return [("hT", 4 * T + r) for r in range(4)]

    def dump(name, ap_sb, keys):
        if debug:
            sc.dma("sp", dbg[name], ap_sb, keys, [], is_out=True)

    def cut(name):
        if upto == name:
            raise _Stop()

    try:
      for s in range(NSEQ):
          if s > 0:
              sc.new_epoch(es)
          load_gain(0)
          pendA = None
          for i in range(NT):
              xt, xk = xt_r.get()
              sc.dma("sp", xt[:, :], x_d[s, i * 128:(i + 1) * 128, :], [], [xk])
              pb = norm_to_T(xt[:, :], xk, i)
              if pendA is not None:
                  pendA()
              pendA = pb
          pendA()
          if debug and s == 0:
              dump("hT", hT[:, :, :].rearrange("p k t -> p (k t)"), [("hT", i) for i in range(NT)])
          cut("A")

          def b_stream(c, w, wk, sl):
              halo_c = halo2[sl]
              hlast_c = hlast2[sl]
              hk = ("halo", sl)
              hlk = ("hlast", sl)
              for T in range(NSUP):
                  cols = slice(T * 512, (T + 1) * 512)
                  b_ux = 2 * sl + 1
                  mm_group(b_ux, 0, 512, lambda k: w[:, k, 0:128], lambda k: hT[:, k, cols], 8, wk + hT_keys(T))
                  b_ug = 2 * sl
                  mm_group(b_ug, 0, 512, lambda k: w[:, k, 128:256], lambda k: hT[:, k, cols], 8, wk + hT_keys(T))
                  yield
                  ux, uxk = tmp(515)
                  if T == 0:
                      sc.op("dve", [], [uxk], lambda e: e.memset(ux[:, 0:3], 0.0))
                  else:
                      sc.op("dve", [hk], [uxk], lambda e: e.tensor_copy(ux[:, 0:3], halo_c[:, 0:3]))
                  sc.op("act", [("ps", b_ux)], [uxk + ("m",)], lambda e: e.copy(ux[:, 3:515], bank(b_ux)))
                  yield
                  uxr = [uxk, uxk + ("m",)]
                  if T < NSUP - 1:
                      sc.op("dve", uxr, [hk], lambda e: e.tensor_copy(halo_c[:, 0:3], ux[:, 512:515]))
                  xr, xrk = tmp(512)
                  sc.op("dve", uxr + ["chp"], [xrk], lambda e: e.tensor_scalar(
                      xr, ux[:, 0:512], chp[:, c, 0:1], chp[:, c, 4:5], ALU.mult, ALU.add))
                  for j in range(1, 4):
                      sc.op("dve", uxr + [xrk, "chp"], [xrk], lambda e, j=j: e.scalar_tensor_tensor(
                          xr, ux[:, j:j + 512], chp[:, c, j:j + 1], xr, ALU.mult, ALU.add))
                  yield
                  xb, xbk = tmp(512, BF16)
                  sc.op("act", [xrk], [xbk], lambda e: e.copy(xb, xr))
                  b_ga = 2 * sl + 1
                  mm_group(b_ga, 0, 512, lambda k: wbd[:, 0, c, :], lambda k: xb, 1, [xbk, "wbd"])
                  yield
                  tr_, trk = tmp(512)
                  sc.op("act", [("ps", b_ga), "chq"], [trk], lambda e: e.activation(
                      tr_, bank(b_ga), AF.Tanh, bias=chq[:, c, 0:1], scale=0.5))
                  b_gx = 2 * sl + 1
                  mm_group(b_gx, 0, 512, lambda k: wbd[:, 1, c, :], lambda k: xb, 1, [xbk, "wbd"])
                  yield
                  ti_, tik = tmp(512)
                  sc.op("act", [("ps", b_gx), "chq"], [tik], lambda e: e.activation(
                      ti_, bank(b_gx), AF.Tanh, bias=chq[:, c, 1:2], scale=0.5))
                  yield
                  a_, ak = tmp(512)
                  sc.op("act", [trk, "chq"], [ak], lambda e: e.activation(
                      a_, tr_, AF.Exp, bias=chq[:, c, 3:4], scale=chq[:, c, 3:4]))
                  a2_, a2k = tmp(512)
                  sc.op("act", [trk, "chq"], [a2k], lambda e: e.activation(
                      a2_, tr_, AF.Exp, bias=chq[:, c, 2:3], scale=chq[:, c, 2:3]))
                  sc.op("act", [trk, "chq"], [trk], lambda e: e.activation(
                      tr_, tr_, AF.Tanh, bias=chq[:, c, 3:4], scale=chq[:, c, 3:4]))
                  yield
                  sc.op("dve", [a2k, trk], [a2k], lambda e: e.scalar_tensor_tensor(
                      a2_, a2_, 1.0, tr_, ALU.add, ALU.mult))
                  sc.op("dve", [tik, xrk], [tik], lambda e: e.scalar_tensor_tensor(
                      ti_, ti_, 1.0, xr, ALU.add, ALU.mult))
                  yield
                  sc.op("act", [a2k], [a2k], lambda e: e.activation(a2_, a2_, AF.Ln, scale=-1.0))
                  sc.op("act", [a2k], [a2k], lambda e: e.activation(a2_, a2_, AF.Exp, scale=0.5))
                  sq, sqk = tmp(512)
                  sc.op("act", [("ps", b_ug)], [sqk], lambda e: e.activation(sq, bank(b_ug), AF.Square))
                  yield
                  sc.op("dve", [tik, a2k], [tik], lambda e: e.scalar_tensor_tensor(
                      ti_, ti_, 0.5, a2_, ALU.mult, ALU.mult))
                  hs, hsk = tmp(512)
                  if T == 0:
                      sc.op("dve", [ak, tik], [hsk], lambda e: e.tensor_tensor_scan(
                          hs, a_, ti_, 0.0, ALU.mult, ALU.add))
                  else:
                      sc.op("dve", [ak, tik, hlk], [hsk], lambda e: e.tensor_tensor_scan(
                          hs, a_, ti_, hlast_c[:, 0:1], ALU.mult, ALU.add))
                  if T < NSUP - 1:
                      sc.op("dve", [hsk], [hlk], lambda e: e.tensor_copy(hlast_c[:, 0:1], hs[:, 511:512]))
                  yield
                  sc.op("dve", [sqk], [sqk], lambda e: e.tensor_scalar(sq, sq, 0.044715, 1.0, ALU.mult, ALU.add))
                  sc.op("dve", [sqk, ("ps", b_ug)], [sqk], lambda e: e.tensor_tensor(sq, sq, bank(b_ug), ALU.mult))
                  yield
                  sc.op("act", [sqk], [sqk], lambda e: e.activation(sq, sq, AF.Tanh, scale=GELU_C))
                  yield
                  sc.op("dve", [sqk, ("ps", b_ug)], [sqk], lambda e: e.scalar_tensor_tensor(
                      sq, sq, 1.0, bank(b_ug), ALU.add, ALU.mult))
                  sc.op("dve", [sqk, hsk], [("yr", c, T), ("x1", c)], lambda e: e.scalar_tensor_tensor(
                      BCb[:, c, cols], sq, 0.5, hs, ALU.mult, ALU.mult))
                  yield

          TP.pool = pool_b
          active = []
          nxt = {"c": 0, "w": None}

          def start_next(sl):
              c = nxt["c"]
              if c >= (0 if skipB else 8):
                  return None
              nxt["c"] += 1
              if c % 2 == 0:
                  nxt["w"] = wnext(second=True)
              wslB, wk = nxt["w"]
              w = wslB[:, (c % 2) * 2048:(c % 2 + 1) * 2048].rearrange("p (k n) -> p k n", k=8)
              return (b_stream(c, w, wk, sl), sl)
          for sl in range(NSTREAM):
              g_ = start_next(sl)
              if g_ is not None:
                  active.append(g_)
          while active:
              for item in list(active):
                  g_, sl = item
                  try:
                      next(g_)
                  except StopIteration:
                      active.remove(item)
                      n_ = start_next(sl)
                      if n_ is not None:
                          active.append(n_)
          TP.pool = pool_a
          if debug and s == 0 and not skipB:
              dump("yrT", BCb[:, 0:8, :].rearrange("p k t -> p (k t)"),
                   [("yr", c, T) for c in range(8) for T in range(4)])
          cut("B")

          def proj_head(h):
              slot = h % 2
              wsl, wk = wnext()
              w = wsl[:, 0:3072].rearrange("p (k n) -> p k n", k=8)
              sc.op("dve", [], [("V1", slot)] + M_KEYS, lambda e: e.memset(V1[slot][:, :, 128:130], 1.0))
              bt = None
              pend = []
              for i in range(NT):
                  r = i % 4
                  b = ps1p3()
                  mm_group(b, 0, 384, lambda k: hT[:, k, i * 128:(i + 1) * 128], lambda k: w[:, k, :], 8,
                           wk + [("hT", i)])
                  if LV < 2:
                      continue
                  X = bank(b, 256).rearrange("p (a h f) -> p a h f", a=4, h=2)
                  cosb = rope[:, 0, i, :].unsqueeze(1).unsqueeze(1).broadcast_to([128, 4, 2, 32])
                  sinb = rope[:, 1, i, :].unsqueeze(1).broadcast_to([128, 4, 32])
                  t1, t1k = tmp(256)
                  t1v = t1.rearrange("p (a h f) -> p a h f", a=4, h=2)
                  sc.op("dve", [("ps", b), "rope"], [t1k], lambda e: e.tensor_tensor(t1v, X, cosb, ALU.mult))
                  tA, tAk = tmp(256)
                  tAv = tA.rearrange("p (a h f) -> p a h f", a=4, h=2)
                  sc.op("dve", [("ps", b), "rope"], [tAk], lambda e: e.tensor_tensor(
                      tAv[:, :, 0, :], X[:, :, 1, :], sinb, ALU.mult))
                  sc.op("dve", [("ps", b), "rope"], [tAk + ("b",)], lambda e: e.tensor_tensor(
                      tAv[:, :, 1, :], X[:, :, 0, :], sinb, ALU.mult))
                  qkb, qkbk = tmp(256, BF16)
                  qkv = qkb.rearrange("p (a h f) -> p a h f", a=4, h=2)
                  sc.op("dve", [t1k, tAk], [qkbk], lambda e: e.tensor_tensor(
                      qkv[:, :, 0, :], t1v[:, :, 0, :], tAv[:, :, 0, :], ALU.subtract))
                  sc.op("dve", [t1k, tAk + ("b",)], [qkbk + ("b",)], lambda e: e.tensor_tensor(
                      qkv[:, :, 1, :], t1v[:, :, 1, :], tAv[:, :, 1, :], ALU.add))
                  if LV < 3:
                      continue
                  sc.op("act", [("ps", b)], [("V1", slot)] + (M_KEYS if i == 0 else []), lambda e: e.copy(
                      V1[slot][:, i, 0:128], bank(b, 128, 256)))
                  if LV < 4:
                      continue
                  bt = 3

                  def tail(r=r, bt=bt, qkb=qkb, qkbk=qkbk, i=i):
                      def trq(e):
                          e.transpose(psb[:, bt * 1024 + r * 128: bt * 1024 + (r + 1) * 128], qkb[:, 0:128],
                                      ident[:, :])
                          return e.transpose(psb[:, bt * 1024 + 512 + r * 128: bt * 1024 + 512 + (r + 1) * 128],
                                             qkb[:, 128:256], ident[:, :])
                      sc.op("pe", [qkbk, qkbk + ("b",), "ident"], [("ps", bt)], trq)
                      if r == 3:
                          T = i // 4
                          sc.op("act", [("ps", bt)], [("qkT", slot, T)] + M_KEYS, lambda e: e.copy(
                              qkT[slot][:, :, T * 512:(T + 1) * 512],
                              psb[:, bt * 1024:(bt + 1) * 1024].rearrange("p (a t) -> p a t", a=2)))
                  if len(pend) >= 2:
                      pend.pop(0)()
                  pend.append(tail)
              while pend:
                  pend.pop(0)()

          def attn_head(h):
              slot = h % 2
              qT = qkT[slot]
              units = [(T, m, j) for T in range(NSUP) for m in range(2) for j in range(4 * T + 4)]
              LOOK = 3
              st_info = {}

              def emit_S(n):
                  T, m, j = units[n]
                  pr = slice(m * 64, (m + 1) * 64)
                  r0 = max(0, j - 4 * T)
                  N = (4 - r0) * 128
                  q0 = T * 512 + r0 * 128
                  bs = ps1lo()
                  sc.op("pe", [("qkT", slot, T), ("qkT", slot, j // 4)], [("ps", bs)],
                        lambda e: e.matmul(bank(bs, N), qT[pr, 1, j * 128:(j + 1) * 128], qT[pr, 0, q0:q0 + N],
                                           start=True, stop=True))
                  et, etk = tmp(512, BF16)
                  sc.op("act", [("ps", bs)], [etk], lambda e: e.activation(
                      et[:, 0:N], bank(bs, N), AF.Exp, scale=0.125))
                  if j >= 4 * T:
                      sc.op("pool", [etk], [etk], lambda e: e.memset(et[64:128, 0:64], 0.0))
                  st_info[n] = (et, etk, r0)

              def emit_PV(n):
                  T, m, j = units[n]
                  et, etk, r0 = st_info.pop(n)
                  ob = 4 + 2 * m

                  def pv(e):
                      ins = None
                      for r in range(r0, 4):
                          bb = ob + r // 2
                          off = (r % 2) * 130
                          ins = e.matmul(bank(bb, 130, off), et[:, (r - r0) * 128:(r - r0 + 1) * 128],
                                         V1[slot][:, j, :], start=(j == 0 and r % 2 == 0),
                                         stop=(j == 4 * T + r and r % 2 == 1))
                      return ins
                  sc.op("pe", [etk, ("V1", slot)], [("ps", ob), ("ps", ob + 1)], pv)

              def finalize(T):
                  osb = []
                  for m in range(2):
                      pair = []
                      for half in range(2):
                          dst, dk = tmp(260)
                          bb = 4 + 2 * m + half
                          sc.op("dve", [("ps", bb)], [dk], lambda e: e.tensor_copy(dst, bank(bb, 260)))
                          pair.append((dst, dk))
                      osb.append(pair)
                  parts = []
                  for half in range(2):
                      (s0, s0k) = osb[0][half]
                      (s1, s1k) = osb[1][half]
                      O0 = s0.rearrange("p (a f) -> p a f", a=2)
                      O1 = s1.rearrange("p (a f) -> p a f", a=2)
                      stt_, stk = st_r.get()
                      sc.op("dve", [s0k], [stk], lambda e: e.reciprocal(stt_[:, 0:2], O0[:, :, 128]))
                      sc.op("dve", [s1k], [stk + ("b",)], lambda e: e.reciprocal(stt_[:, 2:4], O1[:, :, 128]))
                      sc.op("dve", [stk + ("b",), "neglam"], [stk + ("b",)], lambda e: e.tensor_scalar(
                          stt_[:, 2:4], stt_[:, 2:4], neglam[:, 0:1], None, ALU.mult))
                      o0t, o0k = fino_r.get()
                      o0 = o0t[:, :]
                      o0v = o0.rearrange("p (a f) -> p a f", a=2)
                      sc.op("dve", [s0k, stk], [o0k], lambda e: e.tensor_tensor(
                          o0v, O0[:, :, 0:128], stt_[:, 0:2].unsqueeze(2).broadcast_to([128, 2, 128]), ALU.mult))
                      o1, o1k = tmp(256)
                      o1v = o1.rearrange("p (a f) -> p a f", a=2)
                      sc.op("dve", [s1k, stk + ("b",)], [o1k], lambda e: e.tensor_tensor(
                          o1v, O1[:, :, 0:128], stt_[:, 2:4].unsqueeze(2).broadcast_to([128, 2, 128]), ALU.mult))
                      sc.op("dve", [o0k, o1k], [o0k], lambda e: e.tensor_tensor(o0, o0, o1, ALU.add))
                      sc.op("dve", [o0k], [o1k], lambda e: e.tensor_tensor(o1, o0, o0, ALU.mult))
                      st2, st2k = st_r.get()
                      sc.op("dve", [o1k], [st2k], lambda e: e.tensor_reduce(st2[:, 0:2], o1v, AX.X, ALU.add))
                      parts.append((o0v, o0k, st2, st2k))

                  def part2():
                      for half, (o0v, o0k, st2, st2k) in enumerate(parts):
                          sc.op("act", [st2k, "epsc"], [st2k], lambda e: e.activation(
                              st2[:, 0:2], st2[:, 0:2], AF.Ln, bias=epsc[:, 1:2], scale=1.0 / 128.0))
                          sc.op("act", [st2k], [st2k], lambda e: e.activation(
                              st2[:, 0:2], st2[:, 0:2], AF.Exp, scale=-0.5))
                          sc.op("dve", [o0k, st2k], [o0k], lambda e: e.tensor_tensor(
                              o0v, o0v, st2[:, 0:2].unsqueeze(2).broadcast_to([128, 2, 128]), ALU.mult))
                          i0 = 4 * T + 2 * half
                          sc.op("dve", [o0k, "subg"], [("yat", slot, T)] + M_KEYS, lambda e: e.tensor_tensor(
                              yat[slot][:, i0:i0 + 2, :], o0v,
                              subg[:, :].unsqueeze(1).broadcast_to([128, 2, 128]), ALU.mult))

                  def part3():
                      bt = ps1lo()

                      def try_(e):
                          ins = None
                          for r in range(4):
                              ins = e.transpose(psb[:, bt * 1024 + r * 128: bt * 1024 + (r + 1) * 128],
                                                yat[slot][:, 4 * T + r, :], ident[:, :])
                          return ins
                      sc.op("pe", [("yat", slot, T), "ident"], [("ps", bt)], try_)
                      sc.op("act", [("ps", bt)], [("ya", h, T), ("x1", 8 + h)], lambda e: e.copy(
                          BCb[:, 8 + h, T * 512:(T + 1) * 512], psb[:, bt * 1024: bt * 1024 + 512]))
                  deferred.append([6, part2])
                  deferred.append([12, part3])

              nU = len(units)
              for n in range(min(LOOK, nU)):
                  emit_S(n)
              for n in range(nU):
                  if n + LOOK < nU:
                      emit_S(n + LOOK)
                  emit_PV(n)
                  tick_deferred()
                  T, m, j = units[n]
                  if m == 1 and j == 4 * T + 3:
                      finalize(T)
                      if upto == "E0":
                          flush_deferred()
                      cut("E0")

          deferred = []

          def tick_deferred():
              for d in list(deferred):
                  d[0] -= 1
                  if d[0] <= 0:
                      deferred.remove(d)
                      d[1]()

          def flush_deferred():
              while deferred:
                  d = deferred.pop(0)
                  d[1]()

          proj_head(0)
          cut("D")
          for h in range(8):
              if h + 1 < 8:
                  proj_head(h + 1)
              attn_head(h)
          flush_deferred()
          if debug and s == 0:
              dump("yaT", BCb[:, 8:16, :].rearrange("p k t -> p (k t)"),
                   [("ya", c, T) for c in range(8) for T in range(4)])
          cut("E")

          for c in range(8):
              wsl, wk = wnext()
              w = wsl[:, 0:4096].rearrange("p (k n) -> p k n", k=8)
              for T in range(NSUP):
                  cols = slice(T * 512, (T + 1) * 512)
                  b_gr = ps1()
                  mm_group(b_gr, 0, 512, lambda k: w[:, k, 0:128], lambda k: hT[:, k, cols], 8, wk + hT_keys(T))
                  b_ga = ps1()
                  mm_group(b_ga, 0, 512, lambda k: w[:, k, 128:256], lambda k: hT[:, k, cols], 8, wk + hT_keys(T))
                  b_br = ps1()
                  mm_group(b_br, 0, 512, lambda k: w[:, k, 256:384], lambda k: BCb[:, k, cols], 8,
                           wk + [("yr", k, T) for k in range(8)])
                  b_ba = ps1()
                  mm_group(b_ba, 0, 512, lambda k: w[:, k, 384:512], lambda k: BCb[:, 8 + k, cols], 8,
                           wk + [("ya", k, T) for k in range(8)])
                  tr_, trk = tmp(512)
                  sc.op("act", [("ps", b_gr)], [trk], lambda e: e.activation(tr_, bank(b_gr), AF.Tanh, scale=0.5))
                  ta_, tak = tmp(512)
                  sc.op("act", [("ps", b_ga)], [tak], lambda e: e.activation(ta_, bank(b_ga), AF.Tanh, scale=0.5))
                  sc.op("dve", [trk, ("ps", b_br)], [trk], lambda e: e.scalar_tensor_tensor(
                      tr_, tr_, 1.0, bank(b_br), ALU.add, ALU.mult))
                  sc.op("dve", [tak, ("ps", b_ba)], [tak], lambda e: e.scalar_tensor_tensor(
                      ta_, ta_, 1.0, bank(b_ba), ALU.add, ALU.mult))
                  sc.op("dve", [trk, tak], [("m", c, T)] + ATT_KEYS, lambda e: e.tensor_tensor(
                      mT[:, c, cols], tr_, ta_, ALU.add))
          if debug and s == 0:
              dump("mT", mT.rearrange("p k t -> p (k t)"), [("m", c, T) for c in range(8) for T in range(4)])
          cut("E2")

          load_gain(1)
          wF = []
          for hf in range(2):
              wsl, wk = wnext(second=(hf == 1))
              wF.append((wsl[:, 0:4096].rearrange("p (k n) -> p k n", k=8), wk))
          pendF = []
          for i in range(NT):
              xt, xk = xt_r.get()
              sc.dma("sp", xt[:, :], x_d[s, i * 128:(i + 1) * 128, :], [], [xk])
              b = ps2()
              for half in range(2):
                  w, wk = wF[half]
                  mm_group(b + half, 0, 512, lambda k: mT[:, k, i * 128:(i + 1) * 128],
                           lambda k: w[:, k, :], 8, wk + [("m", k, i // 4) for k in range(8)])
              alias = [("yr", i, T) for T in range(4)] if i < 8 else [("ya", i - 8, T) for T in range(4)]
              sc.op("dve", [("ps", b), ("ps", b + 1), xk], [("x1", i)] + alias, lambda e: e.scalar_tensor_tensor(
                  x1[:, i, :], ps[:, b * 512:(b + 2) * 512], 0.5, xt[:, :], ALU.mult, ALU.add))
              if len(pendF) >= 2:
                  pendF.pop(0)()
              pendF.append(norm_to_T(x1[:, i, :], ("x1", i), i))
          while pendF:
              pendF.pop(0)()
          if debug and s == 0:
              dump("x1", x1[:, :, :].rearrange("p k t -> p (k t)"), [("x1", i) for i in range(NT)])
          cut("F")

          load_gain(2)
          TP.pool = pool_b
          gw = {}
          gitems = [(g, T) for g in range(8) for T in range(NSUP)]
          hid_of = {}

          def emit_mlp1(k):
              g, T = gitems[k]
              if T == 0:
                  wsl, wk1 = wnext(second=True)
                  gw[("w1", g)] = (wsl[:, 0:4096].rearrange("p (k n) -> p k n", k=8), wk1)
              w1, wk1 = gw[("w1", g)]
              cols = slice(T * 512, (T + 1) * 512)
              hids = []
              for hh in range(2):
                  b = ps2()
                  for cc in range(2):
                      ch = hh * 2 + cc
                      mm_group(b + cc, 0, 512, lambda k: w1[:, k, ch * 128:(ch + 1) * 128],
                               lambda k: hT[:, k, cols], 8, wk1 + hT_keys(T))
                  hid, hidk = tmp(1024, BF16)
                  for cc in range(2):
                      m_, mk = tmp(512)
                      sc.op("act", [("ps", b + cc)], [mk], lambda e: e.activation(m_, bank(b + cc), AF.Relu))
                      sc.op("dve", [mk], [hidk + (cc,)], lambda e: e.tensor_tensor(
                          hid[:, cc * 512:(cc + 1) * 512], m_, m_, ALU.mult))
                  hids.append((hid, hidk))
              hid_of[k] = hids

          def emit_mlp2(k):
              g, T = gitems[k]
              if T == 0:
                  wsl2, wk2 = wnext(second=True)
                  gw[("w2", g)] = (wsl2[:, 0:4096].rearrange("p (k n) -> p k n", k=4), wk2)
              w2, wk2 = gw[("w2", g)]
              hids = hid_of.pop(k)
              for r in range(4):
                  i = 4 * T + r
                  b = ps2()
                  for half in range(2):
                      mm_group(b + half, 0, 512,
                               lambda k_: hids[k_ // 2][0][:, (k_ % 2) * 512 + r * 128:(k_ % 2) * 512 + (r + 1) * 128],
                               lambda k_: w2[:, k_, half * 512:(half + 1) * 512], 4,
                               wk2 + [hids[a_][1] + (c_,) for a_ in range(2) for c_ in range(2)])
                  sc.op("dve", [("ps", b), ("ps", b + 1), ("x1", i)], [("x1", i)], lambda e: e.tensor_tensor(
                      x1[:, i, :], ps[:, b * 512:(b + 2) * 512], x1[:, i, :], ALU.add))
                  if g == 7:
                      stt_, stk = st_r.get()
                      sc.op("act", [("x1", i)], [stk, "junk"], lambda e: e.activation(
                          junk[:, :], x1[:, i, :], AF.Square, scale=1.0 / 32.0, accum_out=stt_[:, 0:1]))
                      sc.op("act", [stk, "epsc"], [stk + ("l",)], lambda e: e.activation(
                          stt_[:, 2:3], stt_[:, 0:1], AF.Ln, bias=epsc[:, 0:1]))
                      sc.op("act", [stk + ("l",)], [stk + ("r",)], lambda e: e.activation(
                          stt_[:, 1:2], stt_[:, 2:3], AF.Exp, scale=-0.5))
                      ob, obk = xt_r.get()
                      sc.op("dve", [("x1", i), stk + ("r",), "gain"], [obk], lambda e: e.scalar_tensor_tensor(
                          ob[:, :], x1[:, i, :], stt_[:, 1:2], gain[:, :], ALU.mult, ALU.mult))
                      sc.dma("sp", out_d[s, i * 128:(i + 1) * 128, :], ob[:, :], [obk], [], is_out=True)

          emit_mlp1(0)
          for k in range(len(gitems)):
              if k + 1 < len(gitems):
                  emit_mlp1(k + 1)
              emit_mlp2(k)
          TP.pool = pool_a
          if debug:
              break

    except _Stop:
        pass
    sc.finish()
    return nc


def _tile_rows(w, ncols_group):
    K, N = w.shape
    kc = K // 128
    g = N // ncols_group
    return np.ascontiguousarray(
        w.reshape(kc, 128, g, ncols_group).transpose(2, 1, 0, 3).reshape(g, 128, kc * ncols_group))


def _host_layout(inp):
    f = np.float32
    w_in = np.asarray(inp["w_in"][0], f)
    ux = w_in[:, 0:1024].reshape(1024, 8, 128)
    ug = w_in[:, 1024:2048].reshape(1024, 8, 128)
    wB = _tile_rows(np.concatenate([ux, ug], axis=2).reshape(1024, 8 * 256), 256)
    q = w_in[:, 2048:3072].reshape(1024, 8, 128)
    k = w_in[:, 3072:4096].reshape(1024, 8, 128)
    v = w_in[:, 4096:5120].reshape(1024, 8, 128)
    wD = _tile_rows(np.concatenate([q, k, v], axis=2).reshape(1024, 8 * 384), 384)
    gr = w_in[:, 5120:6144].reshape(1024, 8, 128)
    ga = w_in[:, 6144:7168].reshape(1024, 8, 128)
    br = np.asarray(inp["w_br_rnn"][0], f).reshape(1024, 8, 128)
    ba = np.asarray(inp["w_br_attn"][0], f).reshape(1024, 8, 128)
    wE = _tile_rows(np.concatenate([gr, ga, br, ba], axis=2).reshape(1024, 8 * 512), 512)
    wF = _tile_rows(np.asarray(inp["w_out"][0], f), 512)
    wG1 = _tile_rows(np.asarray(inp["w_mlp1"][0], f), 512)
    w2 = np.asarray(inp["w_mlp2"][0], f)
    wG2 = np.ascontiguousarray(w2.reshape(8, 4, 128, 1024).transpose(0, 2, 1, 3).reshape(8, 128, 4096))
    gains = np.stack([np.broadcast_to(np.asarray(inp[n], f).reshape(1, D), (128, D))
                      for n in ("norm1_g", "norm2_g", "normf_g")]).copy()

    def fm(vv):
        return np.asarray(vv, f).reshape(8, 128).T
    chp = np.zeros((128, 8, 8), f)
    cw = np.asarray(inp["conv_w"][0], f)
    for j in range(4):
        chp[:, :, j] = fm(cw[j])
    chp[:, :, 4] = fm(inp["conv_b"][0])
    chp[:, :, 5] = fm(np.asarray(inp["rg_a_b"][0]).reshape(-1))
    chp[:, :, 6] = fm(np.asarray(inp["rg_x_b"][0]).reshape(-1))
    chp[:, :, 7] = fm(inp["lru_lambda"][0])
    wbd = np.zeros((128, 2, 8, 128), f)
    for a, nm in enumerate(("rg_a_w", "rg_x_w")):
        ww = np.asarray(inp[nm][0], f)
        for c in range(8):
            wbd[0:64, a, c, 0:64] = ww[2 * c]
            wbd[64:128, a, c, 64:128] = ww[2 * c + 1]
    half = 32
    inv_freq = 10000.0 ** (-np.arange(half, dtype=np.float64) * 2.0 / 64.0)
    ang = np.arange(S, dtype=np.float64)[:, None] * inv_freq[None, :]
    rope = np.zeros((128, 2, 16, 32), f)
    rope[:, 0] = np.cos(ang).astype(f).reshape(16, 128, 32).transpose(1, 0, 2)
    rope[:, 1] = np.sin(ang).astype(f).reshape(16, 128, 32).transpose(1, 0, 2)
    lamv = np.stack([np.broadcast_to(np.asarray(inp[n][0], f).reshape(1, 64), (128, 64))
                     for n in ("lambda_q1", "lambda_k1", "lambda_q2", "lambda_k2")], axis=1).copy()
    subg = np.broadcast_to(np.asarray(inp["subln_g"][0], f).reshape(1, 128), (128, 128)).copy()
    return {
        "wB": wB, "wD": wD, "wE": wE, "wF": wF, "wG1": wG1, "wG2": wG2,
        "gains": gains, "chp": chp.reshape(128, 64), "wbd": wbd.reshape(128, 2048),
        "rope": rope.reshape(128, 1024), "lamv": lamv.reshape(128, 256), "subg": subg,
        "ident": np.eye(128, dtype=f),
    }


_NC_CACHE = {}


def kernel(**inputs):
    x = np.ascontiguousarray(np.asarray(inputs["x"], np.float32))
    shared = _host_layout(inputs)
    if "nc" not in _NC_CACHE:
        _NC_CACHE["nc"] = build(False)
    nc = _NC_CACHE["nc"]
    in_maps = []
    for c in range(NCORES):
        m = dict(shared)
        m["x"] = x[c * NSEQ:(c + 1) * NSEQ]
        in_maps.append(m)
    res = run_bass_kernel_spmd(nc, in_maps, core_ids=list(range(NCORES)))
    return np.concatenate([np.asarray(r["out"], np.float32) for r in res.results], axis=0)
```
